# Optimizing a Trainium2 kernel written in Bass

```python
import math
import jax, jax.numpy as jnp
from jax import lax
import numpy as np

D_MODEL = 4096
BATCH = 4
SEQ = 2048
DEPTH = 1
DEC_BATCH = 128
DEC_SEQ = 8
PAST_LEN = 8192
PAGE_SIZE = 128

HEAD_DIM = 128
ATTN_HEADS = D_MODEL // (2 * HEAD_DIM)
KV_HEADS = ATTN_HEADS // 4
Q_PER_KV = ATTN_HEADS // KV_HEADS
ATTN_WIDTH = ATTN_HEADS * HEAD_DIM
WINDOW = 128
BLOCK = 128
N_BUCKETS = 32
MAX_DISTANCE = 128
SSD_HEAD_DIM = 64
SSD_HEADS = D_MODEL // (2 * SSD_HEAD_DIM)
SSD_WIDTH = SSD_HEADS * SSD_HEAD_DIM
SSD_GROUPS = 8
HEADS_PER_GROUP = SSD_HEADS // SSD_GROUPS
SSD_STATE = 128
SSD_CONV = 4
SSD_CHUNK = 128
CONV_DIM = SSD_WIDTH + 2 * SSD_GROUPS * SSD_STATE
MIX_WIDTH = ATTN_WIDTH + SSD_WIDTH
IN_DIM = ATTN_WIDTH + 2 * KV_HEADS * HEAD_DIM + SSD_WIDTH + CONV_DIM + SSD_HEADS
D_FF = (8 * D_MODEL // 3 + 255) // 256 * 256
FFN_CONV = 3
EPS = 1e-6

kernel_name = "hymba_ssd_swa_convffn_step"


def _rms(x, w):
    xf = x.astype(jnp.float32)
    y = xf * lax.rsqrt(jnp.mean(xf * xf, axis=-1, keepdims=True) + EPS)
    return (y * w.astype(jnp.float32)).astype(x.dtype)


def _causal_dwconv(x, prev, w, b):
    K = w.shape[0]
    L = x.shape[1]
    xp = jnp.concatenate([prev.astype(x.dtype), x], axis=1)
    y = b + w[0] * xp[:, 0:L]
    for t in range(1, K):
        y = y + w[t] * xp[:, t:t + L]
    return y, xp[:, L:]


def _t5_bucket(dist):
    n = jnp.maximum(dist, 0)
    max_exact = N_BUCKETS // 2
    nf = jnp.maximum(n, 1).astype(jnp.float32)
    large = max_exact + (jnp.log(nf / max_exact) / math.log(MAX_DISTANCE / max_exact)
                         * (N_BUCKETS - max_exact)).astype(jnp.int32)
    large = jnp.minimum(large, N_BUCKETS - 1)
    return jnp.where(n < max_exact, n, large)


def _attend(q, k, v, dist, valid, sinks, rel_bias):
    n, lq, lk = dist.shape
    bias = rel_bias.astype(jnp.float32)[_t5_bucket(dist)].reshape(n, lq, lk, KV_HEADS, Q_PER_KV)
    bias = jnp.transpose(bias, (0, 3, 4, 1, 2))
    s = jnp.einsum("bnqhgd,bnshd->bnhgqs", q, k, preferred_element_type=jnp.float32)
    s = s * (HEAD_DIM ** -0.5) + bias[None]
    s = jnp.where(valid[None, :, None, None], s, -jnp.inf)
    sink = sinks.astype(jnp.float32).reshape(1, 1, KV_HEADS, Q_PER_KV, 1, 1)
    m = jnp.maximum(jnp.max(s, axis=-1, keepdims=True), sink)
    p = jnp.exp(s - m)
    probs = p / (jnp.sum(p, axis=-1, keepdims=True) + jnp.exp(sink - m))
    return jnp.einsum("bnhgqs,bnshd->bnqhgd", probs.astype(v.dtype), v)


def _attn_prompt(q, k, v, sinks, rel_bias):
    b, L = q.shape[:2]
    nb = L // BLOCK
    qb = q.reshape(b, nb, BLOCK, KV_HEADS, Q_PER_KV, HEAD_DIM)

    def band(t):
        cur = t.reshape(b, nb, BLOCK, KV_HEADS, HEAD_DIM)
        prev = jnp.concatenate([jnp.zeros_like(cur[:, :1]), cur[:, :-1]], axis=1)
        return jnp.concatenate([prev, cur], axis=2)

    i = jnp.arange(BLOCK)[:, None]
    j = jnp.arange(2 * BLOCK)[None, :]
    dist = (i + BLOCK - j)[None]
    kpos = jnp.arange(nb)[:, None, None] * BLOCK - BLOCK + j[None]
    valid = (dist >= 0) & (dist < WINDOW) & (kpos >= 0)
    o = _attend(qb, band(k), band(v), dist, valid, sinks, rel_bias)
    return o.reshape(b, L, ATTN_WIDTH)


def _attn_sample(q, k, v, win_k, win_v, sinks, rel_bias):
    b, L = q.shape[:2]
    keys = jnp.concatenate([win_k.astype(k.dtype), k], axis=1)
    vals = jnp.concatenate([win_v.astype(v.dtype), v], axis=1)
    i = jnp.arange(L)[:, None]
    j = jnp.arange(WINDOW + L)[None, :]
    dist = (i + WINDOW - j)[None]
    kpos = PAST_LEN - WINDOW + j
    valid = (dist >= 0) & (dist < WINDOW) & (kpos[None] >= 0)
    o = _attend(q.reshape(b, 1, L, KV_HEADS, Q_PER_KV, HEAD_DIM), keys[:, None], vals[:, None],
                dist, valid, sinks, rel_bias)
    return o.reshape(b, L, ATTN_WIDTH), keys[:, -WINDOW:], vals[:, -WINDOW:]


def _ssd(x, dt, A, Bm, Cm, h0):
    b, L = x.shape[:2]
    Q = SSD_CHUNK if L % SSD_CHUNK == 0 else L
    nc = L // Q
    G, Hg, P, N = SSD_GROUPS, HEADS_PER_GROUP, SSD_HEAD_DIM, SSD_STATE
    xc = x.reshape(b, nc, Q, G, Hg, P)
    dtc = dt.reshape(b, nc, Q, G, Hg)
    Bc = Bm.reshape(b, nc, Q, G, N)
    Cc = Cm.reshape(b, nc, Q, G, N)
    acs = jnp.cumsum(dtc * A.reshape(G, Hg), axis=2)
    xdt = xc * dtc[..., None]
    at = jnp.moveaxis(acs, 2, -1)
    seg = at[..., :, None] - at[..., None, :]
    causal = jnp.tril(jnp.ones((Q, Q), dtype=bool))
    Lmat = jnp.exp(jnp.where(causal, seg, -jnp.inf))
    cb = jnp.einsum("bclgn,bcsgn->bcgls", Cc, Bc)
    y_diag = jnp.einsum("bcgls,bcghls,bcsghp->bclghp", cb, Lmat, xdt)
    decay_end = jnp.exp(acs[:, :, -1:] - acs)
    st = jnp.einsum("bclgn,bclgh,bclghp->bcghpn", Bc, decay_end, xdt)
    chunk_decay = jnp.exp(acs[:, :, -1])

    def step(h, inp):
        s_c, d_c = inp
        return d_c[..., None, None] * h + s_c, h

    hT, h_in = lax.scan(step, h0.reshape(b, G, Hg, P, N),
                        (jnp.moveaxis(st, 1, 0), jnp.moveaxis(chunk_decay, 1, 0)))
    h_in = jnp.moveaxis(h_in, 0, 1)
    y_off = jnp.einsum("bclgn,bcghpn,bclgh->bclghp", Cc, h_in, jnp.exp(acs))
    y = (y_diag + y_off).reshape(b, L, SSD_HEADS, P)
    return y, hT.reshape(b, SSD_HEADS, P, N)


def _layer(x, win_k, win_v, ssm_h, ssd_prev, ffn_prev, rel_bias,
           mix_norm_w, w_in, q_norm_w, k_norm_w, attn_sinks, ssd_conv_w, ssd_conv_b,
           ssd_dt_bias, ssd_A_log, ssd_D, ssd_norm_w, w_out,
           ffn_norm_w, w_gate, w_up, ffn_conv_w, ffn_conv_b, w_down):
    b, L, _ = x.shape
    h = _rms(x, mix_norm_w)
    proj = h @ w_in
    offs = np.cumsum([ATTN_WIDTH, KV_HEADS * HEAD_DIM, KV_HEADS * HEAD_DIM, SSD_WIDTH, CONV_DIM])
    q, k, v, z, xbc, dt = jnp.split(proj, [int(o) for o in offs], axis=-1)

    q = _rms(q.reshape(b, L, ATTN_HEADS, HEAD_DIM), q_norm_w)
    k = _rms(k.reshape(b, L, KV_HEADS, HEAD_DIM), k_norm_w)
    v = v.reshape(b, L, KV_HEADS, HEAD_DIM)
    if win_k is None:
        attn = _attn_prompt(q, k, v, attn_sinks, rel_bias)
        new_k, new_v = k[:, -WINDOW:], v[:, -WINDOW:]
    else:
        attn, new_k, new_v = _attn_sample(q, k, v, win_k, win_v, attn_sinks, rel_bias)

    xbc_c, new_conv = _causal_dwconv(xbc, ssd_prev, ssd_conv_w, ssd_conv_b)
    xbc_c = jax.nn.silu(xbc_c).astype(jnp.float32)
    xs, Bm, Cm = jnp.split(xbc_c, [SSD_WIDTH, SSD_WIDTH + SSD_GROUPS * SSD_STATE], axis=-1)
    xs = xs.reshape(b, L, SSD_HEADS, SSD_HEAD_DIM)
    dtv = jax.nn.softplus(dt.astype(jnp.float32) + ssd_dt_bias.astype(jnp.float32))
    A = -jnp.exp(ssd_A_log.astype(jnp.float32))
    y, hT = _ssd(xs, dtv, A, Bm.reshape(b, L, SSD_GROUPS, SSD_STATE),
                 Cm.reshape(b, L, SSD_GROUPS, SSD_STATE), ssm_h.astype(jnp.float32))
    y = (y + ssd_D.astype(jnp.float32)[:, None] * xs).reshape(b, L, SSD_WIDTH)
    y = _rms(y * jax.nn.silu(z.astype(jnp.float32)), ssd_norm_w).astype(x.dtype)

    x = x + jnp.concatenate([attn.astype(x.dtype), y], axis=-1) @ w_out

    h2 = _rms(x, ffn_norm_w)
    g = h2 @ w_gate
    u = h2 @ w_up
    gc, new_ffn = _causal_dwconv(g, ffn_prev, ffn_conv_w, ffn_conv_b)
    x = x + (jax.nn.silu(gc) * u) @ w_down
    return x, new_k, new_v, hT.astype(x.dtype), new_conv, new_ffn


def setup_inputs(seed: int = 0) -> dict:
    key = jax.random.key(seed)
    ks = jax.random.split(key, 32)
    f32 = jnp.float32
    nrm = lambda k, shape, s=1.0: (jax.random.normal(k, shape, f32) * s)
    dt0 = jnp.exp(jax.random.uniform(ks[20], (DEPTH, SSD_HEADS), f32) * (math.log(0.1) - math.log(0.001))
                  + math.log(0.001))
    return {
        "x_prompt": nrm(ks[0], (BATCH, SEQ, D_MODEL)),
        "x_sample": nrm(ks[1], (DEC_BATCH, DEC_SEQ, D_MODEL)),
        "state_attn_k": nrm(ks[2], (DEPTH, DEC_BATCH, WINDOW, KV_HEADS, HEAD_DIM)),
        "state_attn_v": nrm(ks[3], (DEPTH, DEC_BATCH, WINDOW, KV_HEADS, HEAD_DIM)),
        "state_ssm": nrm(ks[4], (DEPTH, DEC_BATCH, SSD_HEADS, SSD_HEAD_DIM, SSD_STATE), 0.5),
        "state_ssd_conv": nrm(ks[5], (DEPTH, DEC_BATCH, SSD_CONV - 1, CONV_DIM)),
        "state_ffn_conv": nrm(ks[6], (DEPTH, DEC_BATCH, FFN_CONV - 1, D_FF)),
        "rel_bias": nrm(ks[7], (N_BUCKETS, ATTN_HEADS), 0.5),
        "mix_norm_w": 1.0 + nrm(ks[8], (DEPTH, D_MODEL), 0.02),
        "w_in": nrm(ks[9], (DEPTH, D_MODEL, IN_DIM), D_MODEL ** -0.5),
        "q_norm_w": 1.0 + nrm(ks[10], (DEPTH, HEAD_DIM), 0.02),
        "k_norm_w": 1.0 + nrm(ks[11], (DEPTH, HEAD_DIM), 0.02),
        "attn_sinks": nrm(ks[12], (DEPTH, ATTN_HEADS), 0.5),
        "ssd_conv_w": nrm(ks[13], (DEPTH, SSD_CONV, CONV_DIM), SSD_CONV ** -0.5),
        "ssd_conv_b": nrm(ks[14], (DEPTH, CONV_DIM), 0.01),
        "ssd_dt_bias": dt0 + jnp.log(-jnp.expm1(-dt0)),
        "ssd_A_log": jnp.log(jax.random.uniform(ks[15], (DEPTH, SSD_HEADS), f32, 1.0, 16.0)),
        "ssd_D": 1.0 + nrm(ks[16], (DEPTH, SSD_HEADS), 0.1),
        "ssd_norm_w": 1.0 + nrm(ks[17], (DEPTH, SSD_WIDTH), 0.02),
        "w_out": nrm(ks[18], (DEPTH, MIX_WIDTH, D_MODEL), MIX_WIDTH ** -0.5),
        "ffn_norm_w": 1.0 + nrm(ks[19], (DEPTH, D_MODEL), 0.02),
        "w_gate": nrm(ks[21], (DEPTH, D_MODEL, D_FF), D_MODEL ** -0.5),
        "w_up": nrm(ks[22], (DEPTH, D_MODEL, D_FF), D_MODEL ** -0.5),
        "ffn_conv_w": nrm(ks[23], (DEPTH, FFN_CONV, D_FF), FFN_CONV ** -0.5),
        "ffn_conv_b": nrm(ks[24], (DEPTH, D_FF), 0.01),
        "w_down": nrm(ks[25], (DEPTH, D_FF, D_MODEL), D_FF ** -0.5),
    }


def reference(x_prompt, x_sample, state_attn_k, state_attn_v, state_ssm, state_ssd_conv, state_ffn_conv,
              rel_bias, mix_norm_w, w_in, q_norm_w, k_norm_w, attn_sinks, ssd_conv_w, ssd_conv_b,
              ssd_dt_bias, ssd_A_log, ssd_D, ssd_norm_w, w_out, ffn_norm_w, w_gate, w_up,
              ffn_conv_w, ffn_conv_b, w_down):
    yp, ys = x_prompt, x_sample
    bp = x_prompt.shape[0]
    outs_p, outs_s = [], []
    for l in range(DEPTH):
        lw = (mix_norm_w[l], w_in[l], q_norm_w[l], k_norm_w[l], attn_sinks[l], ssd_conv_w[l], ssd_conv_b[l],
              ssd_dt_bias[l], ssd_A_log[l], ssd_D[l], ssd_norm_w[l], w_out[l],
              ffn_norm_w[l], w_gate[l], w_up[l], ffn_conv_w[l], ffn_conv_b[l], w_down[l])
        h0 = jnp.zeros((bp, SSD_HEADS, SSD_HEAD_DIM, SSD_STATE), jnp.float32)
        c0 = jnp.zeros((bp, SSD_CONV - 1, CONV_DIM), yp.dtype)
        f0 = jnp.zeros((bp, FFN_CONV - 1, D_FF), yp.dtype)
        yp, *sp = _layer(yp, None, None, h0, c0, f0, rel_bias, *lw)
        ys, *ss = _layer(ys, state_attn_k[l], state_attn_v[l], state_ssm[l], state_ssd_conv[l],
                         state_ffn_conv[l], rel_bias, *lw)
        outs_p.append(sp)
        outs_s.append(ss)
    p_k = jnp.stack([o[0] for o in outs_p])
    p_v = jnp.stack([o[1] for o in outs_p])
    p_ssm = jnp.stack([o[2] for o in outs_p])
    p_conv = jnp.stack([o[3] for o in outs_p])
    p_ffn = jnp.stack([o[4] for o in outs_p])
    s_k = jnp.stack([o[0] for o in outs_s])
    s_v = jnp.stack([o[1] for o in outs_s])
    s_ssm = jnp.stack([o[2] for o in outs_s])
    s_conv = jnp.stack([o[3] for o in outs_s])
    s_ffn = jnp.stack([o[4] for o in outs_s])
    return (yp, ys, p_k, p_v, p_ssm, p_conv, p_ffn, s_k, s_v, s_ssm, s_conv, s_ffn)
```

```python
import math
import numpy as np
import concourse.bass as bass
import concourse.mybir as mybir
from concourse.alu_op_type import AluOpType as ALU
from concourse.bass_utils import run_bass_kernel_spmd

F32 = mybir.dt.float32
BF16 = mybir.dt.bfloat16
AF = mybir.ActivationFunctionType

D = 4096
DFF = 11008
NFF = 86
IN_DIM = 9248
OFF_Q, OFF_K, OFF_V, OFF_Z, OFF_X, OFF_B, OFF_C, OFF_DT = 0, 2048, 2560, 3072, 5120, 7168, 8192, 9216
EPS = 1e-6
NEG = -30000.0
MIXNW, FFNNW, QNW, KNW, CONVW, CONVB, DEXP, SSDNW, FCW, FCB, NCF = 0, 32, 64, 65, 66, 194, 226, 242, 258, 516, 602
DTB, ALOG, OH32, MSEG, NC32 = 0, 1, 2, 34, 162
IDENT, MASKP, MASKS, ROWM, ANTI, NSTC = 0, 128, 256, 384, 400, 528


class Trk:
    __slots__ = ("w", "r", "excl")

    def __init__(self, excl=False):
        self.w = None
        self.r = {}
        self.excl = excl


class Eng:
    def __init__(self, ctx, name, attr):
        self.name = name
        self.attr = attr
        self.sem = ctx.nc.alloc_semaphore(name="s_" + name)
        self.cnt = 0
        self.seen = {}
        self.prog = []
        self.pend_r = []
        self.pend_w = []


class DSem:
    def __init__(self, ctx, name):
        self.sem = ctx.nc.alloc_semaphore(name="d_" + name)
        self.val = 0
        self.key = "d_" + name


class Ctx:
    def __init__(self, nc):
        self.nc = nc
        self.pe = Eng(self, "pe", "tensor")
        self.act = Eng(self, "act", "scalar")
        self.dve = Eng(self, "dve", "vector")
        self.pool = Eng(self, "pool", "gpsimd")
        self.sp = Eng(self, "sp", "sync")
        self.engs = [self.pe, self.act, self.dve, self.pool, self.sp]
        self.dsems = []
        self.n_ins = 0

    def dsem(self, name):
        d = DSem(self, name)
        self.dsems.append(d)
        return d

    def _deps(self, E, reads, writes):
        deps = []
        for t in reads:
            if t.w is not None:
                deps.append(t.w)
        for t in writes:
            if t.w is not None:
                deps.append(t.w)
            deps.extend(t.r.values())
        for key, sem, val in deps:
            if E.seen.get(key, 0) < val:
                E.prog.append(("wait", sem, val))
                E.seen[key] = val

    def op(self, E, fn, reads=(), writes=(), signal=True):
        if any(t.excl for t in reads):
            writes = list(writes) + [t for t in reads if t.excl]
            reads = [t for t in reads if not t.excl]
        self._deps(E, reads, writes)
        E.pend_r.extend(reads)
        E.pend_w.extend(writes)
        self.n_ins += 1
        if signal:
            E.cnt += 1
            E.prog.append(("op", fn, E.sem, 1))
            stamp = (E.name, E.sem, E.cnt)
            for t in E.pend_w:
                t.w = stamp
                t.r = {}
            for t in E.pend_r:
                t.r[E.name] = stamp
            E.pend_r = []
            E.pend_w = []
        else:
            E.prog.append(("op", fn, None, 0))

    def dma(self, Q, ds, out, in_, reads=(), writes=()):
        self._deps(Q, reads, writes)
        ds.val += 16
        Q.prog.append(("op", lambda h, o=out, i=in_: h.dma_start(out=o, in_=i), ds.sem, 16))
        stamp = (ds.key, ds.sem, ds.val)
        for t in writes:
            t.w = stamp
            t.r = {}
        for t in reads:
            t.r[ds.key] = stamp
        self.n_ins += 1

    def finish(self, E):
        for d in self.dsems:
            if d.val > 0:
                E.prog.append(("wait", d.sem, d.val))
        for X in self.engs:
            if X is not E and X.cnt > 0:
                E.prog.append(("wait", X.sem, X.cnt))

    def emit(self):
        with self.nc.Block() as block:
            for E in self.engs:
                def body(h, E=E):
                    for item in E.prog:
                        if item[0] == "wait":
                            h.wait_ge(item[1], item[2])
                        else:
                            ins = item[1](h)
                            if item[2] is not None:
                                ins.then_inc(item[2], item[3])
                getattr(block, E.attr)(body)


def sap(t, off, dims, parts=128, p0=0):
    Fsz = int(np.prod(t.shape[1:]))
    return bass.AP(t, p0 * Fsz + off, [[Fsz, parts]] + [[int(a), int(b)] for a, b in dims])


class _Stop(Exception):
    pass


def build(NPRE=8, NOWN=8, TSC=4, SAMPLE=True, STOP=99):
    cur = {"st": 0}

    def stage(n):
        if STOP == cur["st"] * 100 + n:
            raise _Stop()
    nc = bass.Bass("TRN2", target_bir_lowering=False)
    TSM = TSC * 128
    di = lambda n, s, dt=F32: nc.dram_tensor(n, s, dt, kind="ExternalInput").ap()
    do = lambda n, s, dt=F32: nc.dram_tensor(n, s, dt, kind="ExternalOutput").ap()
    NPC = NPRE + NOWN
    xp = di("xp", [NPC * 128, D]); xs = di("xs", [128, D])
    hm_d = di("hm", [128, 2])
    stk = di("stk", [16, 128, 512]); stv = di("stv", [16, 128, 512]); stssm = di("stssm", [16, 2048, 128])
    stconv = di("stconv", [48, D]); stffn = di("stffn", [32, DFF])
    w_in = di("w_in", [D, IN_DIM])
    if STOP < 7 or 50 <= STOP < 60:
        w_out = w_gate = w_up = w_down = None
    else:
        w_out = di("w_out", [D, D]); w_gate = di("w_gate", [D, DFF])
        w_up = di("w_up", [D, DFF]); w_down = di("w_down", [DFF, D])
    cf_d = di("cf", [128, NCF]); c32_d = di("c32", [32, NC32]); stc_d = di("stc", [128, NSTC])
    relb_d = di("relb33", [33, 16]); ohd_d = di("ohd33", [33, 384]); sinks_d = di("sinks", [1, 16])
    yp = do("yp", [NOWN * 128, D]); ys = do("ys", [128, D])
    xdum = nc.dram_tensor("xdum", [128, D], F32).ap()
    pk = do("pk", [128, 512]); pv = do("pv", [128, 512]); pssm = do("pssm", [2048, 128])
    pconv = do("pconv", [3, D]); pffn = do("pffn", [2, DFF])
    sk = do("sk", [16, 128, 512]); sv = do("sv", [16, 128, 512]); sssm = do("sssm", [16, 2048, 128])
    sconv = do("sconv", [48, D]); sffn = do("sffn", [32, DFF])
    brow_h = nc.dram_tensor("brow", [16, 384], F32)
    ats_h = nc.dram_tensor("ats", [NFF, 128, TSM], BF16)
    ats = ats_h.ap()

    c = Ctx(nc)
    sbt = lambda n, s, dt=F32: nc.alloc_sbuf_tensor(n, s, dt)
    cf = sbt("cf_s", [128, NCF]); c32 = sbt("c32_s", [32, NC32]); stc = sbt("stc_s", [128, NSTC])
    idb = sbt("idb", [128, 128], BF16); onesb = sbt("onesb", [128, 128], BF16); ones32 = sbt("ones32", [32, 128])
    esink = sbt("esink", [128, 16]); negA = sbt("negA", [32, 1])
    biasC = sbt("biasC", [128, 16, 128], BF16); biasP = sbt("biasP", [128, 16, 128], BF16)
    biasS = sbt("biasS", [128, 16, 128], BF16)
    actT = sbt("actT", [128, 32, TSM], BF16); mixT = sbt("mixT", [128, 32, TSM], BF16)
    RS = 3
    ring = sbt("ring", [128, RS, 4096], BF16)
    KT = sbt("KT", [128, 4, 128 + TSM], BF16); Vtm = sbt("Vtm", [128, TSC + 1, 512], BF16)
    ST = sbt("ST", [128, 2048]); Sb = sbt("Sb", [128, 2048], BF16)
    ccar = sbt("ccar", [128, 32, 3]); fcar = sbt("fcar", [128, NFF, 2])
    xt = sbt("xt", [128, D]); xn = sbt("xn", [128, D], BF16)
    den = sbt("den", [128, 512])
    tS = [sbt(f"tS{i}", [128, 512]) for i in range(2)]
    Eb = [sbt(f"Eb{i}", [128, 512], BF16) for i in range(2)]
    pb = [nc.alloc_psum_tensor(f"pb{i}", [128, 512], F32) for i in range(7)]
    ptb = nc.alloc_psum_tensor("ptb", [128, 1024], BF16)
    Tpb = [Trk(excl=True) for _ in range(7)]
    Tptb = Trk(excl=True)
    T = {}

    def tk(name):
        if name not in T:
            T[name] = Trk()
        return T[name]

    def tt(E, out, in0, in1, op, r, w):
        c.op(E, lambda h: h.tensor_tensor(out=out, in0=in0, in1=in1, op=op), reads=r, writes=w)

    def tsc(E, out, in0, s1, op0, r, w, s2=None, op1=None):
        if op1 is None:
            c.op(E, lambda h: h.tensor_scalar(out=out, in0=in0, scalar1=s1, scalar2=None, op0=op0), reads=r, writes=w)
        else:
            c.op(E, lambda h: h.tensor_scalar(out=out, in0=in0, scalar1=s1, scalar2=s2, op0=op0, op1=op1), reads=r, writes=w)

    def stt(out, in0, sc, in1, op0, op1, r, w):
        c.op(c.dve, lambda h: h.scalar_tensor_tensor(out=out, in0=in0, scalar=sc, in1=in1, op0=op0, op1=op1), reads=r, writes=w)

    def actf(out, in_, func, r, w, **kw):
        c.op(c.act, lambda h: h.activation(out=out, in_=in_, func=func, **kw), reads=r, writes=w)

    def mm(out, lhsT, rhs, st, sp_, r, w, sig=True):
        c.op(c.pe, lambda h: h.matmul(out=out, lhsT=lhsT, rhs=rhs, start=st, stop=sp_), reads=r, writes=w, signal=sig)

    def tr(out, in_, ident, r, w, sig=True):
        c.op(c.pe, lambda h: h.transpose(out=out, in_=in_, identity=ident), reads=r, writes=w, signal=sig)

    def cp(E, out, in_, r, w):
        if E is c.act:
            actf(out, in_, AF.Copy, r, w)
        else:
            c.op(E, lambda h: h.tensor_copy(out=out, in_=in_), reads=r, writes=w)

    def mset(E, ap, v, w):
        c.op(E, lambda h: h.memset(ap, v), writes=w)

    def rstd_from(dst, src, scale, r, w):
        tsc(c.dve, dst, src, scale, ALU.mult, r, w, s2=EPS, op1=ALU.add)
        actf(dst, dst, AF.Sqrt, w, w)
        c.op(c.dve, lambda h: h.reciprocal(out=dst, in_=dst), reads=w, writes=w)

    ring_state = {"i": 0}
    ring_trk = [Trk() for _ in range(RS)]
    ring_ds = [c.dsem(f"w{i}") for i in range(RS)]

    def wload(parts):
        s = ring_state["i"] % RS
        ring_state["i"] += 1
        for n, (off, kt, ncol, src) in enumerate(parts):
            dst = sap(ring, s * 4096 + off, [(ncol, kt), (1, ncol)])
            c.dma(c.pool, ring_ds[s], dst, src.rearrange("(k p) n -> p k n", p=128), writes=[ring_trk[s]] if True else [])
        return s, ring_trk[s]

    def wview(s, off, k, ncol, c0, c1):
        return sap(ring, s * 4096 + off + k * ncol + c0, [(1, c1 - c0)])

    d_c = c.dsem("const")
    Tc = tk("const")
    c.dma(c.sp, d_c, cf[:], cf_d, writes=[Tc])
    c.dma(c.sp, d_c, c32[:], c32_d, writes=[Tc])
    c.dma(c.sp, d_c, stc[:], stc_d, writes=[Tc])
    hmc = sbt("hmc", [128, 2])
    c.dma(c.sp, d_c, hmc[:], hm_d, writes=[Tc])
    c.dma(c.sp, d_c, esink[:], bass.AP(sinks_d.tensor, 0, [[0, 128], [1, 16]]), writes=[Tc])
    relb_s = sbt("relb_s", [33, 16]); ohd_s = sbt("ohd_s", [33, 384])
    c.dma(c.sp, d_c, relb_s[:], relb_d, writes=[Tc])
    c.dma(c.sp, d_c, ohd_s[:], ohd_d, writes=[Tc])
    Tc.w = (d_c.key, d_c.sem, d_c.val)
    ident = stc[:, IDENT:IDENT + 128]
    cp(c.dve, idb[:], ident, [Tc], [tk("idb")])
    mset(c.dve, onesb[:], 1.0, [tk("onesb")])
    mset(c.dve, ones32[:], 1.0, [tk("ones32")])
    mset(c.dve, KT[:], 0.0, [tk("KT")])
    mset(c.dve, Vtm[:], 0.0, [tk("Vtm")])
    mset(c.dve, ST[:], 0.0, [tk("ST")])
    mset(c.dve, Sb[:], 0.0, [tk("Sb")])
    mset(c.dve, ccar[:], 0.0, [tk("ccar")])
    mset(c.dve, fcar[:], 0.0, [tk("fcar")])
    actf(esink[:], esink[:], AF.Exp, [Tc], [tk("esink")])
    actf(negA[:], c32[:, ALOG:ALOG + 1], AF.Exp, [Tc], [tk("negA")])
    tsc(c.dve, negA[:], negA[:], -1.0, ALU.mult, [tk("negA")], [tk("negA")])
    mm(pb[0][0:16, 0:384], relb_s[:], ohd_s[:], True, True, [Tc], [Tpb[0]])
    brow_s = den
    cp(c.dve, den[0:16, 0:384], pb[0][0:16, 0:384], [Tpb[0]], [tk("den")])
    d_b = c.dsem("brow")
    c.dma(c.sp, d_b, brow_h.ap(), den[0:16, 0:384], reads=[tk("den")], writes=[tk("brow_d")])
    hank = xt
    for which, dst in ((0, biasC), (128, biasP)):
        c.dma(c.sp, d_b, sap(xt, 0, [(128, 16), (1, 128)]), bass.AP(brow_h, which, [[1, 128], [384, 16], [1, 128]]), reads=[tk("brow_d")], writes=[tk("xt")])
        for q in range(4):
            mm(pb[q][:, :], stc[:, ANTI:ANTI + 128], sap(xt, q * 512, [(1, 512)]), True, True, [Tc, tk("xt")], [Tpb[q]])
            cp(c.dve, sap(dst, q * 512, [(1, 512)]), pb[q][:, :], [Tpb[q]], [tk("bias")])

    tt(c.dve, biasS[:], biasC[:], sap(stc, MASKS, [(0, 16), (1, 128)]), ALU.add, [tk("bias"), Tc], [tk("bias")])
    st48 = sbt("st48", [48, 128]); tl48 = sbt("tl48", [128, 48]); so48 = sbt("so48", [48, 128])
    xrs = sbt("xrs", [128, 16, 11]); cdx = sbt("cdx", [128, 2, 16]); ecdS = sbt("ecdS", [32, 16])
    fpv = sbt("fpv", [128, 4, 32]); gtl = sbt("gtl", [128, 4, 32])
    d_s48 = c.dsem("s48")
    d_sk = c.dsem("sk"); d_sv = c.dsem("sv"); d_cp = c.dsem("cpy"); d_kw = c.dsem("kw"); d_h0 = c.dsem("h0")
    d_so = c.dsem("so"); d_stg = c.dsem("stg")
    xtb = xt[:].bitcast(BF16)
    Kwj = xtb[:, 0:2048].rearrange("p (i d) -> p i d", d=128)
    Vwj = xtb[:, 2048:4096].rearrange("p (i d) -> p i d", d=128)
    KwT = xtb[:, 4096:6144].rearrange("p (i d) -> p i d", d=128)
    h0g = xt[:].rearrange("p (i t n) -> p i t n", t=2, n=128)
    SiT = xn[:].rearrange("p (i f) -> p i f", f=256)
    d_x = c.dsem("x")
    ss = sbt("ss", [128, 1]); rs = sbt("rs", [128, 1])

    def norm_T(src_rows, src_trk, cidx, nw_off):
        c.dma(c.sp, d_x, xt[:], src_rows, reads=src_trk, writes=[tk("xt")])
        c.op(c.act, lambda h: h.activation(out=xn[:], in_=xt[:], func=AF.Square, accum_out=ss[:]),
             reads=[tk("xt")], writes=[tk("xn"), tk("ss")])
        rstd_from(rs[:], ss[:], 1.0 / D, [tk("ss")], [tk("rs")])
        tsc(c.dve, xn[:], xt[:], rs[:, 0:1], ALU.mult, [tk("xt"), tk("rs")], [tk("xn")])
        for g in range(8):
            for j in range(4):
                k = g * 4 + j
                tr(ptb[:, j * 128:(j + 1) * 128], xn[:, k * 128:(k + 1) * 128], idb[:], [tk("xn"), tk("idb")], [Tptb], sig=(j == 3))
            tt(c.dve, sap(actT, (g * 4) * TSM + cidx * 128, [(TSM, 4), (1, 128)]), sap(ptb, 0, [(128, 4), (1, 128)]),
               sap(cf, nw_off + g * 4, [(1, 4), (0, 128)]), ALU.mult, [Tptb, Tc], [tk("actT")])

    def proj_tile(col, TS, bank, src_w, ncols=128):
        s, trk = wload([(0, 32, ncols, src_w[:, col:col + ncols])])
        for k in range(32):
            mm(pb[bank][0:ncols, 0:TS], wview(s, 0, k, ncols, 0, ncols), sap(actT, k * TSM, [(1, TS)]), k == 0, k == 31,
               [trk, tk("actT")], [Tpb[bank]], sig=(k == 31))

    sqb = sbt("sqb", [128, TSM], BF16); rsb = sbt("rsb", [128, TSM]); hn = sbt("hn", [128, TSM])

    def headnorm(bank, TS, wcol):
        actf(sqb[:, 0:TS], pb[bank][:, 0:TS], AF.Square, [Tpb[bank]], [tk("sqb")])
        mm(pb[3][:, 0:TS], onesb[:], sqb[:, 0:TS], True, True, [tk("onesb"), tk("sqb")], [Tpb[3]])
        rstd_from(rsb[:, 0:TS], pb[3][:, 0:TS], 1.0 / 128, [Tpb[3]], [tk("rsb")])
        stt(hn[:, 0:TS], pb[bank][:, 0:TS], cf[:, wcol:wcol + 1], rsb[:, 0:TS], ALU.mult, ALU.mult, [Tpb[bank], Tc, tk("rsb")], [tk("hn")])

    QT = sbt("QT", [128, 4, TSM], BF16)
    ktm = sbt("ktm", [128, 512]); vtmf = sbt("vtmf", [128, 512])
    if 4 * TSM >= 2048:
        Bm = QT[:].rearrange("p a b -> p (a b)")[:, 0:2048].rearrange("p (i n) -> p i n", n=128)
    else:
        Bm = sbt("Bm", [128, 16, 128], BF16)
    stg = vtmf
    d_o = c.dsem("outs")

    def attn_sample(j):
        c.dma(c.pool, d_kw, Kwj, stk[:, :, j * 128:(j + 1) * 128].rearrange("i s d -> s i d"), writes=[tk("xt")])
        c.dma(c.pool, d_kw, Vwj, stv[:, :, j * 128:(j + 1) * 128].rearrange("i s d -> s i d"), writes=[tk("xt")])
        for i4 in range(4):
            for q in range(4):
                i = i4 * 4 + q
                tr(ptb[:, q * 128:(q + 1) * 128], Kwj[:, i, :], idb[:], [tk("xt"), tk("idb")], [Tptb], sig=(q == 3))
            cp(c.act, KwT[:, i4 * 4:i4 * 4 + 4, :], ptb[:, 0:512].rearrange("p (i d) -> p i d", d=128), [Tptb], [tk("xt")])
        mm(pb[4][:, :], KT[:, j, 128:256], sap(QT, 0, [(TSM, 4), (1, 128)]), True, True, [tk("KT"), tk("QT")], [Tpb[4]])
        tt(c.dve, tS[0][:], pb[4][:, :], sap(biasS, 4 * j * 128, [(1, 512)]), ALU.add, [Tpb[4], tk("bias")], [tk("tS0")])
        actf(Eb[0][:], tS[0][:], AF.Exp, [tk("tS0")], [tk("Eb0")])
        for i in range(16):
            mm(pb[5][:, i * 32:(i + 1) * 32], KwT[:, i, :], sap(QT, 8 * i, [(TSM, 4), (1, 8)]), True, True, [tk("xt"), tk("QT")], [Tpb[5]], sig=(i == 15))
        v3 = [(32, 16), (8, 4), (1, 8)]
        tt(c.dve, sap(tS[1], 0, v3), sap(pb[5], 0, v3), sap(biasP, 4 * j * 128, [(0, 16), (128, 4), (1, 8)]), ALU.add, [Tpb[5], tk("bias")], [tk("tS1")])
        actf(Eb[1][:], tS[1][:], AF.Exp, [tk("tS1")], [tk("Eb1")])
        for lhs_w, lhs_s, bank in ((None, None, 6), (onesb, onesb, 3)):
            for i in range(16):
                lw = Vwj[:, i, :] if lhs_w is None else onesb[:]
                ls = Vtm[:, 1, j * 128:(j + 1) * 128] if lhs_s is None else onesb[:]
                mm(pb[bank][:, i * 32:(i + 1) * 32], lw, Eb[1][:, i * 32:(i + 1) * 32], True, False, [tk("xt"), tk("Eb1"), tk("onesb")], [Tpb[bank]], sig=False)
                mm(pb[bank][:, i * 32:(i + 1) * 32], ls, sap(Eb[0], 8 * i, [(128, 4), (1, 8)]), False, True, [tk("Vtm"), tk("Eb0"), tk("onesb")], [Tpb[bank]], sig=(i == 15))
        tt(c.dve, sap(den, 0, v3), sap(pb[3], 0, v3), sap(esink, 4 * j, [(0, 16), (1, 4), (0, 8)]), ALU.add, [Tpb[3], tk("esink")], [tk("den")])
        c.op(c.dve, lambda h: h.reciprocal(out=den[:], in_=den[:]), reads=[tk("den")], writes=[tk("den")])
        tt(c.dve, sap(mixT, 4 * j * TSM, [(8, 16), (TSM, 4), (1, 8)]), sap(pb[6], 0, v3), sap(den, 0, v3), ALU.mult, [Tpb[6], tk("den")], [tk("mixT")])

    def mixer(chunks, first_seq_chunk, pmask=False):
        nch = len(chunks)
        TS = nch * 128
        sample = chunks[0]["kind"] == "sample"
        if sample:
            c.dma(c.sp, d_cp, sk[:, 0:120, :], stk[:, 8:128, :], writes=[tk("o_k")])
            c.dma(c.sp, d_cp, sv[:, 0:120, :], stv[:, 8:128, :], writes=[tk("o_v")])
        for ci, ch in enumerate(chunks):
            norm_T(ch["src"], [], ci, MIXNW)
        stage(1)
        for j in range(4):
            bank = j % 2
            proj_tile(OFF_K + j * 128, TS, bank, w_in)
            headnorm(bank, TS, KNW)
            cp(c.act, KT[:, j, 128:128 + TS], hn[:, 0:TS], [tk("hn")], [tk("KT")])
            for ci, ch in enumerate(chunks):
                if ch.get("kv_out") is not None:
                    tr(pb[4][:, j * 128:(j + 1) * 128], hn[:, ci * 128:(ci + 1) * 128], ident, [tk("hn"), Tc], [Tpb[4]])
                    cp(c.dve, ktm[:, j * 128:(j + 1) * 128], pb[4][:, j * 128:(j + 1) * 128], [Tpb[4]], [tk("ktm")])
        for ci, ch in enumerate(chunks):
            if ch.get("kv_out") is not None:
                if ch["kind"] == "sample":
                    for i in range(16):
                        c.dma(c.sp, d_sk, sk[i, 120:128, :], ktm[8 * i:8 * i + 8, :], reads=[tk("ktm")], writes=[tk("o_k2")])
                else:
                    c.dma(c.sp, d_sk, ch["kv_out"][0], ktm[:], reads=[tk("ktm")], writes=[tk("o_k")])
        stage(2)
        for j in range(4):
            sv_, tv_ = wload([(0, 32, 128, w_in[:, OFF_V + j * 128:OFF_V + (j + 1) * 128])])
            for ci, ch in enumerate(chunks):
                for k in range(32):
                    mm(pb[ci][:, j * 128:(j + 1) * 128], sap(actT, k * TSM + ci * 128, [(1, 128)]), wview(sv_, 0, k, 128, 0, 128),
                       k == 0, k == 31, [tv_, tk("actT")], [Tpb[ci]], sig=(k == 31))
        stage(21)
        for ci, ch in enumerate(chunks):
            cp(c.act, Vtm[:, 1 + ci, :], pb[ci][:, :], [Tpb[ci]], [tk("Vtm")])
            stage(22 + ci)
            if ch.get("kv_out") is not None:
                cp(c.act, vtmf[:], pb[ci][:, :], [Tpb[ci]], [tk("vtmf")])
                if ch["kind"] == "sample":
                    for i in range(16):
                        c.dma(c.sp, d_sv, sv[i, 120:128, :], vtmf[8 * i:8 * i + 8, :], reads=[tk("vtmf")], writes=[tk("o_v2")])
                else:
                    c.dma(c.sp, d_sv, ch["kv_out"][1], vtmf[:], reads=[tk("vtmf")], writes=[tk("o_v")])
        stage(3)
        for j in range(4):
            for hh in range(4):
                bank = hh % 2
                proj_tile(OFF_Q + (4 * j + hh) * 128, TS, bank, w_in)
                headnorm(bank, TS, QNW)
                actf(QT[:, hh, 0:TS], hn[:, 0:TS], AF.Copy, [tk("hn")], [tk("QT")], scale=128 ** -0.5)
            if sample:
                attn_sample(j)
            for ci, ch in enumerate(chunks):
                if ch["kind"] == "prompt":
                    blocks = [(1, biasC)]
                    if not (first_seq_chunk and ci == 0):
                        blocks.append((0, biasP))
                    for bi, (rel, btab) in enumerate(blocks):
                        kcol = 128 + ci * 128 if rel == 1 else ci * 128
                        mm(pb[4 + bi][:, :], KT[:, j, kcol:kcol + 128], sap(QT, ci * 128, [(TSM, 4), (1, 128)]), True, True,
                           [tk("KT"), tk("QT")], [Tpb[4 + bi]])
                        tt(c.dve, tS[bi][:], pb[4 + bi][:, :], sap(btab, 4 * j * 128, [(1, 512)]), ALU.add, [Tpb[4 + bi], tk("bias")], [tk(f"tS{bi}")])
                        if pmask and ci == 0 and rel == 0:
                            actf(Eb[bi][:], tS[bi][:], AF.Exp, [tk(f"tS{bi}"), Tc], [tk(f"Eb{bi}")], bias=hmc[:, 1:2])
                        else:
                            actf(Eb[bi][:], tS[bi][:], AF.Exp, [tk(f"tS{bi}")], [tk(f"Eb{bi}")])
                    nb = len(blocks)
                    for bi, (rel, btab) in enumerate(blocks):
                        vsl = 1 + ci if rel == 1 else ci
                        mm(pb[6][:, :], Vtm[:, vsl, j * 128:(j + 1) * 128], Eb[bi][:], bi == 0, bi == nb - 1,
                           [tk("Vtm"), tk(f"Eb{bi}")], [Tpb[6]], sig=(bi == nb - 1))
                    for bi in range(nb):
                        mm(pb[3][:, :], onesb[:], Eb[bi][:], bi == 0, bi == nb - 1, [tk("onesb"), tk(f"Eb{bi}")], [Tpb[3]], sig=(bi == nb - 1))
                    tt(c.dve, den[:], pb[3][:, :], sap(esink, 4 * j, [(1, 4), (0, 128)]), ALU.add, [Tpb[3], tk("esink")], [tk("den")])
                    c.op(c.dve, lambda h: h.reciprocal(out=den[:], in_=den[:]), reads=[tk("den")], writes=[tk("den")])
                    tt(c.dve, sap(mixT, 4 * j * TSM + ci * 128, [(TSM, 4), (1, 128)]), pb[6][:, :], den[:], ALU.mult,
                       [Tpb[6], tk("den")], [tk("mixT")])
        stage(4)
        cp(c.act, KT[:, :, 0:128], KT[:, :, TS:TS + 128], [tk("KT")], [tk("KT")])
        cp(c.act, Vtm[:, 0, :], Vtm[:, nch, :], [tk("Vtm")], [tk("Vtm")])
        ssd(chunks, TS)

    dtT = sbt("dtT", [32, TSM]); a2T = sbt("a2T", [32, TSM]); acsT = sbt("acsT", [32, TSM]); dendT = sbt("dendT", [32, TSM])
    ecd = sbt("ecd", [32, TSC]); tm = sbt("tm", [128, TSC, 96])
    xr = sbt("xr", [128, 3 + TSM]); cacc = sbt("cacc", [128, TSM])
    xsT = sbt("xsT", [128, 2, TSM], BF16); BT = sbt("BT", [128, TSM], BF16); CT = sbt("CT", [128, TSM], BF16)
    szT = sbt("szT", [128, 2, TSM], BF16)
    xdt = sbt("xdt", [128, 256], BF16); xdtd = sbt("xdtd", [128, 256], BF16); Btm = sbt("Btm", [128, 128], BF16)
    m4 = den; t1 = tS[0]; EA = tS[1]; MT = Eb[0]; ChT = Eb[1]
    e256 = sbt("e256", [32, 256]); ytmp = sbt("ytmp", [128, 128]); sqy = sbt("sqy", [128, 128], BF16)
    rsy = rsb

    def conv_tile(bank, TS, ft, dst, silu=True):
        cp(c.act, xr[:, 3:3 + TS], pb[bank][:, 0:TS], [Tpb[bank]], [tk("xr")])
        cp(c.act, xr[:, 0:3], ccar[:, ft, :], [tk("ccar")], [tk("xr")])
        cp(c.act, ccar[:, ft, :], xr[:, TS:TS + 3], [tk("xr")], [tk("ccar")])
        wof = CONVW + ft * 4
        c.op(c.act, lambda h: h.activation(out=cacc[:, 0:TS], in_=xr[:, 3:3 + TS], func=AF.Identity,
                                           scale=cf[:, wof + 3:wof + 4], bias=cf[:, CONVB + ft:CONVB + ft + 1]),
             reads=[tk("xr"), Tc], writes=[tk("cacc")])
        for t_ in range(3):
            stt(cacc[:, 0:TS], xr[:, t_:t_ + TS], cf[:, wof + t_:wof + t_ + 1], cacc[:, 0:TS], ALU.mult, ALU.add,
                [tk("xr"), Tc, tk("cacc")], [tk("cacc")])
        actf(dst, cacc[:, 0:TS], AF.Silu, [tk("cacc")], [tk("ssdin")])

    def conv_tile_s(bank, ft, dst):
        v8 = [(11, 16), (1, 8)]
        cp(c.act, sap(xrs, 3, v8), sap(pb[bank], 0, [(8, 16), (1, 8)]), [Tpb[bank]], [tk("xr")])
        c.dma(c.sp, d_stg, st48[:], stconv[:, ft * 128:(ft + 1) * 128], writes=[tk("st48")])
        tr(pb[4][:, 0:48], st48[:], stc[0:48, IDENT:IDENT + 48], [tk("st48"), Tc], [Tpb[4]])
        cp(c.act, sap(xrs, 0, [(11, 16), (1, 3)]), sap(pb[4], 0, [(3, 16), (1, 3)]), [Tpb[4]], [tk("xr")])
        cp(c.act, sap(tl48, 0, [(3, 16), (1, 3)]), sap(xrs, 8, [(11, 16), (1, 3)]), [tk("xr")], [tk("tl48")])
        tr(pb[4][0:48, 128:256], tl48[:], ident, [tk("tl48"), Tc], [Tpb[4]])
        cp(c.act, so48[:], pb[4][0:48, 128:256], [Tpb[4]], [tk("so48")])
        c.dma(c.sp, d_s48, sconv[:, ft * 128:(ft + 1) * 128], so48[:], reads=[tk("so48")], writes=[tk("o_sconv")])
        wof = CONVW + ft * 4
        o8 = sap(cacc, 0, [(8, 16), (1, 8)])
        c.op(c.act, lambda h: h.activation(out=o8, in_=sap(xrs, 3, v8), func=AF.Identity,
                                           scale=cf[:, wof + 3:wof + 4], bias=cf[:, CONVB + ft:CONVB + ft + 1]),
             reads=[tk("xr"), Tc], writes=[tk("cacc")])
        for t_ in range(3):
            stt(o8, sap(xrs, t_, v8), cf[:, wof + t_:wof + t_ + 1], o8, ALU.mult, ALU.add, [tk("xr"), Tc, tk("cacc")], [tk("cacc")])
        actf(dst, cacc[:, 0:128], AF.Silu, [tk("cacc")], [tk("ssdin")])

    def ssd(chunks, TS):
        nch = len(chunks)
        sample = chunks[0]["kind"] == "sample"
        s, trk = wload([(0, 32, 32, w_in[:, OFF_DT:OFF_DT + 32])])
        for k in range(32):
            mm(pb[0][0:32, 0:TS], wview(s, 0, k, 32, 0, 32), sap(actT, k * TSM, [(1, TS)]), k == 0, k == 31, [trk, tk("actT")], [Tpb[0]], sig=(k == 31))
        actf(dtT[:, 0:TS], pb[0][0:32, 0:TS], AF.Exp, [Tpb[0], Tc], [tk("dtT")], bias=c32[:, DTB:DTB + 1])
        actf(dtT[:, 0:TS], dtT[:, 0:TS], AF.Ln, [tk("dtT")], [tk("dtT")], bias=1.0)
        tsc(c.dve, a2T[:, 0:TS], dtT[:, 0:TS], negA[:, 0:1], ALU.mult, [tk("dtT"), tk("negA")], [tk("a2T")])
        for ci, ch in enumerate(chunks):
            cs = slice(ci * 128, (ci + 1) * 128)
            if ch["kind"] == "prompt":
                msk = sap(ones32, 0, [(1, 128)], parts=32)
                nseg, sl = 1, 128
            else:
                msk = c32[:, MSEG:MSEG + 128]
                nseg, sl = 16, 8
            c.op(c.dve, lambda h, cs=cs, msk=msk: h.tensor_tensor_scan(out=acsT[:, cs], data0=msk, data1=a2T[:, cs], initial=0.0,
                                                                      op0=ALU.mult, op1=ALU.add),
                 reads=[tk("a2T"), tk("ones32"), Tc], writes=[tk("acsT")])
            last = sap(acsT, ci * 128 + sl - 1, [(sl, nseg), (0, sl)], parts=32)
            tt(c.dve, sap(dendT, ci * 128, [(sl, nseg), (1, sl)], parts=32), last, sap(acsT, ci * 128, [(sl, nseg), (1, sl)], parts=32),
               ALU.subtract, [tk("acsT")], [tk("dendT")])
            actf(dendT[:, cs], dendT[:, cs], AF.Exp, [tk("dendT")], [tk("dendT")])
            if ch["kind"] == "prompt":
                actf(ecd[:, ci:ci + 1], acsT[:, ci * 128 + 127:ci * 128 + 128], AF.Exp, [tk("acsT")], [tk("ecd")])
            else:
                actf(ecdS[:, :], sap(acsT, ci * 128 + 7, [(8, 16)], parts=32), AF.Exp, [tk("acsT")], [tk("ecd")])
            for q, src in enumerate((dtT, acsT, dendT)):
                tr(pb[1][:, q * 32:(q + 1) * 32], src[:, cs], stc[0:32, IDENT:IDENT + 32], [tk("dtT"), tk("acsT"), tk("dendT"), Tc], [Tpb[1]], sig=(q == 2))
            cp(c.act, tm[:, ci, :], pb[1][:, 0:96], [Tpb[1]], [tk("tm")])
        stage(5)
        for g in range(8):
            cvt = (lambda bank, TS_, ft, dst: conv_tile_s(bank, ft, dst)) if sample else conv_tile
            for t2 in range(2):
                proj_tile(OFF_X + g * 256 + t2 * 128, TS, t2, w_in)
                cvt(t2, TS, 2 * g + t2, xsT[:, t2, 0:TS])
            proj_tile(OFF_B + g * 128, TS, 0, w_in)
            cvt(0, TS, 16 + g, BT[:, 0:TS])
            proj_tile(OFF_C + g * 128, TS, 1, w_in)
            cvt(1, TS, 24 + g, CT[:, 0:TS])
            if sample:
                for t2 in range(2):
                    c.dma(c.sp, d_h0, h0g[:, :, t2, :], stssm[:, g * 256 + t2 * 128:g * 256 + (t2 + 1) * 128, :].rearrange("i p n -> p i n"), writes=[tk("xt")])
                for i in range(16):
                    for t2 in range(2):
                        tr(pb[2][:, t2 * 128:(t2 + 1) * 128], h0g[:, i, t2, :], ident, [tk("xt"), Tc], [Tpb[2]], sig=(t2 == 1))
                    cp(c.act, SiT[:, i, :], pb[2][:, 0:256], [Tpb[2]], [tk("xn")])
            for t2 in range(2):
                proj_tile(OFF_Z + g * 256 + t2 * 128, TS, t2, w_in)
                actf(szT[:, t2, 0:TS], pb[t2][:, 0:TS], AF.Silu, [Tpb[t2]], [tk("szT")])
            stage(52)
            for ci, ch in enumerate(chunks):
                cs = slice(ci * 128, (ci + 1) * 128)
                prompt = ch["kind"] == "prompt"
                for t2 in range(2):
                    tr(ptb[:, t2 * 128:(t2 + 1) * 128], xsT[:, t2, cs], idb[:], [tk("ssdin"), tk("idb")], [Tptb], sig=False)
                tr(ptb[:, 256:384], BT[:, cs], idb[:], [tk("ssdin"), tk("idb")], [Tptb])
                tt(c.dve, sap(xdt, 0, [(64, 4), (1, 64)]), sap(ptb, 0, [(64, 4), (1, 64)]), sap(tm, ci * 96 + 4 * g, [(1, 4), (0, 64)]),
                   ALU.mult, [Tptb, tk("tm")], [tk("xdt")])
                tt(c.dve, sap(xdtd, 0, [(64, 4), (1, 64)]), sap(xdt, 0, [(64, 4), (1, 64)]), sap(tm, ci * 96 + 64 + 4 * g, [(1, 4), (0, 64)]),
                   ALU.mult, [tk("xdt"), tk("tm")], [tk("xdtd")])
                cp(c.act, Btm[:], ptb[:, 256:384], [Tptb], [tk("Btm")])
                stage(53)
                mm(pb[4][:, 0:128], BT[:, cs], CT[:, cs], True, True, [tk("ssdin")], [Tpb[4]])
                tt(c.dve, sap(m4, 0, [(128, 4), (1, 128)], parts=32), sap(acsT, ci * 128, [(0, 4), (1, 128)], parts=32),
                   sap(c32, OH32 + 4 * g, [(1, 4), (0, 128)], parts=32), ALU.mult, [tk("acsT"), Tc], [tk("den")])
                mm(pb[5][:, :], ones32[:], m4[0:32, :], True, True, [tk("ones32"), tk("den")], [Tpb[5]])
                moff = MASKP if prompt else MASKS
                tt(c.dve, t1[:], pb[5][:, :], sap(stc, moff, [(0, 4), (1, 128)]), ALU.add, [Tpb[5], Tc], [tk("tS0")])
                tt(c.dve, t1[:], t1[:], sap(tm, ci * 96 + 32 + 4 * g, [(1, 4), (0, 128)]), ALU.subtract, [tk("tS0"), tk("tm")], [tk("tS0")])
                actf(t1[:], t1[:], AF.Exp, [tk("tS0")], [tk("tS0")])
                tt(c.dve, MT[:], t1[:], sap(pb[4], 0, [(0, 4), (1, 128)]), ALU.mult, [tk("tS0"), Tpb[4]], [tk("Eb0")])
                actf(EA[:], pb[5][:, :], AF.Exp, [Tpb[5]], [tk("tS1")])
                tt(c.dve, ChT[:], EA[:], sap(CT, ci * 128, [(0, 4), (1, 128)]), ALU.mult, [tk("tS1"), tk("ssdin")], [tk("Eb1")])
                stage(54)
                for hh in range(4):
                    o = pb[6][(hh % 2) * 64:(hh % 2) * 64 + 64, (hh // 2) * 128:(hh // 2) * 128 + 128]
                    if prompt:
                        mm(o, xdt[:, hh * 64:(hh + 1) * 64], MT[:, hh * 128:(hh + 1) * 128], True, False, [tk("xdt"), tk("Eb0")], [Tpb[6]], sig=False)
                        mm(o, Sb[:, g * 256 + hh * 64:g * 256 + (hh + 1) * 64], ChT[:, hh * 128:(hh + 1) * 128], False, True,
                           [tk("Sb"), tk("Eb1")], [Tpb[6]], sig=(hh == 3))
                    else:
                        mm(o, xdt[:, hh * 64:(hh + 1) * 64], MT[:, hh * 128:(hh + 1) * 128], True, False, [tk("xdt"), tk("Eb0")], [Tpb[6]], sig=False)
                        for i in range(16):
                            oi = pb[6][(hh % 2) * 64:(hh % 2) * 64 + 64, (hh // 2) * 128 + 8 * i:(hh // 2) * 128 + 8 * i + 8]
                            mm(oi, SiT[:, i, hh * 64:(hh + 1) * 64], ChT[:, hh * 128 + 8 * i:hh * 128 + 8 * i + 8], False, True,
                               [tk("xn"), tk("Eb1")], [Tpb[6]], sig=(hh == 3 and i == 15))
                stage(55)
                for t2 in range(2):
                    ft = 2 * g + t2
                    stt(ytmp[:], xsT[:, t2, cs], cf[:, DEXP + ft:DEXP + ft + 1], pb[6][:, t2 * 128:(t2 + 1) * 128], ALU.mult, ALU.add,
                        [tk("ssdin"), Tc, Tpb[6]], [tk("ytmp")])
                    tt(c.dve, mixT[:, 16 + ft, cs], ytmp[:], szT[:, t2, cs], ALU.mult, [tk("ytmp"), tk("szT")], [tk("mixT")])
                stage(56)
                if prompt:
                    mm(pb[4][:, 128:384], Btm[:], xdtd[:], True, True, [tk("Btm"), tk("xdtd")], [Tpb[4]])
                    tsc(c.dve, sap(e256, 0, [(64, 4), (1, 64)], parts=32), sap(c32, OH32 + 4 * g, [(1, 4), (0, 64)], parts=32), ecd[:, ci:ci + 1], ALU.mult, [Tc, tk("ecd")], [tk("e256")])
                    mm(pb[5][:, 0:256], ones32[:], e256[:], True, True, [tk("ones32"), tk("e256")], [Tpb[5]])
                    gs = slice(g * 256, (g + 1) * 256)
                    tt(c.dve, ST[:, gs], ST[:, gs], pb[5][:, 0:256], ALU.mult, [tk("ST"), Tpb[5]], [tk("ST")])
                    tt(c.dve, ST[:, gs], ST[:, gs], pb[4][:, 128:384], ALU.add, [tk("ST"), Tpb[4]], [tk("ST")])
                    cp(c.act, Sb[:, gs], ST[:, gs], [tk("ST")], [tk("Sb")])
                else:
                    cp(c.dve, sap(e256, 0, [(64, 4), (1, 64)], parts=32), sap(c32, OH32 + 4 * g, [(1, 4), (0, 64)], parts=32), [Tc], [tk("e256")])
                    for t2 in range(2):
                        mm(pb[5][:, t2 * 16:(t2 + 1) * 16], e256[:, t2 * 128:(t2 + 1) * 128], ecdS[:, :], True, True, [tk("e256"), tk("ecd")], [Tpb[5]], sig=(t2 == 1))
                    cp(c.act, sap(cdx, 0, [(1, 32)]), pb[5][:, 0:32], [Tpb[5]], [tk("cdx")])
                    for i in range(16):
                        tsc(c.dve, Bm[:, i, :], Btm[:], stc[:, ROWM + i:ROWM + i + 1], ALU.mult, [tk("Btm"), Tc], [tk("QT")])
                    for i in range(16):
                        for t2 in range(2):
                            q = (i * 2 + t2) % 4
                            mm(pb[4][:, q * 128:(q + 1) * 128], xdtd[:, t2 * 128:(t2 + 1) * 128], Bm[:, i, :], True, True, [tk("xdtd"), tk("QT")], [Tpb[4]])
                            stt(h0g[:, i, t2, :], h0g[:, i, t2, :], cdx[:, t2, i:i + 1], pb[4][:, q * 128:(q + 1) * 128], ALU.mult, ALU.add,
                                [tk("xt"), tk("cdx"), Tpb[4]], [tk("xt")])
                    for t2 in range(2):
                        c.dma(c.sp, d_so, sssm[:, g * 256 + t2 * 128:g * 256 + (t2 + 1) * 128, :].rearrange("i p n -> p i n"), h0g[:, :, t2, :],
                              reads=[tk("xt")], writes=[tk("o_sssm")])
        stage(6)
        for ci in range(nch):
            cs = slice(ci * 128, (ci + 1) * 128)
            for ft in range(16):
                actf(sqy[:], mixT[:, 16 + ft, cs], AF.Square, [tk("mixT")], [tk("sqy")])
                mm(pb[2][:, cs], onesb[:], sqy[:], ft == 0, ft == 15, [tk("onesb"), tk("sqy")], [Tpb[2]])
        rstd_from(rsy[:, 0:TS], pb[2][:, 0:TS], 1.0 / 2048, [Tpb[2]], [tk("rsb")])
        for ft in range(16):
            stt(mixT[:, 16 + ft, 0:TS], mixT[:, 16 + ft, 0:TS], cf[:, SSDNW + ft:SSDNW + ft + 1], rsy[:, 0:TS], ALU.mult, ALU.mult,
                [tk("mixT"), Tc, tk("rsb")], [tk("mixT")])

    xpc = tS
    d_ys = [c.dsem(f"ys{i}") for i in range(2)]
    d_xp = [c.dsem(f"xp{i}") for i in range(2)]
    d_y = c.dsem("y")
    cnt = {"xp": 0}

    def outproj(chunks, Ty):
        nch = len(chunks)
        for cb in range(8):
            for kb in range(4):
                s, trk = wload([(0, 8, 512, w_out[kb * 1024:(kb + 1) * 1024, cb * 512:(cb + 1) * 512])])
                for ci in range(nch):
                    for k8 in range(8):
                        k = kb * 8 + k8
                        mm(pb[ci][:, :], sap(mixT, k * TSM + ci * 128, [(1, 128)]), wview(s, 0, k8, 512, 0, 512), k == 0, k == 31,
                           [trk, tk("mixT")], [Tpb[ci]], sig=(k8 == 7))
            for ci, ch in enumerate(chunks):
                i = cnt["xp"] % 2
                cnt["xp"] += 1
                c.dma(c.sp, d_xp[i], xpc[i][:], ch["src"][:, cb * 512:(cb + 1) * 512], writes=[tk(f"tS{i}")])
                tt(c.dve, xpc[i][:], xpc[i][:], pb[ci][:, :], ALU.add, [tk(f"tS{i}"), Tpb[ci]], [tk(f"tS{i}")])
                c.dma(c.sp, d_ys[i], ch["dst"][:, cb * 512:(cb + 1) * 512], xpc[i][:], reads=[tk(f"tS{i}")], writes=[Ty[ci][cb]])

    gc = cacc; sg = sqb
    abuf = [sbt(f"abuf{i}", [128, TSM], BF16) for i in range(2)]
    d_a = [c.dsem(f"a{i}") for i in range(2)]
    d_ab = [c.dsem(f"ab{i}") for i in range(2)]
    Tats = [Trk() for _ in range(NFF)]

    def ffn(chunks, Ty, TS, gate_only=False):
        nch = len(chunks)
        for ci, ch in enumerate(chunks):
            norm_T(ch["dst"], Ty[ci], ci, FFNNW)
        for j in range(NFF):
            s, trk = wload([(0, 32, 128, w_gate[:, j * 128:(j + 1) * 128])])
            bg, bu = (0, 1) if j % 2 == 0 else (2, 3)
            for k in range(32):
                mm(pb[bg][:, 0:TS], wview(s, 0, k, 128, 0, 128), sap(actT, k * TSM, [(1, TS)]), k == 0, k == 31, [trk, tk("actT")], [Tpb[bg]], sig=(k == 31))
            if gate_only:
                cp(c.act, fcar[:, j, :], pb[bg][:, TS - 2:TS], [Tpb[bg]], [tk("fcar")])
                continue
            s2, trk2 = wload([(0, 32, 128, w_up[:, j * 128:(j + 1) * 128])])
            for k in range(32):
                mm(pb[bu][:, 0:TS], wview(s2, 0, k, 128, 0, 128), sap(actT, k * TSM, [(1, TS)]), k == 0, k == 31, [trk2, tk("actT")], [Tpb[bu]], sig=(k == 31))
            G = pb[bg]
            wof = FCW + j * 3
            bcol = cf[:, FCB + j:FCB + j + 1]
            if chunks[0]["kind"] == "prompt":
                c.op(c.act, lambda h, G=G, wof=wof, bcol=bcol: h.activation(out=gc[:, 0:TS], in_=G[:, 0:TS], func=AF.Identity,
                                                                             scale=cf[:, wof + 2:wof + 3], bias=bcol),
                     reads=[Tpb[bg], Tc], writes=[tk("cacc")])
                stt(gc[:, 1:TS], G[:, 0:TS - 1], cf[:, wof + 1:wof + 2], gc[:, 1:TS], ALU.mult, ALU.add, [Tpb[bg], Tc, tk("cacc")], [tk("cacc")])
                stt(gc[:, 2:TS], G[:, 0:TS - 2], cf[:, wof:wof + 1], gc[:, 2:TS], ALU.mult, ALU.add, [Tpb[bg], Tc, tk("cacc")], [tk("cacc")])
                stt(gc[:, 0:1], fcar[:, j, 1:2], cf[:, wof + 1:wof + 2], gc[:, 0:1], ALU.mult, ALU.add, [tk("fcar"), Tc, tk("cacc")], [tk("cacc")])
                stt(gc[:, 0:2], fcar[:, j, 0:2], cf[:, wof:wof + 1], gc[:, 0:2], ALU.mult, ALU.add, [tk("fcar"), Tc, tk("cacc")], [tk("cacc")])
                cp(c.act, fcar[:, j, :], G[:, TS - 2:TS], [Tpb[bg]], [tk("fcar")])
            else:
                q = j % 4
                if q == 0:
                    nq = min(4, NFF - j)
                    c.dma(c.sp, d_stg, stg[0:32, 0:nq * 128], stffn[:, j * 128:(j + nq) * 128], writes=[tk("vtmf")])
                    for qq in range(nq):
                        tr(pb[4][:, qq * 32:(qq + 1) * 32], stg[0:32, qq * 128:(qq + 1) * 128], stc[0:32, IDENT:IDENT + 32], [tk("vtmf"), Tc], [Tpb[4]], sig=(qq == nq - 1))
                    cp(c.act, sap(fpv, 0, [(1, nq * 32)]), pb[4][:, 0:nq * 32], [Tpb[4]], [tk("fpv")])
                g8 = sap(gc, 0, [(8, 16), (1, 8)])
                c.op(c.act, lambda h, G=G, wof=wof, bcol=bcol, g8=g8: h.activation(out=g8, in_=sap(G, 0, [(8, 16), (1, 8)]), func=AF.Identity,
                                                                                   scale=cf[:, wof + 2:wof + 3], bias=bcol),
                     reads=[Tpb[bg], Tc], writes=[tk("cacc")])
                stt(sap(gc, 1, [(8, 16), (1, 7)]), sap(G, 0, [(8, 16), (1, 7)]), cf[:, wof + 1:wof + 2], sap(gc, 1, [(8, 16), (1, 7)]), ALU.mult, ALU.add,
                    [Tpb[bg], Tc, tk("cacc")], [tk("cacc")])
                stt(sap(gc, 2, [(8, 16), (1, 6)]), sap(G, 0, [(8, 16), (1, 6)]), cf[:, wof:wof + 1], sap(gc, 2, [(8, 16), (1, 6)]), ALU.mult, ALU.add,
                    [Tpb[bg], Tc, tk("cacc")], [tk("cacc")])
                stt(sap(gc, 0, [(8, 16), (1, 1)]), sap(fpv, q * 32 + 1, [(2, 16), (1, 1)]), cf[:, wof + 1:wof + 2], sap(gc, 0, [(8, 16), (1, 1)]), ALU.mult, ALU.add,
                    [tk("fpv"), Tc, tk("cacc")], [tk("cacc")])
                stt(sap(gc, 0, [(8, 16), (1, 2)]), sap(fpv, q * 32, [(2, 16), (1, 2)]), cf[:, wof:wof + 1], sap(gc, 0, [(8, 16), (1, 2)]), ALU.mult, ALU.add,
                    [tk("fpv"), Tc, tk("cacc")], [tk("cacc")])
                cp(c.act, sap(gtl, q * 32, [(2, 16), (1, 2)]), sap(G, 6, [(8, 16), (1, 2)]), [Tpb[bg]], [tk("gtl")])
                if q == 3 or j == NFF - 1:
                    nq = q + 1
                    j0 = j - q
                    for qq in range(nq):
                        tr(pb[5][0:32, qq * 128:(qq + 1) * 128], gtl[:, qq, :], ident, [tk("gtl"), Tc], [Tpb[5]], sig=(qq == nq - 1))
                    cp(c.act, ktm[0:32, 0:nq * 128], pb[5][0:32, 0:nq * 128], [Tpb[5]], [tk("ktm")])
                    c.dma(c.sp, d_sk, sffn[:, j0 * 128:(j0 + nq) * 128], ktm[0:32, 0:nq * 128], reads=[tk("ktm")], writes=[tk("o_sffn")])
            actf(sg[:, 0:TS], gc[:, 0:TS], AF.Silu, [tk("cacc")], [tk("sqb")])
            i = j % 2
            tt(c.dve, abuf[i][:, 0:TS], sg[:, 0:TS], pb[bu][:, 0:TS], ALU.mult, [tk("sqb"), Tpb[bu]], [tk(f"abuf{i}")])
            c.dma(c.sp, d_a[i], ats[j, :, 0:TS], abuf[i][:, 0:TS], reads=[tk(f"abuf{i}")], writes=[Tats[j]])
        if gate_only:
            return
        stage(9)
        nblk = 0
        for cb in range(8):
            for kb in range(11):
                nk = 8 if kb < 10 else 6
                s, trk = wload([(0, nk, 512, w_down[kb * 1024:kb * 1024 + nk * 128, cb * 512:(cb + 1) * 512])])
                i = nblk % 2
                nblk += 1
                abase = xn[:, 0:8 * TSM] if i == 0 else xt[:].bitcast(BF16)[:, 0:8 * TSM]
                abv = abase.rearrange("p (k t) -> p k t", t=TSM)
                abk = "xn" if i == 0 else "xt"
                c.dma(c.sp, d_ab[i], abv[:, 0:nk, 0:TS], ats[kb * 8:kb * 8 + nk, :, 0:TS].rearrange("k p t -> p k t"),
                      reads=Tats[kb * 8:kb * 8 + nk], writes=[tk(abk)])
                for ci in range(nch):
                    for k8 in range(nk):
                        k = kb * 8 + k8
                        mm(pb[ci][:, :], abv[:, k8, ci * 128:(ci + 1) * 128], wview(s, 0, k8, 512, 0, 512), k == 0, k == NFF - 1,
                           [trk, tk(abk)], [Tpb[ci]], sig=(k8 == nk - 1))
            for ci, ch in enumerate(chunks):
                i2 = cnt["xp"] % 2
                cnt["xp"] += 1
                c.dma(c.sp, d_xp[i2], xpc[i2][:], ch["dst"][:, cb * 512:(cb + 1) * 512], reads=[Ty[ci][cb]], writes=[tk(f"tS{i2}")])
                tt(c.dve, xpc[i2][:], xpc[i2][:], pb[ci][:, :], ALU.add, [tk(f"tS{i2}"), Tpb[ci]], [tk(f"tS{i2}")])
                c.dma(c.sp, d_ys[i2], ch["dst"][:, cb * 512:(cb + 1) * 512], xpc[i2][:], reads=[tk(f"tS{i2}")], writes=[Ty[ci][cb]])

    pre = [{"src": xp[i * 128:(i + 1) * 128, :], "dst": xdum, "kind": "prompt"} for i in range(NPRE)]
    own = [{"src": xp[(NPRE + i) * 128:(NPRE + i + 1) * 128, :], "dst": yp[i * 128:(i + 1) * 128, :], "kind": "prompt"} for i in range(NOWN)]
    own[-1]["kv_out"] = (pk, pv)
    try:
        stage(0)
        first = True
        for s0 in range(0, NPRE - 1, TSC):
            mixer(pre[s0:min(s0 + TSC, NPRE - 1)], first)
            first = False
        if NPRE > 0:
            ch7 = [pre[NPRE - 1]]
            Ty = [[Trk() for _ in range(8)]]
            mixer(ch7, first)
            outproj(ch7, Ty)
            ffn(ch7, Ty, 128, gate_only=True)
            hcol = hmc[:, 0:1]
            tsc(c.dve, ST[:], ST[:], hcol, ALU.mult, [tk("ST"), Tc], [tk("ST")])
            tsc(c.dve, Sb[:], Sb[:], hcol, ALU.mult, [tk("Sb"), Tc], [tk("Sb")])
            tsc(c.dve, sap(ccar, 0, [(1, 96)]), sap(ccar, 0, [(1, 96)]), hcol, ALU.mult, [tk("ccar"), Tc], [tk("ccar")])
            tsc(c.dve, sap(fcar, 0, [(1, 2 * NFF)]), sap(fcar, 0, [(1, 2 * NFF)]), hcol, ALU.mult, [tk("fcar"), Tc], [tk("fcar")])
        for s0 in range(0, NOWN, TSC):
            chunks = own[s0:s0 + TSC]
            Ty = [[Trk() for _ in range(8)] for _ in chunks]
            cur["st"] = s0 // TSC
            mixer(chunks, NPRE == 0 and s0 == 0, pmask=(NPRE > 0 and s0 == 0))
            stage(7)
            outproj(chunks, Ty)
            stage(8)
            ffn(chunks, Ty, len(chunks) * 128)
        if SAMPLE:
            cur["st"] = 50
            chunks = [{"src": xs, "dst": ys, "kind": "sample", "kv_out": ("s",)}]
            Ty = [[Trk() for _ in range(8)]]
            mixer(chunks, False)
            stage(7)
            outproj(chunks, Ty)
            stage(8)
            ffn(chunks, Ty, 128)

        osb = ktm
        d_osb = c.dsem("osb")
        for g4 in range(4):
            for q in range(4):
                t_ = g4 * 4 + q
                tr(pb[0][:, q * 128:(q + 1) * 128], ST[:, t_ * 128:(t_ + 1) * 128], ident, [tk("ST"), Tc], [Tpb[0]], sig=(q == 3))
            cp(c.act, osb[:], pb[0][:, :], [Tpb[0]], [tk("ktm")])
            c.dma(c.sp, d_osb, pssm[g4 * 512:(g4 + 1) * 512, :].rearrange("(q p) n -> p q n", p=128), sap(osb, 0, [(128, 4), (1, 128)]),
                  reads=[tk("ktm")], writes=[tk("o_ssm")])
        for g4 in range(8):
            for q in range(4):
                t_ = g4 * 4 + q
                tr(pb[1][0:3, q * 128:(q + 1) * 128], ccar[:, t_, :], ident, [tk("ccar"), Tc], [Tpb[1]], sig=(q == 3))
            cp(c.act, osb[0:3, :], pb[1][0:3, :], [Tpb[1]], [tk("ktm")])
            c.dma(c.sp, d_osb, pconv[:, g4 * 512:(g4 + 1) * 512], osb[0:3, :], reads=[tk("ktm")], writes=[tk("o_conv")])
        for g4 in range(22):
            nq = 4 if g4 < 21 else 2
            for q in range(nq):
                t_ = g4 * 4 + q
                tr(pb[2][0:2, q * 128:(q + 1) * 128], fcar[:, t_, :], ident, [tk("fcar"), Tc], [Tpb[2]], sig=(q == nq - 1))
            cp(c.act, osb[0:2, 0:nq * 128], pb[2][0:2, 0:nq * 128], [Tpb[2]], [tk("ktm")])
            c.dma(c.sp, d_osb, pffn[:, g4 * 512:g4 * 512 + nq * 128], osb[0:2, 0:nq * 128], reads=[tk("ktm")], writes=[tk("o_ffn")])


    except _Stop:
        pass
    c.finish(c.sp)
    c.emit()
    return nc, c


def _t5_bucket_np(dist):
    n = np.maximum(dist, 0)
    nf = np.maximum(n, 1).astype(np.float32)
    large = 16 + (np.log(nf / 16) / math.log(128 / 16) * 16).astype(np.int32)
    large = np.minimum(large, 31)
    return np.where(n < 16, n, large)


def _static_consts():
    stc = np.zeros((128, NSTC), np.float32)
    stc[:, IDENT:IDENT + 128] = np.eye(128)
    s = np.arange(128)[:, None]
    l = np.arange(128)[None, :]
    stc[:, MASKP:MASKP + 128] = np.where(l >= s, 0.0, NEG)
    stc[:, MASKS:MASKS + 128] = np.where((l >= s) & (l // 8 == s // 8), 0.0, NEG)
    stc[:, ROWM:ROWM + 16] = (s // 8 == np.arange(16)[None, :]).astype(np.float32)
    stc[:, ANTI:ANTI + 128] = np.eye(128)[::-1]
    ohd = np.zeros((33, 384), np.float32)
    for i in range(384):
        d = i - 127
        if 0 <= d < 128:
            ohd[int(_t5_bucket_np(np.array(d))), i] = 1.0
        else:
            ohd[32, i] = NEG
    return stc, ohd


_CACHE = {}


def _prep(inputs, NPRE, NOWN):
    f = lambda a: np.ascontiguousarray(np.asarray(a, dtype=np.float32))
    fm = lambda v: np.ascontiguousarray(v.reshape(-1, 128).T)
    cf = np.zeros((128, NCF), np.float32)
    cf[:, MIXNW:MIXNW + 32] = fm(f(inputs["mix_norm_w"])[0])
    cf[:, FFNNW:FFNNW + 32] = fm(f(inputs["ffn_norm_w"])[0])
    cf[:, QNW] = f(inputs["q_norm_w"])[0]
    cf[:, KNW] = f(inputs["k_norm_w"])[0]
    cw = f(inputs["ssd_conv_w"])[0]
    cf[:, CONVW:CONVW + 128] = cw.reshape(4, 32, 128).transpose(2, 1, 0).reshape(128, 128)
    cf[:, CONVB:CONVB + 32] = fm(f(inputs["ssd_conv_b"])[0])
    cf[:, DEXP:DEXP + 16] = fm(np.repeat(f(inputs["ssd_D"])[0], 64))
    cf[:, SSDNW:SSDNW + 16] = fm(f(inputs["ssd_norm_w"])[0])
    fw_ = f(inputs["ffn_conv_w"])[0]
    cf[:, FCW:FCW + 258] = fw_.reshape(3, NFF, 128).transpose(2, 1, 0).reshape(128, 258)
    cf[:, FCB:FCB + NFF] = fm(f(inputs["ffn_conv_b"])[0])
    c32 = np.zeros((32, NC32), np.float32)
    c32[:, DTB] = f(inputs["ssd_dt_bias"])[0]
    c32[:, ALOG] = f(inputs["ssd_A_log"])[0]
    c32[:, OH32:OH32 + 32] = np.eye(32)
    m = np.ones(128, np.float32)
    m[::8] = 0
    c32[:, MSEG:MSEG + 128] = m[None, :]
    stc, ohd = _static_consts()
    relb = np.concatenate([f(inputs["rel_bias"]), np.ones((1, 16), np.float32)], 0)
    shared = {"w_in": f(inputs["w_in"])[0], "w_out": f(inputs["w_out"])[0], "w_gate": f(inputs["w_gate"])[0],
              "w_up": f(inputs["w_up"])[0], "w_down": f(inputs["w_down"])[0], "cf": cf, "c32": c32, "stc": stc,
              "relb33": relb, "ohd33": ohd, "sinks": f(inputs["attn_sinks"])}
    xp = f(inputs["x_prompt"]); xs = f(inputs["x_sample"])
    sk_ = f(inputs["state_attn_k"])[0]; sv_ = f(inputs["state_attn_v"])[0]; ssm = f(inputs["state_ssm"])[0]
    scv = f(inputs["state_ssd_conv"])[0]; sff = f(inputs["state_ffn_conv"])[0]
    maps = []
    for cidx in range(8):
        sl = slice(16 * cidx, 16 * cidx + 16)
        m_ = dict(shared)
        b, half = cidx // 2, cidx % 2
        if half == 0:
            m_["xp"] = np.ascontiguousarray(np.concatenate([np.zeros((NPRE * 128, D), np.float32), xp[b, :NOWN * 128]], 0))
        else:
            m_["xp"] = np.ascontiguousarray(xp[b, :(NPRE + NOWN) * 128])
        hm = np.zeros((128, 2), np.float32)
        hm[:, 0] = float(half)
        hm[:, 1] = 0.0 if half == 1 else NEG
        m_["hm"] = hm
        m_["xs"] = np.ascontiguousarray(xs[sl].reshape(128, D))
        m_["stk"] = np.ascontiguousarray(sk_[sl].reshape(16, 128, 512))
        m_["stv"] = np.ascontiguousarray(sv_[sl].reshape(16, 128, 512))
        m_["stssm"] = np.ascontiguousarray(ssm[sl].reshape(16, 2048, 128))
        m_["stconv"] = np.ascontiguousarray(scv[sl].reshape(48, D))
        m_["stffn"] = np.ascontiguousarray(sff[sl].reshape(32, DFF))
        maps.append(m_)
    return maps


def run(inputs, NPRE=8, NOWN=8, TSC=4, STOP=99, ncores=8):
    key = (NPRE, NOWN, TSC, STOP)
    if key not in _CACHE:
        _CACHE[key] = build(NPRE, NOWN, TSC, STOP=STOP)[0]
    nc = _CACHE[key]
    maps = _prep(inputs, NPRE, NOWN)
    if STOP < 7 or 50 <= STOP < 60:
        maps = [{k: v for k, v in m.items() if k not in ('w_out', 'w_gate', 'w_up', 'w_down')} for m in maps]
    res = run_bass_kernel_spmd(nc, maps[:ncores], core_ids=list(range(ncores)))
    return res


def kernel(**inputs):
    res = run(inputs)
    R = res.results
    yp = np.stack([np.concatenate([R[2 * b]["yp"].reshape(1024, D), R[2 * b + 1]["yp"].reshape(1024, D)], 0) for b in range(4)])
    ys = np.concatenate([R[cidx]["ys"].reshape(16, 8, D) for cidx in range(8)], 0)
    p_k = np.stack([R[2 * b + 1]["pk"].reshape(128, 4, 128) for b in range(4)])[None]
    p_v = np.stack([R[2 * b + 1]["pv"].reshape(128, 4, 128) for b in range(4)])[None]
    p_ssm = np.stack([R[2 * b + 1]["pssm"].reshape(32, 64, 128) for b in range(4)])[None]
    p_conv = np.stack([R[2 * b + 1]["pconv"].reshape(3, D) for b in range(4)])[None]
    p_ffn = np.stack([R[2 * b + 1]["pffn"].reshape(2, DFF) for b in range(4)])[None]
    s_k = np.concatenate([R[cidx]["sk"].reshape(16, 128, 4, 128) for cidx in range(8)], 0)[None]
    s_v = np.concatenate([R[cidx]["sv"].reshape(16, 128, 4, 128) for cidx in range(8)], 0)[None]
    s_ssm = np.concatenate([R[cidx]["sssm"].reshape(16, 32, 64, 128) for cidx in range(8)], 0)[None]
    s_conv = np.concatenate([R[cidx]["sconv"].reshape(16, 3, D) for cidx in range(8)], 0)[None]
    s_ffn = np.concatenate([R[cidx]["sffn"].reshape(16, 2, DFF) for cidx in range(8)], 0)[None]
    outs = (yp, ys, p_k, p_v, p_ssm, p_conv, p_ffn, s_k, s_v, s_ssm, s_conv, s_ffn)
    return tuple(np.ascontiguousarray(o, dtype=np.float32) for o in outs)
```

```python
import math
import numpy as np
import concourse.bass as bass
import concourse.mybir as mybir
from concourse.alu_op_type import AluOpType as ALU
from concourse.bass_utils import run_bass_kernel_spmd

F32 = mybir.dt.float32
BF16 = mybir.dt.bfloat16
AF = mybir.ActivationFunctionType

D = 4096
DFF = 11008
NFF = 86
IN_DIM = 9248
OFF_Q, OFF_K, OFF_V, OFF_Z, OFF_X, OFF_B, OFF_C, OFF_DT = 0, 2048, 2560, 3072, 5120, 7168, 8192, 9216
EPS = 1e-6
NEG = -30000.0
MIXNW, FFNNW, QNW, KNW, CONVW, CONVB, DEXP, SSDNW, FCW, FCB, NCF = 0, 32, 64, 65, 66, 194, 226, 242, 258, 516, 602
DTB, ALOG, OH32, MSEG, NC32 = 0, 1, 2, 34, 162
IDENT, MASKP, MASKS, ROWM, ANTI, NSTC = 0, 128, 256, 384, 400, 528


class Trk:
    __slots__ = ("w", "r", "excl")

    def __init__(self, excl=False):
        self.w = None
        self.r = {}
        self.excl = excl


class Eng:
    def __init__(self, ctx, name, attr):
        self.name = name
        self.attr = attr
        self.sem = ctx.nc.alloc_semaphore(name="s_" + name)
        self.cnt = 0
        self.seen = {}
        self.prog = []
        self.pend_r = []
        self.pend_w = []


class DSem:
    def __init__(self, ctx, name):
        self.sem = ctx.nc.alloc_semaphore(name="d_" + name)
        self.val = 0
        self.key = "d_" + name


class Ctx:
    def __init__(self, nc):
        self.nc = nc
        self.pe = Eng(self, "pe", "tensor")
        self.act = Eng(self, "act", "scalar")
        self.dve = Eng(self, "dve", "vector")
        self.pool = Eng(self, "pool", "gpsimd")
        self.sp = Eng(self, "sp", "sync")
        self.engs = [self.pe, self.act, self.dve, self.pool, self.sp]
        self.dsems = []
        self.n_ins = 0

    def dsem(self, name):
        d = DSem(self, name)
        self.dsems.append(d)
        return d

    def _deps(self, E, reads, writes):
        deps = []
        for t in reads:
            if t.w is not None:
                deps.append(t.w)
        for t in writes:
            if t.w is not None:
                deps.append(t.w)
            deps.extend(t.r.values())
        for key, sem, val in deps:
            if E.seen.get(key, 0) < val:
                E.prog.append(("wait", sem, val))
                E.seen[key] = val

    def op(self, E, fn, reads=(), writes=(), signal=True):
        if any(t.excl for t in reads):
            writes = list(writes) + [t for t in reads if t.excl]
            reads = [t for t in reads if not t.excl]
        self._deps(E, reads, writes)
        E.pend_r.extend(reads)
        E.pend_w.extend(writes)
        self.n_ins += 1
        if signal:
            E.cnt += 1
            E.prog.append(("op", fn, E.sem, 1))
            stamp = (E.name, E.sem, E.cnt)
            for t in E.pend_w:
                t.w = stamp
                t.r = {}
            for t in E.pend_r:
                t.r[E.name] = stamp
            E.pend_r = []
            E.pend_w = []
        else:
            E.prog.append(("op", fn, None, 0))

    def dma(self, Q, ds, out, in_, reads=(), writes=()):
        self._deps(Q, reads, writes)
        ds.val += 16
        Q.prog.append(("op", lambda h, o=out, i=in_: h.dma_start(out=o, in_=i), ds.sem, 16))
        stamp = (ds.key, ds.sem, ds.val)
        for t in writes:
            t.w = stamp
            t.r = {}
        for t in reads:
            t.r[ds.key] = stamp
        self.n_ins += 1

    def finish(self, E):
        for d in self.dsems:
            if d.val > 0:
                E.prog.append(("wait", d.sem, d.val))
        for X in self.engs:
            if X is not E and X.cnt > 0:
                E.prog.append(("wait", X.sem, X.cnt))

    def emit(self):
        with self.nc.Block() as block:
            for E in self.engs:
                def body(h, E=E):
                    for item in E.prog:
                        if item[0] == "wait":
                            h.wait_ge(item[1], item[2])
                        else:
                            ins = item[1](h)
                            if item[2] is not None:
                                ins.then_inc(item[2], item[3])
                getattr(block, E.attr)(body)


def sap(t, off, dims, parts=128, p0=0):
    Fsz = int(np.prod(t.shape[1:]))
    return bass.AP(t, p0 * Fsz + off, [[Fsz, parts]] + [[int(a), int(b)] for a, b in dims])


class _Stop(Exception):
    pass


def build(NPRE=8, NOWN=8, TSC=4, SAMPLE=True, STOP=99):
    cur = {"st": 0}

    def stage(n):
        if STOP == cur["st"] * 100 + n:
            raise _Stop()
    nc = bass.Bass("TRN2", target_bir_lowering=False)
    TSM = TSC * 128
    di = lambda n, s, dt=F32: nc.dram_tensor(n, s, dt, kind="ExternalInput").ap()
    do = lambda n, s, dt=F32: nc.dram_tensor(n, s, dt, kind="ExternalOutput").ap()
    NPC = NPRE + NOWN
    xp = di("xp", [NPC * 128, D]); xs = di("xs", [128, D])
    hm_d = di("hm", [128, 2])
    stk = di("stk", [16, 128, 512]); stv = di("stv", [16, 128, 512]); stssm = di("stssm", [16, 2048, 128])
    stconv = di("stconv", [48, D]); stffn = di("stffn", [32, DFF])
    w_in = di("w_in", [D, IN_DIM])
    if STOP < 7 or 50 <= STOP < 60:
        w_out = w_gate = w_up = w_down = None
    else:
        w_out = di("w_out", [D, D]); w_gate = di("w_gate", [D, DFF])
        w_up = di("w_up", [D, DFF]); w_down = di("w_down", [DFF, D])
    cf_d = di("cf", [128, NCF]); c32_d = di("c32", [32, NC32]); stc_d = di("stc", [128, NSTC])
    relb_d = di("relb33", [33, 16]); ohd_d = di("ohd33", [33, 384]); sinks_d = di("sinks", [1, 16])
    yp = do("yp", [NOWN * 128, D]); ys = do("ys", [128, D])
    xdum = nc.dram_tensor("xdum", [128, D], F32).ap()
    pk = do("pk", [128, 512]); pv = do("pv", [128, 512]); pssm = do("pssm", [2048, 128])
    pconv = do("pconv", [3, D]); pffn = do("pffn", [2, DFF])
    sk = do("sk", [16, 128, 512]); sv = do("sv", [16, 128, 512]); sssm = do("sssm", [16, 2048, 128])
    sconv = do("sconv", [48, D]); sffn = do("sffn", [32, DFF])
    brow_h = nc.dram_tensor("brow", [16, 384], F32)
    ats_h = nc.dram_tensor("ats", [NFF, 128, TSM], BF16)
    ats = ats_h.ap()

    c = Ctx(nc)
    sbt = lambda n, s, dt=F32: nc.alloc_sbuf_tensor(n, s, dt)
    cf = sbt("cf_s", [128, NCF]); c32 = sbt("c32_s", [32, NC32]); stc = sbt("stc_s", [128, NSTC])
    idb = sbt("idb", [128, 128], BF16); onesb = sbt("onesb", [128, 128], BF16); ones32 = sbt("ones32", [32, 128])
    esink = sbt("esink", [128, 16]); negA = sbt("negA", [32, 1])
    biasC = sbt("biasC", [128, 16, 128], BF16); biasP = sbt("biasP", [128, 16, 128], BF16)
    biasS = sbt("biasS", [128, 16, 128], BF16)
    actT = sbt("actT", [128, 32, TSM], BF16); mixT = sbt("mixT", [128, 32, TSM], BF16)
    RS = 3
    ring = sbt("ring", [128, RS, 4096], BF16)
    KT = sbt("KT", [128, 4, 128 + TSM], BF16); Vtm = sbt("Vtm", [128, TSC + 1, 512], BF16)
    ST = sbt("ST", [128, 2048]); Sb = sbt("Sb", [128, 2048], BF16)
    ccar = sbt("ccar", [128, 32, 3]); fcar = sbt("fcar", [128, NFF, 2])
    xt = sbt("xt", [128, D]); xn = sbt("xn", [128, D], BF16)
    den = sbt("den", [128, 512])
    tS = [sbt(f"tS{i}", [128, 512]) for i in range(2)]
    Eb = [sbt(f"Eb{i}", [128, 512], BF16) for i in range(2)]
    pb = [nc.alloc_psum_tensor(f"pb{i}", [128, 512], F32) for i in range(7)]
    ptb = nc.alloc_psum_tensor("ptb", [128, 1024], BF16)
    Tpb = [Trk(excl=True) for _ in range(7)]
    Tptb = Trk(excl=True)
    T = {}

    def tk(name):
        if name not in T:
            T[name] = Trk()
        return T[name]

    def tt(E, out, in0, in1, op, r, w):
        c.op(E, lambda h: h.tensor_tensor(out=out, in0=in0, in1=in1, op=op), reads=r, writes=w)

    def tsc(E, out, in0, s1, op0, r, w, s2=None, op1=None):
        if op1 is None:
            c.op(E, lambda h: h.tensor_scalar(out=out, in0=in0, scalar1=s1, scalar2=None, op0=op0), reads=r, writes=w)
        else:
            c.op(E, lambda h: h.tensor_scalar(out=out, in0=in0, scalar1=s1, scalar2=s2, op0=op0, op1=op1), reads=r, writes=w)

    def stt(out, in0, sc, in1, op0, op1, r, w):
        c.op(c.dve, lambda h: h.scalar_tensor_tensor(out=out, in0=in0, scalar=sc, in1=in1, op0=op0, op1=op1), reads=r, writes=w)

    def actf(out, in_, func, r, w, **kw):
        c.op(c.act, lambda h: h.activation(out=out, in_=in_, func=func, **kw), reads=r, writes=w)

    def mm(out, lhsT, rhs, st, sp_, r, w, sig=True):
        c.op(c.pe, lambda h: h.matmul(out=out, lhsT=lhsT, rhs=rhs, start=st, stop=sp_), reads=r, writes=w, signal=sig)

    def tr(out, in_, ident, r, w, sig=True):
        c.op(c.pe, lambda h: h.transpose(out=out, in_=in_, identity=ident), reads=r, writes=w, signal=sig)

    def cp(E, out, in_, r, w):
        if E is c.act:
            actf(out, in_, AF.Copy, r, w)
        else:
            c.op(E, lambda h: h.tensor_copy(out=out, in_=in_), reads=r, writes=w)

    def mset(E, ap, v, w):
        c.op(E, lambda h: h.memset(ap, v), writes=w)

    def rstd_from(dst, src, scale, r, w):
        tsc(c.dve, dst, src, scale, ALU.mult, r, w, s2=EPS, op1=ALU.add)
        actf(dst, dst, AF.Sqrt, w, w)
        c.op(c.dve, lambda h: h.reciprocal(out=dst, in_=dst), reads=w, writes=w)

    ring_state = {"i": 0}
    ring_trk = [Trk() for _ in range(RS)]
    ring_ds = [c.dsem(f"w{i}") for i in range(RS)]

    def wload(parts):
        s = ring_state["i"] % RS
        ring_state["i"] += 1
        for n, (off, kt, ncol, src) in enumerate(parts):
            dst = sap(ring, s * 4096 + off, [(ncol, kt), (1, ncol)])
            c.dma(c.pool, ring_ds[s], dst, src.rearrange("(k p) n -> p k n", p=128), writes=[ring_trk[s]] if True else [])
        return s, ring_trk[s]

    def wview(s, off, k, ncol, c0, c1):
        return sap(ring, s * 4096 + off + k * ncol + c0, [(1, c1 - c0)])

    d_c = c.dsem("const")
    Tc = tk("const")
    c.dma(c.sp, d_c, cf[:], cf_d, writes=[Tc])
    c.dma(c.sp, d_c, c32[:], c32_d, writes=[Tc])
    c.dma(c.sp, d_c, stc[:], stc_d, writes=[Tc])
    hmc = sbt("hmc", [128, 2])
    c.dma(c.sp, d_c, hmc[:], hm_d, writes=[Tc])
    c.dma(c.sp, d_c, esink[:], bass.AP(sinks_d.tensor, 0, [[0, 128], [1, 16]]), writes=[Tc])
    relb_s = sbt("relb_s", [33, 16]); ohd_s = sbt("ohd_s", [33, 384])
    c.dma(c.sp, d_c, relb_s[:], relb_d, writes=[Tc])
    c.dma(c.sp, d_c, ohd_s[:], ohd_d, writes=[Tc])
    Tc.w = (d_c.key, d_c.sem, d_c.val)
    ident = stc[:, IDENT:IDENT + 128]
    cp(c.dve, idb[:], ident, [Tc], [tk("idb")])
    mset(c.dve, onesb[:], 1.0, [tk("onesb")])
    mset(c.dve, ones32[:], 1.0, [tk("ones32")])
    mset(c.dve, KT[:], 0.0, [tk("KT")])
    mset(c.dve, Vtm[:], 0.0, [tk("Vtm")])
    mset(c.dve, ST[:], 0.0, [tk("ST")])
    mset(c.dve, Sb[:], 0.0, [tk("Sb")])
    mset(c.dve, ccar[:], 0.0, [tk("ccar")])
    mset(c.dve, fcar[:], 0.0, [tk("fcar")])
    actf(esink[:], esink[:], AF.Exp, [Tc], [tk("esink")])
    actf(negA[:], c32[:, ALOG:ALOG + 1], AF.Exp, [Tc], [tk("negA")])
    tsc(c.dve, negA[:], negA[:], -1.0, ALU.mult, [tk("negA")], [tk("negA")])
    mm(pb[0][0:16, 0:384], relb_s[:], ohd_s[:], True, True, [Tc], [Tpb[0]])
    brow_s = den
    cp(c.dve, den[0:16, 0:384], pb[0][0:16, 0:384], [Tpb[0]], [tk("den")])
    d_b = c.dsem("brow")
    c.dma(c.sp, d_b, brow_h.ap(), den[0:16, 0:384], reads=[tk("den")], writes=[tk("brow_d")])
    hank = xt
    for which, dst in ((0, biasC), (128, biasP)):
        c.dma(c.sp, d_b, sap(xt, 0, [(128, 16), (1, 128)]), bass.AP(brow_h, which, [[1, 128], [384, 16], [1, 128]]), reads=[tk("brow_d")], writes=[tk("xt")])
        for q in range(4):
            mm(pb[q][:, :], stc[:, ANTI:ANTI + 128], sap(xt, q * 512, [(1, 512)]), True, True, [Tc, tk("xt")], [Tpb[q]])
            cp(c.dve, sap(dst, q * 512, [(1, 512)]), pb[q][:, :], [Tpb[q]], [tk("bias")])

    tt(c.dve, biasS[:], biasC[:], sap(stc, MASKS, [(0, 16), (1, 128)]), ALU.add, [tk("bias"), Tc], [tk("bias")])
    st48 = sbt("st48", [48, 128]); tl48 = sbt("tl48", [128, 48]); so48 = sbt("so48", [48, 128])
    xrs = sbt("xrs", [128, 16, 11]); cdx = sbt("cdx", [128, 2, 16]); ecdS = sbt("ecdS", [32, 16])
    fpv = sbt("fpv", [128, 4, 32]); gtl = sbt("gtl", [128, 4, 32])
    d_s48 = c.dsem("s48")
    d_sk = c.dsem("sk"); d_sv = c.dsem("sv"); d_cp = c.dsem("cpy"); d_kw = c.dsem("kw"); d_h0 = c.dsem("h0")
    d_so = c.dsem("so"); d_stg = c.dsem("stg")
    xtb = xt[:].bitcast(BF16)
    Kwj = xtb[:, 0:2048].rearrange("p (i d) -> p i d", d=128)
    Vwj = xtb[:, 2048:4096].rearrange("p (i d) -> p i d", d=128)
    KwT = xtb[:, 4096:6144].rearrange("p (i d) -> p i d", d=128)
    h0g = xt[:].rearrange("p (i t n) -> p i t n", t=2, n=128)
    SiT = xn[:].rearrange("p (i f) -> p i f", f=256)
    d_x = c.dsem("x")
    ss = sbt("ss", [128, 1]); rs = sbt("rs", [128, 1])

    def norm_T(src_rows, src_trk, cidx, nw_off):
        c.dma(c.sp, d_x, xt[:], src_rows, reads=src_trk, writes=[tk("xt")])
        c.op(c.act, lambda h: h.activation(out=xn[:], in_=xt[:], func=AF.Square, accum_out=ss[:]),
             reads=[tk("xt")], writes=[tk("xn"), tk("ss")])
        rstd_from(rs[:], ss[:], 1.0 / D, [tk("ss")], [tk("rs")])
        tsc(c.dve, xn[:], xt[:], rs[:, 0:1], ALU.mult, [tk("xt"), tk("rs")], [tk("xn")])
        for g in range(8):
            for j in range(4):
                k = g * 4 + j
                tr(ptb[:, j * 128:(j + 1) * 128], xn[:, k * 128:(k + 1) * 128], idb[:], [tk("xn"), tk("idb")], [Tptb], sig=(j == 3))
            tt(c.dve, sap(actT, (g * 4) * TSM + cidx * 128, [(TSM, 4), (1, 128)]), sap(ptb, 0, [(128, 4), (1, 128)]),
               sap(cf, nw_off + g * 4, [(1, 4), (0, 128)]), ALU.mult, [Tptb, Tc], [tk("actT")])

    def proj_tile(col, TS, bank, src_w, ncols=128):
        s, trk = wload([(0, 32, ncols, src_w[:, col:col + ncols])])
        for k in range(32):
            mm(pb[bank][0:ncols, 0:TS], wview(s, 0, k, ncols, 0, ncols), sap(actT, k * TSM, [(1, TS)]), k == 0, k == 31,
               [trk, tk("actT")], [Tpb[bank]], sig=(k == 31))

    sqb = sbt("sqb", [128, TSM], BF16); rsb = sbt("rsb", [128, TSM]); hn = sbt("hn", [128, TSM])

    def headnorm(bank, TS, wcol):
        actf(sqb[:, 0:TS], pb[bank][:, 0:TS], AF.Square, [Tpb[bank]], [tk("sqb")])
        mm(pb[3][:, 0:TS], onesb[:], sqb[:, 0:TS], True, True, [tk("onesb"), tk("sqb")], [Tpb[3]])
        rstd_from(rsb[:, 0:TS], pb[3][:, 0:TS], 1.0 / 128, [Tpb[3]], [tk("rsb")])
        stt(hn[:, 0:TS], pb[bank][:, 0:TS], cf[:, wcol:wcol + 1], rsb[:, 0:TS], ALU.mult, ALU.mult, [Tpb[bank], Tc, tk("rsb")], [tk("hn")])

    QT = sbt("QT", [128, 4, TSM], BF16)
    ktm = sbt("ktm", [128, 512]); vtmf = sbt("vtmf", [128, 512])
    if 4 * TSM >= 2048:
        Bm = QT[:].rearrange("p a b -> p (a b)")[:, 0:2048].rearrange("p (i n) -> p i n", n=128)
    else:
        Bm = sbt("Bm", [128, 16, 128], BF16)
    stg = vtmf
    d_o = c.dsem("outs")

    def attn_sample(j):
        c.dma(c.pool, d_kw, Kwj, stk[:, :, j * 128:(j + 1) * 128].rearrange("i s d -> s i d"), writes=[tk("xt")])
        c.dma(c.pool, d_kw, Vwj, stv[:, :, j * 128:(j + 1) * 128].rearrange("i s d -> s i d"), writes=[tk("xt")])
        for i4 in range(4):
            for q in range(4):
                i = i4 * 4 + q
                tr(ptb[:, q * 128:(q + 1) * 128], Kwj[:, i, :], idb[:], [tk("xt"), tk("idb")], [Tptb], sig=(q == 3))
            cp(c.act, KwT[:, i4 * 4:i4 * 4 + 4, :], ptb[:, 0:512].rearrange("p (i d) -> p i d", d=128), [Tptb], [tk("xt")])
        mm(pb[4][:, :], KT[:, j, 128:256], sap(QT, 0, [(TSM, 4), (1, 128)]), True, True, [tk("KT"), tk("QT")], [Tpb[4]])
        tt(c.dve, tS[0][:], pb[4][:, :], sap(biasS, 4 * j * 128, [(1, 512)]), ALU.add, [Tpb[4], tk("bias")], [tk("tS0")])
        actf(Eb[0][:], tS[0][:], AF.Exp, [tk("tS0")], [tk("Eb0")])
        for i in range(16):
            mm(pb[5][:, i * 32:(i + 1) * 32], KwT[:, i, :], sap(QT, 8 * i, [(TSM, 4), (1, 8)]), True, True, [tk("xt"), tk("QT")], [Tpb[5]], sig=(i == 15))
        v3 = [(32, 16), (8, 4), (1, 8)]
        tt(c.dve, sap(tS[1], 0, v3), sap(pb[5], 0, v3), sap(biasP, 4 * j * 128, [(0, 16), (128, 4), (1, 8)]), ALU.add, [Tpb[5], tk("bias")], [tk("tS1")])
        actf(Eb[1][:], tS[1][:], AF.Exp, [tk("tS1")], [tk("Eb1")])
        for lhs_w, lhs_s, bank in ((None, None, 6), (onesb, onesb, 3)):
            for i in range(16):
                lw = Vwj[:, i, :] if lhs_w is None else onesb[:]
                ls = Vtm[:, 1, j * 128:(j + 1) * 128] if lhs_s is None else onesb[:]
                mm(pb[bank][:, i * 32:(i + 1) * 32], lw, Eb[1][:, i * 32:(i + 1) * 32], True, False, [tk("xt"), tk("Eb1"), tk("onesb")], [Tpb[bank]], sig=False)
                mm(pb[bank][:, i * 32:(i + 1) * 32], ls, sap(Eb[0], 8 * i, [(128, 4), (1, 8)]), False, True, [tk("Vtm"), tk("Eb0"), tk("onesb")], [Tpb[bank]], sig=(i == 15))
        tt(c.dve, sap(den, 0, v3), sap(pb[3], 0, v3), sap(esink, 4 * j, [(0, 16), (1, 4), (0, 8)]), ALU.add, [Tpb[3], tk("esink")], [tk("den")])
        c.op(c.dve, lambda h: h.reciprocal(out=den[:], in_=den[:]), reads=[tk("den")], writes=[tk("den")])
        tt(c.dve, sap(mixT, 4 * j * TSM, [(8, 16), (TSM, 4), (1, 8)]), sap(pb[6], 0, v3), sap(den, 0, v3), ALU.mult, [Tpb[6], tk("den")], [tk("mixT")])

    def mixer(chunks, first_seq_chunk, pmask=False, state_only=False):
        nch = len(chunks)
        TS = nch * 128
        sample = chunks[0]["kind"] == "sample"
        if sample:
            c.dma(c.sp, d_cp, sk[:, 0:120, :], stk[:, 8:128, :], writes=[tk("o_k")])
            c.dma(c.sp, d_cp, sv[:, 0:120, :], stv[:, 8:128, :], writes=[tk("o_v")])
        for ci, ch in enumerate(chunks):
            norm_T(ch["src"], [], ci, MIXNW)
        stage(1)
        for j in range(4):
            bank = j % 2
            proj_tile(OFF_K + j * 128, TS, bank, w_in)
            headnorm(bank, TS, KNW)
            cp(c.act, KT[:, j, 128:128 + TS], hn[:, 0:TS], [tk("hn")], [tk("KT")])
            for ci, ch in enumerate(chunks):
                if ch.get("kv_out") is not None:
                    tr(pb[4][:, j * 128:(j + 1) * 128], hn[:, ci * 128:(ci + 1) * 128], ident, [tk("hn"), Tc], [Tpb[4]])
                    cp(c.dve, ktm[:, j * 128:(j + 1) * 128], pb[4][:, j * 128:(j + 1) * 128], [Tpb[4]], [tk("ktm")])
        for ci, ch in enumerate(chunks):
            if ch.get("kv_out") is not None:
                if ch["kind"] == "sample":
                    for i in range(16):
                        c.dma(c.sp, d_sk, sk[i, 120:128, :], ktm[8 * i:8 * i + 8, :], reads=[tk("ktm")], writes=[tk("o_k2")])
                else:
                    c.dma(c.sp, d_sk, ch["kv_out"][0], ktm[:], reads=[tk("ktm")], writes=[tk("o_k")])
        stage(2)
        for j in range(4):
            sv_, tv_ = wload([(0, 32, 128, w_in[:, OFF_V + j * 128:OFF_V + (j + 1) * 128])])
            for ci, ch in enumerate(chunks):
                for k in range(32):
                    mm(pb[ci][:, j * 128:(j + 1) * 128], sap(actT, k * TSM + ci * 128, [(1, 128)]), wview(sv_, 0, k, 128, 0, 128),
                       k == 0, k == 31, [tv_, tk("actT")], [Tpb[ci]], sig=(k == 31))
        stage(21)
        for ci, ch in enumerate(chunks):
            cp(c.act, Vtm[:, 1 + ci, :], pb[ci][:, :], [Tpb[ci]], [tk("Vtm")])
            stage(22 + ci)
            if ch.get("kv_out") is not None:
                cp(c.act, vtmf[:], pb[ci][:, :], [Tpb[ci]], [tk("vtmf")])
                if ch["kind"] == "sample":
                    for i in range(16):
                        c.dma(c.sp, d_sv, sv[i, 120:128, :], vtmf[8 * i:8 * i + 8, :], reads=[tk("vtmf")], writes=[tk("o_v2")])
                else:
                    c.dma(c.sp, d_sv, ch["kv_out"][1], vtmf[:], reads=[tk("vtmf")], writes=[tk("o_v")])
        stage(3)
        for j in (() if state_only else range(4)):
            for hh in range(4):
                bank = hh % 2
                proj_tile(OFF_Q + (4 * j + hh) * 128, TS, bank, w_in)
                headnorm(bank, TS, QNW)
                actf(QT[:, hh, 0:TS], hn[:, 0:TS], AF.Copy, [tk("hn")], [tk("QT")], scale=128 ** -0.5)
            if sample:
                attn_sample(j)
            for ci, ch in enumerate(chunks):
                if ch["kind"] == "prompt":
                    blocks = [(1, biasC)]
                    if not (first_seq_chunk and ci == 0):
                        blocks.append((0, biasP))
                    for bi, (rel, btab) in enumerate(blocks):
                        kcol = 128 + ci * 128 if rel == 1 else ci * 128
                        mm(pb[4 + bi][:, :], KT[:, j, kcol:kcol + 128], sap(QT, ci * 128, [(TSM, 4), (1, 128)]), True, True,
                           [tk("KT"), tk("QT")], [Tpb[4 + bi]])
                        tt(c.dve, tS[bi][:], pb[4 + bi][:, :], sap(btab, 4 * j * 128, [(1, 512)]), ALU.add, [Tpb[4 + bi], tk("bias")], [tk(f"tS{bi}")])
                        if pmask and ci == 0 and rel == 0:
                            actf(Eb[bi][:], tS[bi][:], AF.Exp, [tk(f"tS{bi}"), Tc], [tk(f"Eb{bi}")], bias=hmc[:, 1:2])
                        else:
                            actf(Eb[bi][:], tS[bi][:], AF.Exp, [tk(f"tS{bi}")], [tk(f"Eb{bi}")])
                    nb = len(blocks)
                    for bi, (rel, btab) in enumerate(blocks):
                        vsl = 1 + ci if rel == 1 else ci
                        mm(pb[6][:, :], Vtm[:, vsl, j * 128:(j + 1) * 128], Eb[bi][:], bi == 0, bi == nb - 1,
                           [tk("Vtm"), tk(f"Eb{bi}")], [Tpb[6]], sig=(bi == nb - 1))
                    for bi in range(nb):
                        mm(pb[3][:, :], onesb[:], Eb[bi][:], bi == 0, bi == nb - 1, [tk("onesb"), tk(f"Eb{bi}")], [Tpb[3]], sig=(bi == nb - 1))
                    tt(c.dve, den[:], pb[3][:, :], sap(esink, 4 * j, [(1, 4), (0, 128)]), ALU.add, [Tpb[3], tk("esink")], [tk("den")])
                    c.op(c.dve, lambda h: h.reciprocal(out=den[:], in_=den[:]), reads=[tk("den")], writes=[tk("den")])
                    tt(c.dve, sap(mixT, 4 * j * TSM + ci * 128, [(TSM, 4), (1, 128)]), pb[6][:, :], den[:], ALU.mult,
                       [Tpb[6], tk("den")], [tk("mixT")])
        stage(4)
        cp(c.act, KT[:, :, 0:128], KT[:, :, TS:TS + 128], [tk("KT")], [tk("KT")])
        cp(c.act, Vtm[:, 0, :], Vtm[:, nch, :], [tk("Vtm")], [tk("Vtm")])
        ssd(chunks, TS, state_only)

    dtT = sbt("dtT", [32, TSM]); a2T = sbt("a2T", [32, TSM]); acsT = sbt("acsT", [32, TSM]); dendT = sbt("dendT", [32, TSM])
    ecd = sbt("ecd", [32, TSC]); tm = sbt("tm", [128, TSC, 96])
    xr = sbt("xr", [128, 3 + TSM]); cacc = sbt("cacc", [128, TSM])
    xsT = sbt("xsT", [128, 2, TSM], BF16); BT = sbt("BT", [128, TSM], BF16); CT = sbt("CT", [128, TSM], BF16)
    szT = sbt("szT", [128, 2, TSM], BF16)
    xdt = sbt("xdt", [128, 256], BF16); xdtd = sbt("xdtd", [128, 256], BF16); Btm = sbt("Btm", [128, 128], BF16)
    m4 = den; t1 = tS[0]; EA = tS[1]; MT = Eb[0]; ChT = Eb[1]
    e256 = sbt("e256", [32, 256]); ytmp = sbt("ytmp", [128, 128]); sqy = sbt("sqy", [128, 128], BF16)
    rsy = rsb

    def conv_tile(bank, TS, ft, dst, silu=True):
        cp(c.act, xr[:, 3:3 + TS], pb[bank][:, 0:TS], [Tpb[bank]], [tk("xr")])
        cp(c.act, xr[:, 0:3], ccar[:, ft, :], [tk("ccar")], [tk("xr")])
        cp(c.act, ccar[:, ft, :], xr[:, TS:TS + 3], [tk("xr")], [tk("ccar")])
        wof = CONVW + ft * 4
        c.op(c.act, lambda h: h.activation(out=cacc[:, 0:TS], in_=xr[:, 3:3 + TS], func=AF.Identity,
                                           scale=cf[:, wof + 3:wof + 4], bias=cf[:, CONVB + ft:CONVB + ft + 1]),
             reads=[tk("xr"), Tc], writes=[tk("cacc")])
        for t_ in range(3):
            stt(cacc[:, 0:TS], xr[:, t_:t_ + TS], cf[:, wof + t_:wof + t_ + 1], cacc[:, 0:TS], ALU.mult, ALU.add,
                [tk("xr"), Tc, tk("cacc")], [tk("cacc")])
        actf(dst, cacc[:, 0:TS], AF.Silu, [tk("cacc")], [tk("ssdin")])

    def conv_tile_s(bank, ft, dst):
        v8 = [(11, 16), (1, 8)]
        cp(c.act, sap(xrs, 3, v8), sap(pb[bank], 0, [(8, 16), (1, 8)]), [Tpb[bank]], [tk("xr")])
        c.dma(c.sp, d_stg, st48[:], stconv[:, ft * 128:(ft + 1) * 128], writes=[tk("st48")])
        tr(pb[4][:, 0:48], st48[:], stc[0:48, IDENT:IDENT + 48], [tk("st48"), Tc], [Tpb[4]])
        cp(c.act, sap(xrs, 0, [(11, 16), (1, 3)]), sap(pb[4], 0, [(3, 16), (1, 3)]), [Tpb[4]], [tk("xr")])
        cp(c.act, sap(tl48, 0, [(3, 16), (1, 3)]), sap(xrs, 8, [(11, 16), (1, 3)]), [tk("xr")], [tk("tl48")])
        tr(pb[4][0:48, 128:256], tl48[:], ident, [tk("tl48"), Tc], [Tpb[4]])
        cp(c.act, so48[:], pb[4][0:48, 128:256], [Tpb[4]], [tk("so48")])
        c.dma(c.sp, d_s48, sconv[:, ft * 128:(ft + 1) * 128], so48[:], reads=[tk("so48")], writes=[tk("o_sconv")])
        wof = CONVW + ft * 4
        o8 = sap(cacc, 0, [(8, 16), (1, 8)])
        c.op(c.act, lambda h: h.activation(out=o8, in_=sap(xrs, 3, v8), func=AF.Identity,
                                           scale=cf[:, wof + 3:wof + 4], bias=cf[:, CONVB + ft:CONVB + ft + 1]),
             reads=[tk("xr"), Tc], writes=[tk("cacc")])
        for t_ in range(3):
            stt(o8, sap(xrs, t_, v8), cf[:, wof + t_:wof + t_ + 1], o8, ALU.mult, ALU.add, [tk("xr"), Tc, tk("cacc")], [tk("cacc")])
        actf(dst, cacc[:, 0:128], AF.Silu, [tk("cacc")], [tk("ssdin")])

    def ssd(chunks, TS, state_only=False):
        nch = len(chunks)
        sample = chunks[0]["kind"] == "sample"
        s, trk = wload([(0, 32, 32, w_in[:, OFF_DT:OFF_DT + 32])])
        for k in range(32):
            mm(pb[0][0:32, 0:TS], wview(s, 0, k, 32, 0, 32), sap(actT, k * TSM, [(1, TS)]), k == 0, k == 31, [trk, tk("actT")], [Tpb[0]], sig=(k == 31))
        actf(dtT[:, 0:TS], pb[0][0:32, 0:TS], AF.Exp, [Tpb[0], Tc], [tk("dtT")], bias=c32[:, DTB:DTB + 1])
        actf(dtT[:, 0:TS], dtT[:, 0:TS], AF.Ln, [tk("dtT")], [tk("dtT")], bias=1.0)
        tsc(c.dve, a2T[:, 0:TS], dtT[:, 0:TS], negA[:, 0:1], ALU.mult, [tk("dtT"), tk("negA")], [tk("a2T")])
        for ci, ch in enumerate(chunks):
            cs = slice(ci * 128, (ci + 1) * 128)
            if ch["kind"] == "prompt":
                msk = sap(ones32, 0, [(1, 128)], parts=32)
                nseg, sl = 1, 128
            else:
                msk = c32[:, MSEG:MSEG + 128]
                nseg, sl = 16, 8
            c.op(c.dve, lambda h, cs=cs, msk=msk: h.tensor_tensor_scan(out=acsT[:, cs], data0=msk, data1=a2T[:, cs], initial=0.0,
                                                                      op0=ALU.mult, op1=ALU.add),
                 reads=[tk("a2T"), tk("ones32"), Tc], writes=[tk("acsT")])
            last = sap(acsT, ci * 128 + sl - 1, [(sl, nseg), (0, sl)], parts=32)
            tt(c.dve, sap(dendT, ci * 128, [(sl, nseg), (1, sl)], parts=32), last, sap(acsT, ci * 128, [(sl, nseg), (1, sl)], parts=32),
               ALU.subtract, [tk("acsT")], [tk("dendT")])
            actf(dendT[:, cs], dendT[:, cs], AF.Exp, [tk("dendT")], [tk("dendT")])
            if ch["kind"] == "prompt":
                actf(ecd[:, ci:ci + 1], acsT[:, ci * 128 + 127:ci * 128 + 128], AF.Exp, [tk("acsT")], [tk("ecd")])
            else:
                actf(ecdS[:, :], sap(acsT, ci * 128 + 7, [(8, 16)], parts=32), AF.Exp, [tk("acsT")], [tk("ecd")])
            for q, src in enumerate((dtT, acsT, dendT)):
                tr(pb[1][:, q * 32:(q + 1) * 32], src[:, cs], stc[0:32, IDENT:IDENT + 32], [tk("dtT"), tk("acsT"), tk("dendT"), Tc], [Tpb[1]], sig=(q == 2))
            cp(c.act, tm[:, ci, :], pb[1][:, 0:96], [Tpb[1]], [tk("tm")])
        stage(5)
        for g in range(8):
            cvt = (lambda bank, TS_, ft, dst: conv_tile_s(bank, ft, dst)) if sample else conv_tile
            for t2 in range(2):
                proj_tile(OFF_X + g * 256 + t2 * 128, TS, t2, w_in)
                cvt(t2, TS, 2 * g + t2, xsT[:, t2, 0:TS])
            proj_tile(OFF_B + g * 128, TS, 0, w_in)
            cvt(0, TS, 16 + g, BT[:, 0:TS])
            proj_tile(OFF_C + g * 128, TS, 1, w_in)
            cvt(1, TS, 24 + g, CT[:, 0:TS])
            if sample:
                for t2 in range(2):
                    c.dma(c.sp, d_h0, h0g[:, :, t2, :], stssm[:, g * 256 + t2 * 128:g * 256 + (t2 + 1) * 128, :].rearrange("i p n -> p i n"), writes=[tk("xt")])
                for i in range(16):
                    for t2 in range(2):
                        tr(pb[2][:, t2 * 128:(t2 + 1) * 128], h0g[:, i, t2, :], ident, [tk("xt"), Tc], [Tpb[2]], sig=(t2 == 1))
                    cp(c.act, SiT[:, i, :], pb[2][:, 0:256], [Tpb[2]], [tk("xn")])
            for t2 in (() if state_only else range(2)):
                proj_tile(OFF_Z + g * 256 + t2 * 128, TS, t2, w_in)
                actf(szT[:, t2, 0:TS], pb[t2][:, 0:TS], AF.Silu, [Tpb[t2]], [tk("szT")])
            stage(52)
            for ci, ch in enumerate(chunks):
                cs = slice(ci * 128, (ci + 1) * 128)
                prompt = ch["kind"] == "prompt"
                for t2 in range(2):
                    tr(ptb[:, t2 * 128:(t2 + 1) * 128], xsT[:, t2, cs], idb[:], [tk("ssdin"), tk("idb")], [Tptb], sig=False)
                tr(ptb[:, 256:384], BT[:, cs], idb[:], [tk("ssdin"), tk("idb")], [Tptb])
                tt(c.dve, sap(xdt, 0, [(64, 4), (1, 64)]), sap(ptb, 0, [(64, 4), (1, 64)]), sap(tm, ci * 96 + 4 * g, [(1, 4), (0, 64)]),
                   ALU.mult, [Tptb, tk("tm")], [tk("xdt")])
                tt(c.dve, sap(xdtd, 0, [(64, 4), (1, 64)]), sap(xdt, 0, [(64, 4), (1, 64)]), sap(tm, ci * 96 + 64 + 4 * g, [(1, 4), (0, 64)]),
                   ALU.mult, [tk("xdt"), tk("tm")], [tk("xdtd")])
                cp(c.act, Btm[:], ptb[:, 256:384], [Tptb], [tk("Btm")])
                stage(53)
                if not state_only:
                    mm(pb[4][:, 0:128], BT[:, cs], CT[:, cs], True, True, [tk("ssdin")], [Tpb[4]])
                    tt(c.dve, sap(m4, 0, [(128, 4), (1, 128)], parts=32), sap(acsT, ci * 128, [(0, 4), (1, 128)], parts=32),
                       sap(c32, OH32 + 4 * g, [(1, 4), (0, 128)], parts=32), ALU.mult, [tk("acsT"), Tc], [tk("den")])
                    mm(pb[5][:, :], ones32[:], m4[0:32, :], True, True, [tk("ones32"), tk("den")], [Tpb[5]])
                    moff = MASKP if prompt else MASKS
                    tt(c.dve, t1[:], pb[5][:, :], sap(stc, moff, [(0, 4), (1, 128)]), ALU.add, [Tpb[5], Tc], [tk("tS0")])
                    tt(c.dve, t1[:], t1[:], sap(tm, ci * 96 + 32 + 4 * g, [(1, 4), (0, 128)]), ALU.subtract, [tk("tS0"), tk("tm")], [tk("tS0")])
                    actf(t1[:], t1[:], AF.Exp, [tk("tS0")], [tk("tS0")])
                    tt(c.dve, MT[:], t1[:], sap(pb[4], 0, [(0, 4), (1, 128)]), ALU.mult, [tk("tS0"), Tpb[4]], [tk("Eb0")])
                    actf(EA[:], pb[5][:, :], AF.Exp, [Tpb[5]], [tk("tS1")])
                    tt(c.dve, ChT[:], EA[:], sap(CT, ci * 128, [(0, 4), (1, 128)]), ALU.mult, [tk("tS1"), tk("ssdin")], [tk("Eb1")])
                    stage(54)
                    for hh in range(4):
                        o = pb[6][(hh % 2) * 64:(hh % 2) * 64 + 64, (hh // 2) * 128:(hh // 2) * 128 + 128]
                        if prompt:
                            mm(o, xdt[:, hh * 64:(hh + 1) * 64], MT[:, hh * 128:(hh + 1) * 128], True, False, [tk("xdt"), tk("Eb0")], [Tpb[6]], sig=False)
                            mm(o, Sb[:, g * 256 + hh * 64:g * 256 + (hh + 1) * 64], ChT[:, hh * 128:(hh + 1) * 128], False, True,
                               [tk("Sb"), tk("Eb1")], [Tpb[6]], sig=(hh == 3))
                        else:
                            mm(o, xdt[:, hh * 64:(hh + 1) * 64], MT[:, hh * 128:(hh + 1) * 128], True, False, [tk("xdt"), tk("Eb0")], [Tpb[6]], sig=False)
                            for i in range(16):
                                oi = pb[6][(hh % 2) * 64:(hh % 2) * 64 + 64, (hh // 2) * 128 + 8 * i:(hh // 2) * 128 + 8 * i + 8]
                                mm(oi, SiT[:, i, hh * 64:(hh + 1) * 64], ChT[:, hh * 128 + 8 * i:hh * 128 + 8 * i + 8], False, True,
                                   [tk("xn"), tk("Eb1")], [Tpb[6]], sig=(hh == 3 and i == 15))
                    stage(55)
                    for t2 in range(2):
                        ft = 2 * g + t2
                        stt(ytmp[:], xsT[:, t2, cs], cf[:, DEXP + ft:DEXP + ft + 1], pb[6][:, t2 * 128:(t2 + 1) * 128], ALU.mult, ALU.add,
                            [tk("ssdin"), Tc, Tpb[6]], [tk("ytmp")])
                        tt(c.dve, mixT[:, 16 + ft, cs], ytmp[:], szT[:, t2, cs], ALU.mult, [tk("ytmp"), tk("szT")], [tk("mixT")])
                stage(56)
                if prompt:
                    mm(pb[4][:, 128:384], Btm[:], xdtd[:], True, True, [tk("Btm"), tk("xdtd")], [Tpb[4]])
                    tsc(c.dve, sap(e256, 0, [(64, 4), (1, 64)], parts=32), sap(c32, OH32 + 4 * g, [(1, 4), (0, 64)], parts=32), ecd[:, ci:ci + 1], ALU.mult, [Tc, tk("ecd")], [tk("e256")])
                    mm(pb[5][:, 0:256], ones32[:], e256[:], True, True, [tk("ones32"), tk("e256")], [Tpb[5]])
                    gs = slice(g * 256, (g + 1) * 256)
                    tt(c.dve, ST[:, gs], ST[:, gs], pb[5][:, 0:256], ALU.mult, [tk("ST"), Tpb[5]], [tk("ST")])
                    tt(c.dve, ST[:, gs], ST[:, gs], pb[4][:, 128:384], ALU.add, [tk("ST"), Tpb[4]], [tk("ST")])
                    cp(c.act, Sb[:, gs], ST[:, gs], [tk("ST")], [tk("Sb")])
                else:
                    cp(c.dve, sap(e256, 0, [(64, 4), (1, 64)], parts=32), sap(c32, OH32 + 4 * g, [(1, 4), (0, 64)], parts=32), [Tc], [tk("e256")])
                    for t2 in range(2):
                        mm(pb[5][:, t2 * 16:(t2 + 1) * 16], e256[:, t2 * 128:(t2 + 1) * 128], ecdS[:, :], True, True, [tk("e256"), tk("ecd")], [Tpb[5]], sig=(t2 == 1))
                    cp(c.act, sap(cdx, 0, [(1, 32)]), pb[5][:, 0:32], [Tpb[5]], [tk("cdx")])
                    for i in range(16):
                        tsc(c.dve, Bm[:, i, :], Btm[:], stc[:, ROWM + i:ROWM + i + 1], ALU.mult, [tk("Btm"), Tc], [tk("QT")])
                    for i in range(16):
                        for t2 in range(2):
                            q = (i * 2 + t2) % 4
                            mm(pb[4][:, q * 128:(q + 1) * 128], xdtd[:, t2 * 128:(t2 + 1) * 128], Bm[:, i, :], True, True, [tk("xdtd"), tk("QT")], [Tpb[4]])
                            stt(h0g[:, i, t2, :], h0g[:, i, t2, :], cdx[:, t2, i:i + 1], pb[4][:, q * 128:(q + 1) * 128], ALU.mult, ALU.add,
                                [tk("xt"), tk("cdx"), Tpb[4]], [tk("xt")])
                    for t2 in range(2):
                        c.dma(c.sp, d_so, sssm[:, g * 256 + t2 * 128:g * 256 + (t2 + 1) * 128, :].rearrange("i p n -> p i n"), h0g[:, :, t2, :],
                              reads=[tk("xt")], writes=[tk("o_sssm")])
        stage(6)
        if state_only:
            return
        for ci in range(nch):
            cs = slice(ci * 128, (ci + 1) * 128)
            for ft in range(16):
                actf(sqy[:], mixT[:, 16 + ft, cs], AF.Square, [tk("mixT")], [tk("sqy")])
                mm(pb[2][:, cs], onesb[:], sqy[:], ft == 0, ft == 15, [tk("onesb"), tk("sqy")], [Tpb[2]])
        rstd_from(rsy[:, 0:TS], pb[2][:, 0:TS], 1.0 / 2048, [Tpb[2]], [tk("rsb")])
        for ft in range(16):
            stt(mixT[:, 16 + ft, 0:TS], mixT[:, 16 + ft, 0:TS], cf[:, SSDNW + ft:SSDNW + ft + 1], rsy[:, 0:TS], ALU.mult, ALU.mult,
                [tk("mixT"), Tc, tk("rsb")], [tk("mixT")])

    xpc = tS
    d_ys = [c.dsem(f"ys{i}") for i in range(2)]
    d_xp = [c.dsem(f"xp{i}") for i in range(2)]
    d_y = c.dsem("y")
    cnt = {"xp": 0}

    def outproj(chunks, Ty):
        nch = len(chunks)
        for cb in range(8):
            for kb in range(4):
                s, trk = wload([(0, 8, 512, w_out[kb * 1024:(kb + 1) * 1024, cb * 512:(cb + 1) * 512])])
                for ci in range(nch):
                    for k8 in range(8):
                        k = kb * 8 + k8
                        mm(pb[ci][:, :], sap(mixT, k * TSM + ci * 128, [(1, 128)]), wview(s, 0, k8, 512, 0, 512), k == 0, k == 31,
                           [trk, tk("mixT")], [Tpb[ci]], sig=(k8 == 7))
            for ci, ch in enumerate(chunks):
                i = cnt["xp"] % 2
                cnt["xp"] += 1
                c.dma(c.sp, d_xp[i], xpc[i][:], ch["src"][:, cb * 512:(cb + 1) * 512], writes=[tk(f"tS{i}")])
                tt(c.dve, xpc[i][:], xpc[i][:], pb[ci][:, :], ALU.add, [tk(f"tS{i}"), Tpb[ci]], [tk(f"tS{i}")])
                c.dma(c.sp, d_ys[i], ch["dst"][:, cb * 512:(cb + 1) * 512], xpc[i][:], reads=[tk(f"tS{i}")], writes=[Ty[ci][cb]])

    gc = cacc; sg = sqb
    abuf = [sbt(f"abuf{i}", [128, TSM], BF16) for i in range(2)]
    d_a = [c.dsem(f"a{i}") for i in range(2)]
    d_ab = [c.dsem(f"ab{i}") for i in range(2)]
    Tats = [Trk() for _ in range(NFF)]

    def ffn(chunks, Ty, TS, gate_only=False):
        nch = len(chunks)
        for ci, ch in enumerate(chunks):
            norm_T(ch["dst"], Ty[ci], ci, FFNNW)
        for j in range(NFF):
            s, trk = wload([(0, 32, 128, w_gate[:, j * 128:(j + 1) * 128])])
            bg, bu = (0, 1) if j % 2 == 0 else (2, 3)
            for k in range(32):
                mm(pb[bg][:, 0:TS], wview(s, 0, k, 128, 0, 128), sap(actT, k * TSM, [(1, TS)]), k == 0, k == 31, [trk, tk("actT")], [Tpb[bg]], sig=(k == 31))
            if gate_only:
                cp(c.act, fcar[:, j, :], pb[bg][:, TS - 2:TS], [Tpb[bg]], [tk("fcar")])
                continue
            s2, trk2 = wload([(0, 32, 128, w_up[:, j * 128:(j + 1) * 128])])
            for k in range(32):
                mm(pb[bu][:, 0:TS], wview(s2, 0, k, 128, 0, 128), sap(actT, k * TSM, [(1, TS)]), k == 0, k == 31, [trk2, tk("actT")], [Tpb[bu]], sig=(k == 31))
            G = pb[bg]
            wof = FCW + j * 3
            bcol = cf[:, FCB + j:FCB + j + 1]
            if chunks[0]["kind"] == "prompt":
                c.op(c.act, lambda h, G=G, wof=wof, bcol=bcol: h.activation(out=gc[:, 0:TS], in_=G[:, 0:TS], func=AF.Identity,
                                                                             scale=cf[:, wof + 2:wof + 3], bias=bcol),
                     reads=[Tpb[bg], Tc], writes=[tk("cacc")])
                stt(gc[:, 1:TS], G[:, 0:TS - 1], cf[:, wof + 1:wof + 2], gc[:, 1:TS], ALU.mult, ALU.add, [Tpb[bg], Tc, tk("cacc")], [tk("cacc")])
                stt(gc[:, 2:TS], G[:, 0:TS - 2], cf[:, wof:wof + 1], gc[:, 2:TS], ALU.mult, ALU.add, [Tpb[bg], Tc, tk("cacc")], [tk("cacc")])
                stt(gc[:, 0:1], fcar[:, j, 1:2], cf[:, wof + 1:wof + 2], gc[:, 0:1], ALU.mult, ALU.add, [tk("fcar"), Tc, tk("cacc")], [tk("cacc")])
                stt(gc[:, 0:2], fcar[:, j, 0:2], cf[:, wof:wof + 1], gc[:, 0:2], ALU.mult, ALU.add, [tk("fcar"), Tc, tk("cacc")], [tk("cacc")])
                cp(c.act, fcar[:, j, :], G[:, TS - 2:TS], [Tpb[bg]], [tk("fcar")])
            else:
                q = j % 4
                if q == 0:
                    nq = min(4, NFF - j)
                    c.dma(c.sp, d_stg, stg[0:32, 0:nq * 128], stffn[:, j * 128:(j + nq) * 128], writes=[tk("vtmf")])
                    for qq in range(nq):
                        tr(pb[4][:, qq * 32:(qq + 1) * 32], stg[0:32, qq * 128:(qq + 1) * 128], stc[0:32, IDENT:IDENT + 32], [tk("vtmf"), Tc], [Tpb[4]], sig=(qq == nq - 1))
                    cp(c.act, sap(fpv, 0, [(1, nq * 32)]), pb[4][:, 0:nq * 32], [Tpb[4]], [tk("fpv")])
                g8 = sap(gc, 0, [(8, 16), (1, 8)])
                c.op(c.act, lambda h, G=G, wof=wof, bcol=bcol, g8=g8: h.activation(out=g8, in_=sap(G, 0, [(8, 16), (1, 8)]), func=AF.Identity,
                                                                                   scale=cf[:, wof + 2:wof + 3], bias=bcol),
                     reads=[Tpb[bg], Tc], writes=[tk("cacc")])
                stt(sap(gc, 1, [(8, 16), (1, 7)]), sap(G, 0, [(8, 16), (1, 7)]), cf[:, wof + 1:wof + 2], sap(gc, 1, [(8, 16), (1, 7)]), ALU.mult, ALU.add,
                    [Tpb[bg], Tc, tk("cacc")], [tk("cacc")])
                stt(sap(gc, 2, [(8, 16), (1, 6)]), sap(G, 0, [(8, 16), (1, 6)]), cf[:, wof:wof + 1], sap(gc, 2, [(8, 16), (1, 6)]), ALU.mult, ALU.add,
                    [Tpb[bg], Tc, tk("cacc")], [tk("cacc")])
                stt(sap(gc, 0, [(8, 16), (1, 1)]), sap(fpv, q * 32 + 1, [(2, 16), (1, 1)]), cf[:, wof + 1:wof + 2], sap(gc, 0, [(8, 16), (1, 1)]), ALU.mult, ALU.add,
                    [tk("fpv"), Tc, tk("cacc")], [tk("cacc")])
                stt(sap(gc, 0, [(8, 16), (1, 2)]), sap(fpv, q * 32, [(2, 16), (1, 2)]), cf[:, wof:wof + 1], sap(gc, 0, [(8, 16), (1, 2)]), ALU.mult, ALU.add,
                    [tk("fpv"), Tc, tk("cacc")], [tk("cacc")])
                cp(c.act, sap(gtl, q * 32, [(2, 16), (1, 2)]), sap(G, 6, [(8, 16), (1, 2)]), [Tpb[bg]], [tk("gtl")])
                if q == 3 or j == NFF - 1:
                    nq = q + 1
                    j0 = j - q
                    for qq in range(nq):
                        tr(pb[5][0:32, qq * 128:(qq + 1) * 128], gtl[:, qq, :], ident, [tk("gtl"), Tc], [Tpb[5]], sig=(qq == nq - 1))
                    cp(c.act, ktm[0:32, 0:nq * 128], pb[5][0:32, 0:nq * 128], [Tpb[5]], [tk("ktm")])
                    c.dma(c.sp, d_sk, sffn[:, j0 * 128:(j0 + nq) * 128], ktm[0:32, 0:nq * 128], reads=[tk("ktm")], writes=[tk("o_sffn")])
            actf(sg[:, 0:TS], gc[:, 0:TS], AF.Silu, [tk("cacc")], [tk("sqb")])
            i = j % 2
            tt(c.dve, abuf[i][:, 0:TS], sg[:, 0:TS], pb[bu][:, 0:TS], ALU.mult, [tk("sqb"), Tpb[bu]], [tk(f"abuf{i}")])
            c.dma(c.sp, d_a[i], ats[j, :, 0:TS], abuf[i][:, 0:TS], reads=[tk(f"abuf{i}")], writes=[Tats[j]])
        if gate_only:
            return
        stage(9)
        nblk = 0
        for cb in range(8):
            for kb in range(11):
                nk = 8 if kb < 10 else 6
                s, trk = wload([(0, nk, 512, w_down[kb * 1024:kb * 1024 + nk * 128, cb * 512:(cb + 1) * 512])])
                i = nblk % 2
                nblk += 1
                abase = xn[:, 0:8 * TSM] if i == 0 else xt[:].bitcast(BF16)[:, 0:8 * TSM]
                abv = abase.rearrange("p (k t) -> p k t", t=TSM)
                abk = "xn" if i == 0 else "xt"
                c.dma(c.pool, d_ab[i], abv[:, 0:nk, 0:TS], ats[kb * 8:kb * 8 + nk, :, 0:TS].rearrange("k p t -> p k t"),
                      reads=Tats[kb * 8:kb * 8 + nk], writes=[tk(abk)])
                for ci in range(nch):
                    for k8 in range(nk):
                        k = kb * 8 + k8
                        mm(pb[ci][:, :], abv[:, k8, ci * 128:(ci + 1) * 128], wview(s, 0, k8, 512, 0, 512), k == 0, k == NFF - 1,
                           [trk, tk(abk)], [Tpb[ci]], sig=(k8 == nk - 1))
            for ci, ch in enumerate(chunks):
                i2 = cnt["xp"] % 2
                cnt["xp"] += 1
                c.dma(c.sp, d_xp[i2], xpc[i2][:], ch["dst"][:, cb * 512:(cb + 1) * 512], reads=[Ty[ci][cb]], writes=[tk(f"tS{i2}")])
                tt(c.dve, xpc[i2][:], xpc[i2][:], pb[ci][:, :], ALU.add, [tk(f"tS{i2}"), Tpb[ci]], [tk(f"tS{i2}")])
                c.dma(c.sp, d_ys[i2], ch["dst"][:, cb * 512:(cb + 1) * 512], xpc[i2][:], reads=[tk(f"tS{i2}")], writes=[Ty[ci][cb]])

    pre = [{"src": xp[i * 128:(i + 1) * 128, :], "dst": xdum, "kind": "prompt"} for i in range(NPRE)]
    own = [{"src": xp[(NPRE + i) * 128:(NPRE + i + 1) * 128, :], "dst": yp[i * 128:(i + 1) * 128, :], "kind": "prompt"} for i in range(NOWN)]
    own[-1]["kv_out"] = (pk, pv)
    try:
        stage(0)
        first = True
        for s0 in range(0, NPRE - 1, TSC):
            mixer(pre[s0:min(s0 + TSC, NPRE - 1)], first, state_only=True)
            first = False
        if NPRE > 0:
            ch7 = [pre[NPRE - 1]]
            Ty = [[Trk() for _ in range(8)]]
            mixer(ch7, first)
            outproj(ch7, Ty)
            ffn(ch7, Ty, 128, gate_only=True)
            hcol = hmc[:, 0:1]
            tsc(c.dve, ST[:], ST[:], hcol, ALU.mult, [tk("ST"), Tc], [tk("ST")])
            tsc(c.dve, Sb[:], Sb[:], hcol, ALU.mult, [tk("Sb"), Tc], [tk("Sb")])
            tsc(c.dve, sap(ccar, 0, [(1, 96)]), sap(ccar, 0, [(1, 96)]), hcol, ALU.mult, [tk("ccar"), Tc], [tk("ccar")])
            tsc(c.dve, sap(fcar, 0, [(1, 2 * NFF)]), sap(fcar, 0, [(1, 2 * NFF)]), hcol, ALU.mult, [tk("fcar"), Tc], [tk("fcar")])
        for s0 in range(0, NOWN, TSC):
            chunks = own[s0:s0 + TSC]
            Ty = [[Trk() for _ in range(8)] for _ in chunks]
            cur["st"] = s0 // TSC
            mixer(chunks, NPRE == 0 and s0 == 0, pmask=(NPRE > 0 and s0 == 0))
            stage(7)
            outproj(chunks, Ty)
            stage(8)
            ffn(chunks, Ty, len(chunks) * 128)
        if SAMPLE:
            cur["st"] = 50
            chunks = [{"src": xs, "dst": ys, "kind": "sample", "kv_out": ("s",)}]
            Ty = [[Trk() for _ in range(8)]]
            mixer(chunks, False)
            stage(7)
            outproj(chunks, Ty)
            stage(8)
            ffn(chunks, Ty, 128)

        osb = ktm
        d_osb = c.dsem("osb")
        for g4 in range(4):
            for q in range(4):
                t_ = g4 * 4 + q
                tr(pb[0][:, q * 128:(q + 1) * 128], ST[:, t_ * 128:(t_ + 1) * 128], ident, [tk("ST"), Tc], [Tpb[0]], sig=(q == 3))
            cp(c.act, osb[:], pb[0][:, :], [Tpb[0]], [tk("ktm")])
            c.dma(c.sp, d_osb, pssm[g4 * 512:(g4 + 1) * 512, :].rearrange("(q p) n -> p q n", p=128), sap(osb, 0, [(128, 4), (1, 128)]),
                  reads=[tk("ktm")], writes=[tk("o_ssm")])
        for g4 in range(8):
            for q in range(4):
                t_ = g4 * 4 + q
                tr(pb[1][0:3, q * 128:(q + 1) * 128], ccar[:, t_, :], ident, [tk("ccar"), Tc], [Tpb[1]], sig=(q == 3))
            cp(c.act, osb[0:3, :], pb[1][0:3, :], [Tpb[1]], [tk("ktm")])
            c.dma(c.sp, d_osb, pconv[:, g4 * 512:(g4 + 1) * 512], osb[0:3, :], reads=[tk("ktm")], writes=[tk("o_conv")])
        for g4 in range(22):
            nq = 4 if g4 < 21 else 2
            for q in range(nq):
                t_ = g4 * 4 + q
                tr(pb[2][0:2, q * 128:(q + 1) * 128], fcar[:, t_, :], ident, [tk("fcar"), Tc], [Tpb[2]], sig=(q == nq - 1))
            cp(c.act, osb[0:2, 0:nq * 128], pb[2][0:2, 0:nq * 128], [Tpb[2]], [tk("ktm")])
            c.dma(c.sp, d_osb, pffn[:, g4 * 512:g4 * 512 + nq * 128], osb[0:2, 0:nq * 128], reads=[tk("ktm")], writes=[tk("o_ffn")])


    except _Stop:
        pass
    c.finish(c.sp)
    c.emit()
    return nc, c


def _t5_bucket_np(dist):
    n = np.maximum(dist, 0)
    nf = np.maximum(n, 1).astype(np.float32)
    large = 16 + (np.log(nf / 16) / math.log(128 / 16) * 16).astype(np.int32)
    large = np.minimum(large, 31)
    return np.where(n < 16, n, large)


def _static_consts():
    stc = np.zeros((128, NSTC), np.float32)
    stc[:, IDENT:IDENT + 128] = np.eye(128)
    s = np.arange(128)[:, None]
    l = np.arange(128)[None, :]
    stc[:, MASKP:MASKP + 128] = np.where(l >= s, 0.0, NEG)
    stc[:, MASKS:MASKS + 128] = np.where((l >= s) & (l // 8 == s // 8), 0.0, NEG)
    stc[:, ROWM:ROWM + 16] = (s // 8 == np.arange(16)[None, :]).astype(np.float32)
    stc[:, ANTI:ANTI + 128] = np.eye(128)[::-1]
    ohd = np.zeros((33, 384), np.float32)
    for i in range(384):
        d = i - 127
        if 0 <= d < 128:
            ohd[int(_t5_bucket_np(np.array(d))), i] = 1.0
        else:
            ohd[32, i] = NEG
    return stc, ohd


_CACHE = {}


def _prep(inputs, NPRE, NOWN):
    f = lambda a: np.ascontiguousarray(np.asarray(a, dtype=np.float32))
    fm = lambda v: np.ascontiguousarray(v.reshape(-1, 128).T)
    cf = np.zeros((128, NCF), np.float32)
    cf[:, MIXNW:MIXNW + 32] = fm(f(inputs["mix_norm_w"])[0])
    cf[:, FFNNW:FFNNW + 32] = fm(f(inputs["ffn_norm_w"])[0])
    cf[:, QNW] = f(inputs["q_norm_w"])[0]
    cf[:, KNW] = f(inputs["k_norm_w"])[0]
    cw = f(inputs["ssd_conv_w"])[0]
    cf[:, CONVW:CONVW + 128] = cw.reshape(4, 32, 128).transpose(2, 1, 0).reshape(128, 128)
    cf[:, CONVB:CONVB + 32] = fm(f(inputs["ssd_conv_b"])[0])
    cf[:, DEXP:DEXP + 16] = fm(np.repeat(f(inputs["ssd_D"])[0], 64))
    cf[:, SSDNW:SSDNW + 16] = fm(f(inputs["ssd_norm_w"])[0])
    fw_ = f(inputs["ffn_conv_w"])[0]
    cf[:, FCW:FCW + 258] = fw_.reshape(3, NFF, 128).transpose(2, 1, 0).reshape(128, 258)
    cf[:, FCB:FCB + NFF] = fm(f(inputs["ffn_conv_b"])[0])
    c32 = np.zeros((32, NC32), np.float32)
    c32[:, DTB] = f(inputs["ssd_dt_bias"])[0]
    c32[:, ALOG] = f(inputs["ssd_A_log"])[0]
    c32[:, OH32:OH32 + 32] = np.eye(32)
    m = np.ones(128, np.float32)
    m[::8] = 0
    c32[:, MSEG:MSEG + 128] = m[None, :]
    stc, ohd = _static_consts()
    relb = np.concatenate([f(inputs["rel_bias"]), np.ones((1, 16), np.float32)], 0)
    shared = {"w_in": f(inputs["w_in"])[0], "w_out": f(inputs["w_out"])[0], "w_gate": f(inputs["w_gate"])[0],
              "w_up": f(inputs["w_up"])[0], "w_down": f(inputs["w_down"])[0], "cf": cf, "c32": c32, "stc": stc,
              "relb33": relb, "ohd33": ohd, "sinks": f(inputs["attn_sinks"])}
    xp = f(inputs["x_prompt"]); xs = f(inputs["x_sample"])
    sk_ = f(inputs["state_attn_k"])[0]; sv_ = f(inputs["state_attn_v"])[0]; ssm = f(inputs["state_ssm"])[0]
    scv = f(inputs["state_ssd_conv"])[0]; sff = f(inputs["state_ffn_conv"])[0]
    maps = []
    for cidx in range(8):
        sl = slice(16 * cidx, 16 * cidx + 16)
        m_ = dict(shared)
        b, half = cidx // 2, cidx % 2
        if half == 0:
            m_["xp"] = np.ascontiguousarray(np.concatenate([np.zeros((NPRE * 128, D), np.float32), xp[b, :NOWN * 128]], 0))
        else:
            m_["xp"] = np.ascontiguousarray(xp[b, :(NPRE + NOWN) * 128])
        hm = np.zeros((128, 2), np.float32)
        hm[:, 0] = float(half)
        hm[:, 1] = 0.0 if half == 1 else NEG
        m_["hm"] = hm
        m_["xs"] = np.ascontiguousarray(xs[sl].reshape(128, D))
        m_["stk"] = np.ascontiguousarray(sk_[sl].reshape(16, 128, 512))
        m_["stv"] = np.ascontiguousarray(sv_[sl].reshape(16, 128, 512))
        m_["stssm"] = np.ascontiguousarray(ssm[sl].reshape(16, 2048, 128))
        m_["stconv"] = np.ascontiguousarray(scv[sl].reshape(48, D))
        m_["stffn"] = np.ascontiguousarray(sff[sl].reshape(32, DFF))
        maps.append(m_)
    return maps


def run(inputs, NPRE=8, NOWN=8, TSC=4, STOP=99, ncores=8):
    key = (NPRE, NOWN, TSC, STOP)
    if key not in _CACHE:
        _CACHE[key] = build(NPRE, NOWN, TSC, STOP=STOP)[0]
    nc = _CACHE[key]
    maps = _prep(inputs, NPRE, NOWN)
    if STOP < 7 or 50 <= STOP < 60:
        maps = [{k: v for k, v in m.items() if k not in ('w_out', 'w_gate', 'w_up', 'w_down')} for m in maps]
    res = run_bass_kernel_spmd(nc, maps[:ncores], core_ids=list(range(ncores)))
    return res


def kernel(**inputs):
    res = run(inputs)
    R = res.results
    yp = np.stack([np.concatenate([R[2 * b]["yp"].reshape(1024, D), R[2 * b + 1]["yp"].reshape(1024, D)], 0) for b in range(4)])
    ys = np.concatenate([R[cidx]["ys"].reshape(16, 8, D) for cidx in range(8)], 0)
    p_k = np.stack([R[2 * b + 1]["pk"].reshape(128, 4, 128) for b in range(4)])[None]
    p_v = np.stack([R[2 * b + 1]["pv"].reshape(128, 4, 128) for b in range(4)])[None]
    p_ssm = np.stack([R[2 * b + 1]["pssm"].reshape(32, 64, 128) for b in range(4)])[None]
    p_conv = np.stack([R[2 * b + 1]["pconv"].reshape(3, D) for b in range(4)])[None]
    p_ffn = np.stack([R[2 * b + 1]["pffn"].reshape(2, DFF) for b in range(4)])[None]
    s_k = np.concatenate([R[cidx]["sk"].reshape(16, 128, 4, 128) for cidx in range(8)], 0)[None]
    s_v = np.concatenate([R[cidx]["sv"].reshape(16, 128, 4, 128) for cidx in range(8)], 0)[None]
    s_ssm = np.concatenate([R[cidx]["sssm"].reshape(16, 32, 64, 128) for cidx in range(8)], 0)[None]
    s_conv = np.concatenate([R[cidx]["sconv"].reshape(16, 3, D) for cidx in range(8)], 0)[None]
    s_ffn = np.concatenate([R[cidx]["sffn"].reshape(16, 2, DFF) for cidx in range(8)], 0)[None]
    outs = (yp, ys, p_k, p_v, p_ssm, p_conv, p_ffn, s_k, s_v, s_ssm, s_conv, s_ffn)
    return tuple(np.ascontiguousarray(o, dtype=np.float32) for o in outs)
```

```python
import math
import numpy as np
import concourse.bass as bass
import concourse.mybir as mybir
from concourse.alu_op_type import AluOpType as ALU
from concourse.bass_utils import run_bass_kernel_spmd

F32 = mybir.dt.float32
BF16 = mybir.dt.bfloat16
AF = mybir.ActivationFunctionType

D = 4096
DFF = 11008
NFF = 86
IN_DIM = 9248
OFF_Q, OFF_K, OFF_V, OFF_Z, OFF_X, OFF_B, OFF_C, OFF_DT = 0, 2048, 2560, 3072, 5120, 7168, 8192, 9216
EPS = 1e-6
NEG = -30000.0
MIXNW, FFNNW, QNW, KNW, CONVW, CONVB, DEXP, SSDNW, FCW, FCB, NCF = 0, 32, 64, 65, 66, 194, 226, 242, 258, 516, 602
DTB, ALOG, OH32, MSEG, NC32 = 0, 1, 2, 34, 162
IDENT, MASKP, MASKS, ROWM, ANTI, NSTC = 0, 128, 256, 384, 400, 528


class Trk:
    __slots__ = ("w", "r", "excl")

    def __init__(self, excl=False):
        self.w = None
        self.r = {}
        self.excl = excl


class Eng:
    def __init__(self, ctx, name, attr):
        self.name = name
        self.attr = attr
        self.sem = ctx.nc.alloc_semaphore(name="s_" + name)
        self.cnt = 0
        self.seen = {}
        self.prog = []
        self.pend_r = []
        self.pend_w = []


class DSem:
    def __init__(self, ctx, name):
        self.sem = ctx.nc.alloc_semaphore(name="d_" + name)
        self.val = 0
        self.key = "d_" + name


class Ctx:
    def __init__(self, nc):
        self.nc = nc
        self.pe = Eng(self, "pe", "tensor")
        self.act = Eng(self, "act", "scalar")
        self.dve = Eng(self, "dve", "vector")
        self.pool = Eng(self, "pool", "gpsimd")
        self.sp = Eng(self, "sp", "sync")
        self.engs = [self.pe, self.act, self.dve, self.pool, self.sp]
        self.dsems = []
        self.n_ins = 0

    def dsem(self, name):
        d = DSem(self, name)
        self.dsems.append(d)
        return d

    def _deps(self, E, reads, writes):
        deps = []
        for t in reads:
            if t.w is not None:
                deps.append(t.w)
        for t in writes:
            if t.w is not None:
                deps.append(t.w)
            deps.extend(t.r.values())
        for key, sem, val in deps:
            if E.seen.get(key, 0) < val:
                E.prog.append(("wait", sem, val))
                E.seen[key] = val

    def op(self, E, fn, reads=(), writes=(), signal=True):
        if any(t.excl for t in reads):
            writes = list(writes) + [t for t in reads if t.excl]
            reads = [t for t in reads if not t.excl]
        self._deps(E, reads, writes)
        E.pend_r.extend(reads)
        E.pend_w.extend(writes)
        self.n_ins += 1
        if signal:
            E.cnt += 1
            E.prog.append(("op", fn, E.sem, 1))
            stamp = (E.name, E.sem, E.cnt)
            for t in E.pend_w:
                t.w = stamp
                t.r = {}
            for t in E.pend_r:
                t.r[E.name] = stamp
            E.pend_r = []
            E.pend_w = []
        else:
            E.prog.append(("op", fn, None, 0))

    def dma(self, Q, ds, out, in_, reads=(), writes=()):
        self._deps(Q, reads, writes)
        ds.val += 16
        Q.prog.append(("op", lambda h, o=out, i=in_: h.dma_start(out=o, in_=i), ds.sem, 16))
        stamp = (ds.key, ds.sem, ds.val)
        for t in writes:
            t.w = stamp
            t.r = {}
        for t in reads:
            t.r[ds.key] = stamp
        self.n_ins += 1

    def finish(self, E):
        for d in self.dsems:
            if d.val > 0:
                E.prog.append(("wait", d.sem, d.val))
        for X in self.engs:
            if X is not E and X.cnt > 0:
                E.prog.append(("wait", X.sem, X.cnt))

    def emit(self):
        with self.nc.Block() as block:
            for E in self.engs:
                def body(h, E=E):
                    for item in E.prog:
                        if item[0] == "wait":
                            h.wait_ge(item[1], item[2])
                        else:
                            ins = item[1](h)
                            if item[2] is not None:
                                ins.then_inc(item[2], item[3])
                getattr(block, E.attr)(body)


def sap(t, off, dims, parts=128, p0=0):
    Fsz = int(np.prod(t.shape[1:]))
    return bass.AP(t, p0 * Fsz + off, [[Fsz, parts]] + [[int(a), int(b)] for a, b in dims])


class _Stop(Exception):
    pass


def build(NPRE=8, NOWN=8, TSC=4, SAMPLE=True, STOP=99):
    cur = {"st": 0}

    def stage(n):
        if STOP == cur["st"] * 100 + n:
            raise _Stop()
    nc = bass.Bass("TRN2", target_bir_lowering=False)
    TSM = TSC * 128
    di = lambda n, s, dt=F32: nc.dram_tensor(n, s, dt, kind="ExternalInput").ap()
    do = lambda n, s, dt=F32: nc.dram_tensor(n, s, dt, kind="ExternalOutput").ap()
    NPC = NPRE + NOWN
    xp = di("xp", [NPC * 128, D]); xs = di("xs", [128, D])
    hm_d = di("hm", [128, 2])
    stk = di("stk", [16, 128, 512]); stv = di("stv", [16, 128, 512]); stssm = di("stssm", [16, 2048, 128])
    stconv = di("stconv", [48, D]); stffn = di("stffn", [32, DFF])
    w_in = di("w_in", [D, IN_DIM])
    if STOP < 7 or 50 <= STOP < 60:
        w_out = w_gate = w_up = w_down = None
    else:
        w_out = di("w_out", [D, D]); w_gate = di("w_gate", [D, DFF])
        w_up = di("w_up", [D, DFF]); w_down = di("w_down", [DFF, D])
    cf_d = di("cf", [128, NCF]); c32_d = di("c32", [32, NC32]); stc_d = di("stc", [128, NSTC])
    relb_d = di("relb33", [33, 16]); ohd_d = di("ohd33", [33, 384]); sinks_d = di("sinks", [1, 16])
    yp = do("yp", [NOWN * 128, D]); ys = do("ys", [128, D])
    xdum = nc.dram_tensor("xdum", [128, D], F32).ap()
    pk = do("pk", [128, 512]); pv = do("pv", [128, 512]); pssm = do("pssm", [2048, 128])
    pconv = do("pconv", [3, D]); pffn = do("pffn", [2, DFF])
    sk = do("sk", [16, 128, 512]); sv = do("sv", [16, 128, 512]); sssm = do("sssm", [16, 2048, 128])
    sconv = do("sconv", [48, D]); sffn = do("sffn", [32, DFF])
    brow_h = nc.dram_tensor("brow", [16, 384], F32)
    ats_h = nc.dram_tensor("ats", [NFF, 128, TSM], BF16)
    ats = ats_h.ap()

    c = Ctx(nc)
    sbt = lambda n, s, dt=F32: nc.alloc_sbuf_tensor(n, s, dt)
    cf = sbt("cf_s", [128, NCF]); c32 = sbt("c32_s", [32, NC32]); stc = sbt("stc_s", [128, NSTC])
    idb = sbt("idb", [128, 128], BF16); onesb = sbt("onesb", [128, 128], BF16); ones32 = sbt("ones32", [32, 128])
    esink = sbt("esink", [128, 16]); negA = sbt("negA", [32, 1])
    biasC = sbt("biasC", [128, 16, 128], BF16); biasP = sbt("biasP", [128, 16, 128], BF16)
    actT = sbt("actT", [128, 32, TSM], BF16); mixT = sbt("mixT", [128, 32, TSM], BF16)
    RS = 4
    ring = sbt("ring", [128, RS, 4096], BF16)
    KT = sbt("KT", [128, 4, 128 + TSM], BF16); Vtm = sbt("Vtm", [128, TSC + 1, 512], BF16)
    ST = sbt("ST", [128, 2048]); Sb = sbt("Sb", [128, 2048], BF16)
    ccar = sbt("ccar", [128, 32, 3]); fcar = sbt("fcar", [128, NFF, 2])
    xt = sbt("xt", [128, D]); xn = sbt("xn", [128, D], BF16)
    den = sbt("den", [128, 512])
    tS = [sbt(f"tS{i}", [128, 512]) for i in range(2)]
    Eb = [sbt(f"Eb{i}", [128, 512], BF16) for i in range(2)]
    pb = [nc.alloc_psum_tensor(f"pb{i}", [128, 512], F32) for i in range(7)]
    ptb = nc.alloc_psum_tensor("ptb", [128, 1024], BF16)
    Tpb = [Trk(excl=True) for _ in range(7)]
    Tptb = Trk(excl=True)
    T = {}

    def tk(name):
        if name not in T:
            T[name] = Trk()
        return T[name]

    def tt(E, out, in0, in1, op, r, w):
        c.op(E, lambda h: h.tensor_tensor(out=out, in0=in0, in1=in1, op=op), reads=r, writes=w)

    def tsc(E, out, in0, s1, op0, r, w, s2=None, op1=None):
        if op1 is None:
            c.op(E, lambda h: h.tensor_scalar(out=out, in0=in0, scalar1=s1, scalar2=None, op0=op0), reads=r, writes=w)
        else:
            c.op(E, lambda h: h.tensor_scalar(out=out, in0=in0, scalar1=s1, scalar2=s2, op0=op0, op1=op1), reads=r, writes=w)

    def stt(out, in0, sc, in1, op0, op1, r, w):
        c.op(c.dve, lambda h: h.scalar_tensor_tensor(out=out, in0=in0, scalar=sc, in1=in1, op0=op0, op1=op1), reads=r, writes=w)

    def actf(out, in_, func, r, w, **kw):
        c.op(c.act, lambda h: h.activation(out=out, in_=in_, func=func, **kw), reads=r, writes=w)

    def mm(out, lhsT, rhs, st, sp_, r, w, sig=True):
        c.op(c.pe, lambda h: h.matmul(out=out, lhsT=lhsT, rhs=rhs, start=st, stop=sp_), reads=r, writes=w, signal=sig)

    def tr(out, in_, ident, r, w, sig=True):
        c.op(c.pe, lambda h: h.transpose(out=out, in_=in_, identity=ident), reads=r, writes=w, signal=sig)

    def cp(E, out, in_, r, w):
        if E is c.act:
            actf(out, in_, AF.Copy, r, w)
        else:
            c.op(E, lambda h: h.tensor_copy(out=out, in_=in_), reads=r, writes=w)

    def mset(E, ap, v, w):
        c.op(E, lambda h: h.memset(ap, v), writes=w)

    def rstd_from(dst, src, scale, r, w):
        tsc(c.dve, dst, src, scale, ALU.mult, r, w, s2=EPS, op1=ALU.add)
        actf(dst, dst, AF.Sqrt, w, w)
        c.op(c.dve, lambda h: h.reciprocal(out=dst, in_=dst), reads=w, writes=w)

    ring_state = {"i": 0}
    ring_trk = [Trk() for _ in range(RS)]
    ring_ds = [c.dsem(f"w{i}") for i in range(RS)]

    def wload(parts):
        s = ring_state["i"] % RS
        ring_state["i"] += 1
        for n, (off, kt, ncol, src) in enumerate(parts):
            dst = sap(ring, s * 4096 + off, [(ncol, kt), (1, ncol)])
            c.dma(c.pool, ring_ds[s], dst, src.rearrange("(k p) n -> p k n", p=128), writes=[ring_trk[s]] if True else [])
        return s, ring_trk[s]

    def wload2(src):
        if ring_state["i"] % 2:
            ring_state["i"] += 1
        s = ring_state["i"] % RS
        ring_state["i"] += 2
        c.dma(c.pool, ring_ds[s], sap(ring, s * 4096, [(256, 32), (1, 256)]), src.rearrange("(k p) n -> p k n", p=128),
              writes=[ring_trk[s], ring_trk[s + 1]])
        return s, [ring_trk[s], ring_trk[s + 1]]

    def wview(s, off, k, ncol, c0, c1):
        return sap(ring, s * 4096 + off + k * ncol + c0, [(1, c1 - c0)])

    d_c = c.dsem("const")
    Tc = tk("const")
    c.dma(c.sp, d_c, cf[:], cf_d, writes=[Tc])
    c.dma(c.sp, d_c, c32[:], c32_d, writes=[Tc])
    c.dma(c.sp, d_c, stc[:], stc_d, writes=[Tc])
    hmc = sbt("hmc", [128, 2])
    c.dma(c.sp, d_c, hmc[:], hm_d, writes=[Tc])
    c.dma(c.sp, d_c, esink[:], bass.AP(sinks_d.tensor, 0, [[0, 128], [1, 16]]), writes=[Tc])
    relb_s = sbt("relb_s", [33, 16]); ohd_s = sbt("ohd_s", [33, 384])
    c.dma(c.sp, d_c, relb_s[:], relb_d, writes=[Tc])
    c.dma(c.sp, d_c, ohd_s[:], ohd_d, writes=[Tc])
    Tc.w = (d_c.key, d_c.sem, d_c.val)
    ident = stc[:, IDENT:IDENT + 128]
    cp(c.dve, idb[:], ident, [Tc], [tk("idb")])
    mset(c.dve, onesb[:], 1.0, [tk("onesb")])
    mset(c.dve, ones32[:], 1.0, [tk("ones32")])
    mset(c.dve, KT[:], 0.0, [tk("KT")])
    mset(c.dve, Vtm[:], 0.0, [tk("Vtm")])
    mset(c.dve, ST[:], 0.0, [tk("ST")])
    mset(c.dve, Sb[:], 0.0, [tk("Sb")])
    mset(c.dve, ccar[:], 0.0, [tk("ccar")])
    mset(c.dve, fcar[:], 0.0, [tk("fcar")])
    actf(esink[:], esink[:], AF.Exp, [Tc], [tk("esink")])
    actf(negA[:], c32[:, ALOG:ALOG + 1], AF.Exp, [Tc], [tk("negA")])
    tsc(c.dve, negA[:], negA[:], -1.0, ALU.mult, [tk("negA")], [tk("negA")])
    mm(pb[0][0:16, 0:384], relb_s[:], ohd_s[:], True, True, [Tc], [Tpb[0]])
    brow_s = den
    cp(c.dve, den[0:16, 0:384], pb[0][0:16, 0:384], [Tpb[0]], [tk("den")])
    d_b = c.dsem("brow")
    c.dma(c.sp, d_b, brow_h.ap(), den[0:16, 0:384], reads=[tk("den")], writes=[tk("brow_d")])
    hank = xt
    for which, dst in ((0, biasC), (128, biasP)):
        c.dma(c.sp, d_b, sap(xt, 0, [(128, 16), (1, 128)]), bass.AP(brow_h, which, [[1, 128], [384, 16], [1, 128]]), reads=[tk("brow_d")], writes=[tk("xt")])
        for q in range(4):
            mm(pb[q][:, :], stc[:, ANTI:ANTI + 128], sap(xt, q * 512, [(1, 512)]), True, True, [Tc, tk("xt")], [Tpb[q]])
            cp(c.dve, sap(dst, q * 512, [(1, 512)]), pb[q][:, :], [Tpb[q]], [tk("bias")])

    st48 = sbt("st48", [48, 128]); tl48 = sbt("tl48", [128, 48]); so48 = sbt("so48", [48, 128])
    xrs = sbt("xrs", [128, 16, 11]); cdx = sbt("cdx", [128, 2, 16]); ecdS = sbt("ecdS", [32, 16])
    fpv = sbt("fpv", [128, 4, 32]); gtl = sbt("gtl", [128, 4, 32])
    d_s48 = c.dsem("s48")
    d_sk = c.dsem("sk"); d_sv = c.dsem("sv"); d_cp = c.dsem("cpy"); d_kw = c.dsem("kw"); d_h0 = c.dsem("h0")
    d_so = c.dsem("so"); d_stg = c.dsem("stg")
    xtb = xt[:].bitcast(BF16)
    Kwj = xtb[:, 0:2048].rearrange("p (i d) -> p i d", d=128)
    Vwj = xtb[:, 2048:4096].rearrange("p (i d) -> p i d", d=128)
    KwT = xtb[:, 4096:6144].rearrange("p (i d) -> p i d", d=128)
    h0g = xt[:].rearrange("p (i t n) -> p i t n", t=2, n=128)
    SiT = xn[:].rearrange("p (i f) -> p i f", f=256)
    d_x = c.dsem("x")
    ss = sbt("ss", [128, 1]); rs = sbt("rs", [128, 1])

    def norm_T(src_rows, src_trk, cidx, nw_off):
        c.dma(c.sp, d_x, xt[:], src_rows, reads=src_trk, writes=[tk("xt")])
        c.op(c.act, lambda h: h.activation(out=xn[:], in_=xt[:], func=AF.Square, accum_out=ss[:]),
             reads=[tk("xt")], writes=[tk("xn"), tk("ss")])
        rstd_from(rs[:], ss[:], 1.0 / D, [tk("ss")], [tk("rs")])
        tsc(c.dve, xn[:], xt[:], rs[:, 0:1], ALU.mult, [tk("xt"), tk("rs")], [tk("xn")])
        for g in range(8):
            for j in range(4):
                k = g * 4 + j
                tr(ptb[:, j * 128:(j + 1) * 128], xn[:, k * 128:(k + 1) * 128], idb[:], [tk("xn"), tk("idb")], [Tptb], sig=(j == 3))
            tt(c.dve, sap(actT, (g * 4) * TSM + cidx * 128, [(TSM, 4), (1, 128)]), sap(ptb, 0, [(128, 4), (1, 128)]),
               sap(cf, nw_off + g * 4, [(1, 4), (0, 128)]), ALU.mult, [Tptb, Tc], [tk("actT")])

    def proj_tile(col, TS, bank, src_w, ncols=128):
        s, trk = wload([(0, 32, ncols, src_w[:, col:col + ncols])])
        for k in range(32):
            mm(pb[bank][0:ncols, 0:TS], wview(s, 0, k, ncols, 0, ncols), sap(actT, k * TSM, [(1, TS)]), k == 0, k == 31,
               [trk, tk("actT")], [Tpb[bank]], sig=(k == 31))

    sqb = sbt("sqb", [128, TSM], BF16); rsb = sbt("rsb", [128, TSM]); hn = sbt("hn", [128, TSM])

    def headnorm(bank, TS, wcol):
        actf(sqb[:, 0:TS], pb[bank][:, 0:TS], AF.Square, [Tpb[bank]], [tk("sqb")])
        mm(pb[3][:, 0:TS], onesb[:], sqb[:, 0:TS], True, True, [tk("onesb"), tk("sqb")], [Tpb[3]])
        rstd_from(rsb[:, 0:TS], pb[3][:, 0:TS], 1.0 / 128, [Tpb[3]], [tk("rsb")])
        stt(hn[:, 0:TS], pb[bank][:, 0:TS], cf[:, wcol:wcol + 1], rsb[:, 0:TS], ALU.mult, ALU.mult, [Tpb[bank], Tc, tk("rsb")], [tk("hn")])

    QT = sbt("QT", [128, 4, TSM], BF16)
    ktm = sbt("ktm", [128, 512]); vtmf = sbt("vtmf", [128, 512])
    if 4 * TSM >= 2048:
        Bm = QT[:].rearrange("p a b -> p (a b)")[:, 0:2048].rearrange("p (i n) -> p i n", n=128)
    else:
        Bm = sbt("Bm", [128, 16, 128], BF16)
    stg = vtmf
    d_o = c.dsem("outs")

    def attn_sample(j):
        c.dma(c.pool, d_kw, Kwj, stk[:, :, j * 128:(j + 1) * 128].rearrange("i s d -> s i d"), writes=[tk("xt")])
        c.dma(c.pool, d_kw, Vwj, stv[:, :, j * 128:(j + 1) * 128].rearrange("i s d -> s i d"), writes=[tk("xt")])
        for i4 in range(4):
            for q in range(4):
                i = i4 * 4 + q
                tr(ptb[:, q * 128:(q + 1) * 128], Kwj[:, i, :], idb[:], [tk("xt"), tk("idb")], [Tptb], sig=(q == 3))
            cp(c.act, KwT[:, i4 * 4:i4 * 4 + 4, :], ptb[:, 0:512].rearrange("p (i d) -> p i d", d=128), [Tptb], [tk("xt")])
        mm(pb[4][:, :], KT[:, j, 128:256], sap(QT, 0, [(TSM, 4), (1, 128)]), True, True, [tk("KT"), tk("QT")], [Tpb[4]])
        tt(c.dve, tS[0][:], pb[4][:, :], sap(Sb, 4 * j * 128, [(1, 512)]), ALU.add, [Tpb[4], tk("Sb")], [tk("tS0")])
        actf(Eb[0][:], tS[0][:], AF.Exp, [tk("tS0")], [tk("Eb0")])
        for i in range(16):
            mm(pb[5][:, i * 32:(i + 1) * 32], KwT[:, i, :], sap(QT, 8 * i, [(TSM, 4), (1, 8)]), True, True, [tk("xt"), tk("QT")], [Tpb[5]], sig=(i == 15))
        v3 = [(32, 16), (8, 4), (1, 8)]
        tt(c.dve, sap(tS[1], 0, v3), sap(pb[5], 0, v3), sap(biasP, 4 * j * 128, [(0, 16), (128, 4), (1, 8)]), ALU.add, [Tpb[5], tk("bias")], [tk("tS1")])
        actf(Eb[1][:], tS[1][:], AF.Exp, [tk("tS1")], [tk("Eb1")])
        for lhs_w, lhs_s, bank in ((None, None, 6), (onesb, onesb, 3)):
            for i in range(16):
                lw = Vwj[:, i, :] if lhs_w is None else onesb[:]
                ls = Vtm[:, 1, j * 128:(j + 1) * 128] if lhs_s is None else onesb[:]
                mm(pb[bank][:, i * 32:(i + 1) * 32], lw, Eb[1][:, i * 32:(i + 1) * 32], True, False, [tk("xt"), tk("Eb1"), tk("onesb")], [Tpb[bank]], sig=False)
                mm(pb[bank][:, i * 32:(i + 1) * 32], ls, sap(Eb[0], 8 * i, [(128, 4), (1, 8)]), False, True, [tk("Vtm"), tk("Eb0"), tk("onesb")], [Tpb[bank]], sig=(i == 15))
        tt(c.dve, sap(den, 0, v3), sap(pb[3], 0, v3), sap(esink, 4 * j, [(0, 16), (1, 4), (0, 8)]), ALU.add, [Tpb[3], tk("esink")], [tk("den")])
        c.op(c.dve, lambda h: h.reciprocal(out=den[:], in_=den[:]), reads=[tk("den")], writes=[tk("den")])
        tt(c.dve, sap(mixT, 4 * j * TSM, [(8, 16), (TSM, 4), (1, 8)]), sap(pb[6], 0, v3), sap(den, 0, v3), ALU.mult, [Tpb[6], tk("den")], [tk("mixT")])

    def mixer(chunks, first_seq_chunk, pmask=False, state_only=False):
        nch = len(chunks)
        TS = nch * 128
        sample = chunks[0]["kind"] == "sample"
        if sample:
            c.dma(c.sp, d_cp, sk[:, 0:120, :], stk[:, 8:128, :], writes=[tk("o_k")])
            c.dma(c.sp, d_cp, sv[:, 0:120, :], stv[:, 8:128, :], writes=[tk("o_v")])
        for ci, ch in enumerate(chunks):
            norm_T(ch["src"], [], ci, MIXNW)
        stage(1)
        for j in range(4):
            bank = j % 2
            proj_tile(OFF_K + j * 128, TS, bank, w_in)
            headnorm(bank, TS, KNW)
            cp(c.act, KT[:, j, 128:128 + TS], hn[:, 0:TS], [tk("hn")], [tk("KT")])
            for ci, ch in enumerate(chunks):
                if ch.get("kv_out") is not None:
                    tr(pb[4][:, j * 128:(j + 1) * 128], hn[:, ci * 128:(ci + 1) * 128], ident, [tk("hn"), Tc], [Tpb[4]])
                    cp(c.dve, ktm[:, j * 128:(j + 1) * 128], pb[4][:, j * 128:(j + 1) * 128], [Tpb[4]], [tk("ktm")])
        for ci, ch in enumerate(chunks):
            if ch.get("kv_out") is not None:
                if ch["kind"] == "sample":
                    for i in range(16):
                        c.dma(c.sp, d_sk, sk[i, 120:128, :], ktm[8 * i:8 * i + 8, :], reads=[tk("ktm")], writes=[tk("o_k2")])
                else:
                    c.dma(c.sp, d_sk, ch["kv_out"][0], ktm[:], reads=[tk("ktm")], writes=[tk("o_k")])
        stage(2)
        for j in range(4):
            sv_, tv_ = wload([(0, 32, 128, w_in[:, OFF_V + j * 128:OFF_V + (j + 1) * 128])])
            for ci, ch in enumerate(chunks):
                for k in range(32):
                    mm(pb[ci][:, j * 128:(j + 1) * 128], sap(actT, k * TSM + ci * 128, [(1, 128)]), wview(sv_, 0, k, 128, 0, 128),
                       k == 0, k == 31, [tv_, tk("actT")], [Tpb[ci]], sig=(k == 31))
        stage(21)
        for ci, ch in enumerate(chunks):
            cp(c.act, Vtm[:, 1 + ci, :], pb[ci][:, :], [Tpb[ci]], [tk("Vtm")])
            stage(22 + ci)
            if ch.get("kv_out") is not None:
                cp(c.act, vtmf[:], pb[ci][:, :], [Tpb[ci]], [tk("vtmf")])
                if ch["kind"] == "sample":
                    for i in range(16):
                        c.dma(c.sp, d_sv, sv[i, 120:128, :], vtmf[8 * i:8 * i + 8, :], reads=[tk("vtmf")], writes=[tk("o_v2")])
                else:
                    c.dma(c.sp, d_sv, ch["kv_out"][1], vtmf[:], reads=[tk("vtmf")], writes=[tk("o_v")])
        stage(3)
        for j in (() if state_only else range(4)):
            for hh in range(4):
                bank = hh % 2
                proj_tile(OFF_Q + (4 * j + hh) * 128, TS, bank, w_in)
                headnorm(bank, TS, QNW)
                actf(QT[:, hh, 0:TS], hn[:, 0:TS], AF.Copy, [tk("hn")], [tk("QT")], scale=128 ** -0.5)
            if sample:
                attn_sample(j)
            for ci, ch in enumerate(chunks):
                if ch["kind"] == "prompt":
                    blocks = [(1, biasC)]
                    if not (first_seq_chunk and ci == 0):
                        blocks.append((0, biasP))
                    for bi, (rel, btab) in enumerate(blocks):
                        kcol = 128 + ci * 128 if rel == 1 else ci * 128
                        mm(pb[4 + bi][:, :], KT[:, j, kcol:kcol + 128], sap(QT, ci * 128, [(TSM, 4), (1, 128)]), True, True,
                           [tk("KT"), tk("QT")], [Tpb[4 + bi]])
                        tt(c.dve, tS[bi][:], pb[4 + bi][:, :], sap(btab, 4 * j * 128, [(1, 512)]), ALU.add, [Tpb[4 + bi], tk("bias")], [tk(f"tS{bi}")])
                        if pmask and ci == 0 and rel == 0:
                            actf(Eb[bi][:], tS[bi][:], AF.Exp, [tk(f"tS{bi}"), Tc], [tk(f"Eb{bi}")], bias=hmc[:, 1:2])
                        else:
                            actf(Eb[bi][:], tS[bi][:], AF.Exp, [tk(f"tS{bi}")], [tk(f"Eb{bi}")])
                    nb = len(blocks)
                    for bi, (rel, btab) in enumerate(blocks):
                        vsl = 1 + ci if rel == 1 else ci
                        mm(pb[6][:, :], Vtm[:, vsl, j * 128:(j + 1) * 128], Eb[bi][:], bi == 0, bi == nb - 1,
                           [tk("Vtm"), tk(f"Eb{bi}")], [Tpb[6]], sig=(bi == nb - 1))
                    for bi in range(nb):
                        mm(pb[3][:, :], onesb[:], Eb[bi][:], bi == 0, bi == nb - 1, [tk("onesb"), tk(f"Eb{bi}")], [Tpb[3]], sig=(bi == nb - 1))
                    tt(c.dve, den[:], pb[3][:, :], sap(esink, 4 * j, [(1, 4), (0, 128)]), ALU.add, [Tpb[3], tk("esink")], [tk("den")])
                    c.op(c.dve, lambda h: h.reciprocal(out=den[:], in_=den[:]), reads=[tk("den")], writes=[tk("den")])
                    tt(c.dve, sap(mixT, 4 * j * TSM + ci * 128, [(TSM, 4), (1, 128)]), pb[6][:, :], den[:], ALU.mult,
                       [Tpb[6], tk("den")], [tk("mixT")])
        stage(4)
        cp(c.act, KT[:, :, 0:128], KT[:, :, TS:TS + 128], [tk("KT")], [tk("KT")])
        cp(c.act, Vtm[:, 0, :], Vtm[:, nch, :], [tk("Vtm")], [tk("Vtm")])
        ssd(chunks, TS, state_only)

    dtT = sbt("dtT", [32, TSM]); acsT = sbt("acsT", [32, TSM]); dendT = sbt("dendT", [32, TSM])
    a2T = dendT
    ecd = sbt("ecd", [32, TSC]); tm = sbt("tm", [128, TSC, 96])
    xr = sbt("xr", [128, 3 + TSM]); cacc = sbt("cacc", [128, TSM])
    xsT = sbt("xsT", [128, 2, TSM], BF16); BT = sbt("BT", [128, TSM], BF16); CT = sbt("CT", [128, TSM], BF16)
    szT = sbt("szT", [128, 2, TSM], BF16)
    xdt = sbt("xdt", [128, 256], BF16); xdtd = sbt("xdtd", [128, 256], BF16); Btm = sbt("Btm", [128, 128], BF16)
    m4 = den; t1 = tS[0]; EA = tS[1]; MT = Eb[0]; ChT = Eb[1]
    e256 = sbt("e256", [32, 256]); ytmp = sbt("ytmp", [128, 128]); sqy = sbt("sqy", [128, 128], BF16)
    rsy = rsb

    def conv_tile(bank, TS, ft, dst, silu=True):
        cp(c.act, xr[:, 3:3 + TS], pb[bank][:, 0:TS], [Tpb[bank]], [tk("xr")])
        cp(c.act, xr[:, 0:3], ccar[:, ft, :], [tk("ccar")], [tk("xr")])
        cp(c.act, ccar[:, ft, :], xr[:, TS:TS + 3], [tk("xr")], [tk("ccar")])
        wof = CONVW + ft * 4
        c.op(c.act, lambda h: h.activation(out=cacc[:, 0:TS], in_=xr[:, 3:3 + TS], func=AF.Identity,
                                           scale=cf[:, wof + 3:wof + 4], bias=cf[:, CONVB + ft:CONVB + ft + 1]),
             reads=[tk("xr"), Tc], writes=[tk("cacc")])
        for t_ in range(3):
            stt(cacc[:, 0:TS], xr[:, t_:t_ + TS], cf[:, wof + t_:wof + t_ + 1], cacc[:, 0:TS], ALU.mult, ALU.add,
                [tk("xr"), Tc, tk("cacc")], [tk("cacc")])
        actf(dst, cacc[:, 0:TS], AF.Silu, [tk("cacc")], [tk("ssdin")])

    def conv_tile_s(bank, ft, dst):
        v8 = [(11, 16), (1, 8)]
        cp(c.act, sap(xrs, 3, v8), sap(pb[bank], 0, [(8, 16), (1, 8)]), [Tpb[bank]], [tk("xr")])
        c.dma(c.sp, d_stg, st48[:], stconv[:, ft * 128:(ft + 1) * 128], writes=[tk("st48")])
        tr(pb[4][:, 0:48], st48[:], stc[0:48, IDENT:IDENT + 48], [tk("st48"), Tc], [Tpb[4]])
        cp(c.act, sap(xrs, 0, [(11, 16), (1, 3)]), sap(pb[4], 0, [(3, 16), (1, 3)]), [Tpb[4]], [tk("xr")])
        cp(c.act, sap(tl48, 0, [(3, 16), (1, 3)]), sap(xrs, 8, [(11, 16), (1, 3)]), [tk("xr")], [tk("tl48")])
        tr(pb[4][0:48, 128:256], tl48[:], ident, [tk("tl48"), Tc], [Tpb[4]])
        cp(c.act, so48[:], pb[4][0:48, 128:256], [Tpb[4]], [tk("so48")])
        c.dma(c.sp, d_s48, sconv[:, ft * 128:(ft + 1) * 128], so48[:], reads=[tk("so48")], writes=[tk("o_sconv")])
        wof = CONVW + ft * 4
        o8 = sap(cacc, 0, [(8, 16), (1, 8)])
        c.op(c.act, lambda h: h.activation(out=o8, in_=sap(xrs, 3, v8), func=AF.Identity,
                                           scale=cf[:, wof + 3:wof + 4], bias=cf[:, CONVB + ft:CONVB + ft + 1]),
             reads=[tk("xr"), Tc], writes=[tk("cacc")])
        for t_ in range(3):
            stt(o8, sap(xrs, t_, v8), cf[:, wof + t_:wof + t_ + 1], o8, ALU.mult, ALU.add, [tk("xr"), Tc, tk("cacc")], [tk("cacc")])
        actf(dst, cacc[:, 0:128], AF.Silu, [tk("cacc")], [tk("ssdin")])

    def ssd(chunks, TS, state_only=False):
        nch = len(chunks)
        sample = chunks[0]["kind"] == "sample"
        s, trk = wload([(0, 32, 32, w_in[:, OFF_DT:OFF_DT + 32])])
        for k in range(32):
            mm(pb[0][0:32, 0:TS], wview(s, 0, k, 32, 0, 32), sap(actT, k * TSM, [(1, TS)]), k == 0, k == 31, [trk, tk("actT")], [Tpb[0]], sig=(k == 31))
        actf(dtT[:, 0:TS], pb[0][0:32, 0:TS], AF.Exp, [Tpb[0], Tc], [tk("dtT")], bias=c32[:, DTB:DTB + 1])
        actf(dtT[:, 0:TS], dtT[:, 0:TS], AF.Ln, [tk("dtT")], [tk("dtT")], bias=1.0)
        tsc(c.dve, a2T[:, 0:TS], dtT[:, 0:TS], negA[:, 0:1], ALU.mult, [tk("dtT"), tk("negA")], [tk("dendT")])
        for ci, ch in enumerate(chunks):
            cs = slice(ci * 128, (ci + 1) * 128)
            if ch["kind"] == "prompt":
                msk = sap(ones32, 0, [(1, 128)], parts=32)
                nseg, sl = 1, 128
            else:
                msk = c32[:, MSEG:MSEG + 128]
                nseg, sl = 16, 8
            c.op(c.dve, lambda h, cs=cs, msk=msk: h.tensor_tensor_scan(out=acsT[:, cs], data0=msk, data1=a2T[:, cs], initial=0.0,
                                                                      op0=ALU.mult, op1=ALU.add),
                 reads=[tk("dendT"), tk("ones32"), Tc], writes=[tk("acsT")])
            last = sap(acsT, ci * 128 + sl - 1, [(sl, nseg), (0, sl)], parts=32)
            tt(c.dve, sap(dendT, ci * 128, [(sl, nseg), (1, sl)], parts=32), last, sap(acsT, ci * 128, [(sl, nseg), (1, sl)], parts=32),
               ALU.subtract, [tk("acsT")], [tk("dendT")])
            actf(dendT[:, cs], dendT[:, cs], AF.Exp, [tk("dendT")], [tk("dendT")])
            if ch["kind"] == "prompt":
                actf(ecd[:, ci:ci + 1], acsT[:, ci * 128 + 127:ci * 128 + 128], AF.Exp, [tk("acsT")], [tk("ecd")])
            else:
                actf(ecdS[:, :], sap(acsT, ci * 128 + 7, [(8, 16)], parts=32), AF.Exp, [tk("acsT")], [tk("ecd")])
            for q, src in enumerate((dtT, acsT, dendT)):
                tr(pb[1][:, q * 32:(q + 1) * 32], src[:, cs], stc[0:32, IDENT:IDENT + 32], [tk("dtT"), tk("acsT"), tk("dendT"), Tc], [Tpb[1]], sig=(q == 2))
            cp(c.act, tm[:, ci, :], pb[1][:, 0:96], [Tpb[1]], [tk("tm")])
        stage(5)
        for g in range(8):
            cvt = (lambda bank, TS_, ft, dst: conv_tile_s(bank, ft, dst)) if sample else conv_tile
            for t2 in range(2):
                proj_tile(OFF_X + g * 256 + t2 * 128, TS, t2, w_in)
                cvt(t2, TS, 2 * g + t2, xsT[:, t2, 0:TS])
            proj_tile(OFF_B + g * 128, TS, 0, w_in)
            cvt(0, TS, 16 + g, BT[:, 0:TS])
            proj_tile(OFF_C + g * 128, TS, 1, w_in)
            cvt(1, TS, 24 + g, CT[:, 0:TS])
            if sample:
                for t2 in range(2):
                    c.dma(c.sp, d_h0, h0g[:, :, t2, :], stssm[:, g * 256 + t2 * 128:g * 256 + (t2 + 1) * 128, :].rearrange("i p n -> p i n"), writes=[tk("xt")])
                for i in range(16):
                    for t2 in range(2):
                        tr(pb[2][:, t2 * 128:(t2 + 1) * 128], h0g[:, i, t2, :], ident, [tk("xt"), Tc], [Tpb[2]], sig=(t2 == 1))
                    cp(c.act, SiT[:, i, :], pb[2][:, 0:256], [Tpb[2]], [tk("xn")])
            for t2 in (() if state_only else range(2)):
                proj_tile(OFF_Z + g * 256 + t2 * 128, TS, t2, w_in)
                actf(szT[:, t2, 0:TS], pb[t2][:, 0:TS], AF.Silu, [Tpb[t2]], [tk("szT")])
            stage(52)
            for ci, ch in enumerate(chunks):
                cs = slice(ci * 128, (ci + 1) * 128)
                prompt = ch["kind"] == "prompt"
                for t2 in range(2):
                    tr(ptb[:, t2 * 128:(t2 + 1) * 128], xsT[:, t2, cs], idb[:], [tk("ssdin"), tk("idb")], [Tptb], sig=False)
                tr(ptb[:, 256:384], BT[:, cs], idb[:], [tk("ssdin"), tk("idb")], [Tptb])
                tt(c.dve, sap(xdt, 0, [(64, 4), (1, 64)]), sap(ptb, 0, [(64, 4), (1, 64)]), sap(tm, ci * 96 + 4 * g, [(1, 4), (0, 64)]),
                   ALU.mult, [Tptb, tk("tm")], [tk("xdt")])
                tt(c.dve, sap(xdtd, 0, [(64, 4), (1, 64)]), sap(xdt, 0, [(64, 4), (1, 64)]), sap(tm, ci * 96 + 64 + 4 * g, [(1, 4), (0, 64)]),
                   ALU.mult, [tk("xdt"), tk("tm")], [tk("xdtd")])
                cp(c.act, Btm[:], ptb[:, 256:384], [Tptb], [tk("Btm")])
                stage(53)
                if not state_only:
                    mm(pb[4][:, 0:128], BT[:, cs], CT[:, cs], True, True, [tk("ssdin")], [Tpb[4]])
                    tt(c.dve, sap(m4, 0, [(128, 4), (1, 128)], parts=32), sap(acsT, ci * 128, [(0, 4), (1, 128)], parts=32),
                       sap(c32, OH32 + 4 * g, [(1, 4), (0, 128)], parts=32), ALU.mult, [tk("acsT"), Tc], [tk("den")])
                    mm(pb[5][:, :], ones32[:], m4[0:32, :], True, True, [tk("ones32"), tk("den")], [Tpb[5]])
                    moff = MASKP if prompt else MASKS
                    tt(c.dve, t1[:], pb[5][:, :], sap(stc, moff, [(0, 4), (1, 128)]), ALU.add, [Tpb[5], Tc], [tk("tS0")])
                    tt(c.dve, t1[:], t1[:], sap(tm, ci * 96 + 32 + 4 * g, [(1, 4), (0, 128)]), ALU.subtract, [tk("tS0"), tk("tm")], [tk("tS0")])
                    actf(t1[:], t1[:], AF.Exp, [tk("tS0")], [tk("tS0")])
                    tt(c.dve, MT[:], t1[:], sap(pb[4], 0, [(0, 4), (1, 128)]), ALU.mult, [tk("tS0"), Tpb[4]], [tk("Eb0")])
                    actf(EA[:], pb[5][:, :], AF.Exp, [Tpb[5]], [tk("tS1")])
                    tt(c.dve, ChT[:], EA[:], sap(CT, ci * 128, [(0, 4), (1, 128)]), ALU.mult, [tk("tS1"), tk("ssdin")], [tk("Eb1")])
                    stage(54)
                    for hh in range(4):
                        o = pb[6][(hh % 2) * 64:(hh % 2) * 64 + 64, (hh // 2) * 128:(hh // 2) * 128 + 128]
                        if prompt:
                            mm(o, xdt[:, hh * 64:(hh + 1) * 64], MT[:, hh * 128:(hh + 1) * 128], True, False, [tk("xdt"), tk("Eb0")], [Tpb[6]], sig=False)
                            mm(o, Sb[:, g * 256 + hh * 64:g * 256 + (hh + 1) * 64], ChT[:, hh * 128:(hh + 1) * 128], False, True,
                               [tk("Sb"), tk("Eb1")], [Tpb[6]], sig=(hh == 3))
                        else:
                            mm(o, xdt[:, hh * 64:(hh + 1) * 64], MT[:, hh * 128:(hh + 1) * 128], True, False, [tk("xdt"), tk("Eb0")], [Tpb[6]], sig=False)
                            for i in range(16):
                                oi = pb[6][(hh % 2) * 64:(hh % 2) * 64 + 64, (hh // 2) * 128 + 8 * i:(hh // 2) * 128 + 8 * i + 8]
                                mm(oi, SiT[:, i, hh * 64:(hh + 1) * 64], ChT[:, hh * 128 + 8 * i:hh * 128 + 8 * i + 8], False, True,
                                   [tk("xn"), tk("Eb1")], [Tpb[6]], sig=(hh == 3 and i == 15))
                    stage(55)
                    for t2 in range(2):
                        ft = 2 * g + t2
                        stt(ytmp[:], xsT[:, t2, cs], cf[:, DEXP + ft:DEXP + ft + 1], pb[6][:, t2 * 128:(t2 + 1) * 128], ALU.mult, ALU.add,
                            [tk("ssdin"), Tc, Tpb[6]], [tk("ytmp")])
                        tt(c.dve, mixT[:, 16 + ft, cs], ytmp[:], szT[:, t2, cs], ALU.mult, [tk("ytmp"), tk("szT")], [tk("mixT")])
                stage(56)
                if prompt:
                    mm(pb[4][:, 128:384], Btm[:], xdtd[:], True, True, [tk("Btm"), tk("xdtd")], [Tpb[4]])
                    tsc(c.dve, sap(e256, 0, [(64, 4), (1, 64)], parts=32), sap(c32, OH32 + 4 * g, [(1, 4), (0, 64)], parts=32), ecd[:, ci:ci + 1], ALU.mult, [Tc, tk("ecd")], [tk("e256")])
                    mm(pb[5][:, 0:256], ones32[:], e256[:], True, True, [tk("ones32"), tk("e256")], [Tpb[5]])
                    gs = slice(g * 256, (g + 1) * 256)
                    tt(c.dve, ST[:, gs], ST[:, gs], pb[5][:, 0:256], ALU.mult, [tk("ST"), Tpb[5]], [tk("ST")])
                    tt(c.dve, ST[:, gs], ST[:, gs], pb[4][:, 128:384], ALU.add, [tk("ST"), Tpb[4]], [tk("ST")])
                    cp(c.act, Sb[:, gs], ST[:, gs], [tk("ST")], [tk("Sb")])
                else:
                    cp(c.dve, sap(e256, 0, [(64, 4), (1, 64)], parts=32), sap(c32, OH32 + 4 * g, [(1, 4), (0, 64)], parts=32), [Tc], [tk("e256")])
                    for t2 in range(2):
                        mm(pb[5][:, t2 * 16:(t2 + 1) * 16], e256[:, t2 * 128:(t2 + 1) * 128], ecdS[:, :], True, True, [tk("e256"), tk("ecd")], [Tpb[5]], sig=(t2 == 1))
                    cp(c.act, sap(cdx, 0, [(1, 32)]), pb[5][:, 0:32], [Tpb[5]], [tk("cdx")])
                    for i in range(16):
                        tsc(c.dve, Bm[:, i, :], Btm[:], stc[:, ROWM + i:ROWM + i + 1], ALU.mult, [tk("Btm"), Tc], [tk("QT")])
                    for i in range(16):
                        for t2 in range(2):
                            q = (i * 2 + t2) % 4
                            mm(pb[4][:, q * 128:(q + 1) * 128], xdtd[:, t2 * 128:(t2 + 1) * 128], Bm[:, i, :], True, True, [tk("xdtd"), tk("QT")], [Tpb[4]])
                            stt(h0g[:, i, t2, :], h0g[:, i, t2, :], cdx[:, t2, i:i + 1], pb[4][:, q * 128:(q + 1) * 128], ALU.mult, ALU.add,
                                [tk("xt"), tk("cdx"), Tpb[4]], [tk("xt")])
                    for t2 in range(2):
                        c.dma(c.sp, d_so, sssm[:, g * 256 + t2 * 128:g * 256 + (t2 + 1) * 128, :].rearrange("i p n -> p i n"), h0g[:, :, t2, :],
                              reads=[tk("xt")], writes=[tk("o_sssm")])
        stage(6)
        if state_only:
            return
        for ci in range(nch):
            cs = slice(ci * 128, (ci + 1) * 128)
            for ft in range(16):
                actf(sqy[:], mixT[:, 16 + ft, cs], AF.Square, [tk("mixT")], [tk("sqy")])
                mm(pb[2][:, cs], onesb[:], sqy[:], ft == 0, ft == 15, [tk("onesb"), tk("sqy")], [Tpb[2]])
        rstd_from(rsy[:, 0:TS], pb[2][:, 0:TS], 1.0 / 2048, [Tpb[2]], [tk("rsb")])
        for ft in range(16):
            stt(mixT[:, 16 + ft, 0:TS], mixT[:, 16 + ft, 0:TS], cf[:, SSDNW + ft:SSDNW + ft + 1], rsy[:, 0:TS], ALU.mult, ALU.mult,
                [tk("mixT"), Tc, tk("rsb")], [tk("mixT")])

    xpc = tS
    d_ys = [c.dsem(f"ys{i}") for i in range(2)]
    d_xp = [c.dsem(f"xp{i}") for i in range(2)]
    d_y = c.dsem("y")
    cnt = {"xp": 0}

    def outproj(chunks, Ty):
        nch = len(chunks)
        for cb in range(8):
            for kb in range(4):
                s, trk = wload([(0, 8, 512, w_out[kb * 1024:(kb + 1) * 1024, cb * 512:(cb + 1) * 512])])
                for ci in range(nch):
                    for k8 in range(8):
                        k = kb * 8 + k8
                        mm(pb[ci][:, :], sap(mixT, k * TSM + ci * 128, [(1, 128)]), wview(s, 0, k8, 512, 0, 512), k == 0, k == 31,
                           [trk, tk("mixT")], [Tpb[ci]], sig=(k8 == 7))
            for ci, ch in enumerate(chunks):
                i = cnt["xp"] % 2
                cnt["xp"] += 1
                c.dma(c.sp, d_xp[i], xpc[i][:], ch["src"][:, cb * 512:(cb + 1) * 512], writes=[tk(f"tS{i}")])
                tt(c.dve, xpc[i][:], xpc[i][:], pb[ci][:, :], ALU.add, [tk(f"tS{i}"), Tpb[ci]], [tk(f"tS{i}")])
                c.dma(c.sp, d_ys[i], ch["dst"][:, cb * 512:(cb + 1) * 512], xpc[i][:], reads=[tk(f"tS{i}")], writes=[Ty[ci][cb]])

    gc = cacc; sg = sqb
    abuf = [sbt(f"abuf{i}", [128, TSM], BF16) for i in range(2)]
    d_a = [c.dsem(f"a{i}") for i in range(2)]
    d_ab = [c.dsem(f"ab{i}") for i in range(2)]
    Tats = [Trk() for _ in range(NFF)]

    def ffn_post(j, bg, bu, TS, chunks):
        G = pb[bg]
        wof = FCW + j * 3
        bcol = cf[:, FCB + j:FCB + j + 1]
        if chunks[0]["kind"] == "prompt":
            c.op(c.act, lambda h, G=G, wof=wof, bcol=bcol: h.activation(out=gc[:, 0:TS], in_=G[:, 0:TS], func=AF.Identity,
                                                                         scale=cf[:, wof + 2:wof + 3], bias=bcol),
                 reads=[Tpb[bg], Tc], writes=[tk("cacc")])
            stt(gc[:, 1:TS], G[:, 0:TS - 1], cf[:, wof + 1:wof + 2], gc[:, 1:TS], ALU.mult, ALU.add, [Tpb[bg], Tc, tk("cacc")], [tk("cacc")])
            stt(gc[:, 2:TS], G[:, 0:TS - 2], cf[:, wof:wof + 1], gc[:, 2:TS], ALU.mult, ALU.add, [Tpb[bg], Tc, tk("cacc")], [tk("cacc")])
            stt(gc[:, 0:1], fcar[:, j, 1:2], cf[:, wof + 1:wof + 2], gc[:, 0:1], ALU.mult, ALU.add, [tk("fcar"), Tc, tk("cacc")], [tk("cacc")])
            stt(gc[:, 0:2], fcar[:, j, 0:2], cf[:, wof:wof + 1], gc[:, 0:2], ALU.mult, ALU.add, [tk("fcar"), Tc, tk("cacc")], [tk("cacc")])
            cp(c.act, fcar[:, j, :], G[:, TS - 2:TS], [Tpb[bg]], [tk("fcar")])
        else:
            q = j % 4
            if q == 0:
                nq = min(4, NFF - j)
                c.dma(c.sp, d_stg, stg[0:32, 0:nq * 128], stffn[:, j * 128:(j + nq) * 128], writes=[tk("vtmf")])
                for qq in range(nq):
                    tr(pb[4][:, qq * 32:(qq + 1) * 32], stg[0:32, qq * 128:(qq + 1) * 128], stc[0:32, IDENT:IDENT + 32], [tk("vtmf"), Tc], [Tpb[4]], sig=(qq == nq - 1))
                cp(c.act, sap(fpv, 0, [(1, nq * 32)]), pb[4][:, 0:nq * 32], [Tpb[4]], [tk("fpv")])
            g8 = sap(gc, 0, [(8, 16), (1, 8)])
            c.op(c.act, lambda h, G=G, wof=wof, bcol=bcol, g8=g8: h.activation(out=g8, in_=sap(G, 0, [(8, 16), (1, 8)]), func=AF.Identity,
                                                                               scale=cf[:, wof + 2:wof + 3], bias=bcol),
                 reads=[Tpb[bg], Tc], writes=[tk("cacc")])
            stt(sap(gc, 1, [(8, 16), (1, 7)]), sap(G, 0, [(8, 16), (1, 7)]), cf[:, wof + 1:wof + 2], sap(gc, 1, [(8, 16), (1, 7)]), ALU.mult, ALU.add,
                [Tpb[bg], Tc, tk("cacc")], [tk("cacc")])
            stt(sap(gc, 2, [(8, 16), (1, 6)]), sap(G, 0, [(8, 16), (1, 6)]), cf[:, wof:wof + 1], sap(gc, 2, [(8, 16), (1, 6)]), ALU.mult, ALU.add,
                [Tpb[bg], Tc, tk("cacc")], [tk("cacc")])
            stt(sap(gc, 0, [(8, 16), (1, 1)]), sap(fpv, q * 32 + 1, [(2, 16), (1, 1)]), cf[:, wof + 1:wof + 2], sap(gc, 0, [(8, 16), (1, 1)]), ALU.mult, ALU.add,
                [tk("fpv"), Tc, tk("cacc")], [tk("cacc")])
            stt(sap(gc, 0, [(8, 16), (1, 2)]), sap(fpv, q * 32, [(2, 16), (1, 2)]), cf[:, wof:wof + 1], sap(gc, 0, [(8, 16), (1, 2)]), ALU.mult, ALU.add,
                [tk("fpv"), Tc, tk("cacc")], [tk("cacc")])
            cp(c.act, sap(gtl, q * 32, [(2, 16), (1, 2)]), sap(G, 6, [(8, 16), (1, 2)]), [Tpb[bg]], [tk("gtl")])
            if q == 3 or j == NFF - 1:
                nq = q + 1
                j0 = j - q
                for qq in range(nq):
                    tr(pb[5][0:32, qq * 128:(qq + 1) * 128], gtl[:, qq, :], ident, [tk("gtl"), Tc], [Tpb[5]], sig=(qq == nq - 1))
                cp(c.act, ktm[0:32, 0:nq * 128], pb[5][0:32, 0:nq * 128], [Tpb[5]], [tk("ktm")])
                c.dma(c.sp, d_sk, sffn[:, j0 * 128:(j0 + nq) * 128], ktm[0:32, 0:nq * 128], reads=[tk("ktm")], writes=[tk("o_sffn")])
        actf(sg[:, 0:TS], gc[:, 0:TS], AF.Silu, [tk("cacc")], [tk("sqb")])
        i = j % 2
        tt(c.dve, abuf[i][:, 0:TS], sg[:, 0:TS], pb[bu][:, 0:TS], ALU.mult, [tk("sqb"), Tpb[bu]], [tk(f"abuf{i}")])
        c.dma(c.sp, d_a[i], ats[j, :, 0:TS], abuf[i][:, 0:TS], reads=[tk(f"abuf{i}")], writes=[Tats[j]])

    def ffn(chunks, Ty, TS, gate_only=False):
        nch = len(chunks)
        for ci, ch in enumerate(chunks):
            norm_T(ch["dst"], Ty[ci], ci, FFNNW)
        for jp in range(0, NFF, 2):
            sg_, trg = wload2(w_gate[:, jp * 128:(jp + 2) * 128])
            for t_ in range(2):
                for k in range(32):
                    mm(pb[t_][:, 0:TS], wview(sg_, 0, k, 256, t_ * 128, t_ * 128 + 128), sap(actT, k * TSM, [(1, TS)]), k == 0, k == 31,
                       trg + [tk("actT")], [Tpb[t_]], sig=(k == 31))
            if gate_only:
                for t_ in range(2):
                    cp(c.act, fcar[:, jp + t_, :], pb[t_][:, TS - 2:TS], [Tpb[t_]], [tk("fcar")])
                continue
            su_, tru = wload2(w_up[:, jp * 128:(jp + 2) * 128])
            for t_ in range(2):
                for k in range(32):
                    mm(pb[2 + t_][:, 0:TS], wview(su_, 0, k, 256, t_ * 128, t_ * 128 + 128), sap(actT, k * TSM, [(1, TS)]), k == 0, k == 31,
                       tru + [tk("actT")], [Tpb[2 + t_]], sig=(k == 31))
            for t_ in range(2):
                ffn_post(jp + t_, t_, 2 + t_, TS, chunks)
        if gate_only:
            return
        stage(9)
        nblk = 0
        for cb in range(8):
            for kb in range(11):
                nk = 8 if kb < 10 else 6
                s, trk = wload([(0, nk, 512, w_down[kb * 1024:kb * 1024 + nk * 128, cb * 512:(cb + 1) * 512])])
                i = nblk % 2
                nblk += 1
                abase = xn[:, 0:8 * TSM] if i == 0 else xt[:].bitcast(BF16)[:, 0:8 * TSM]
                abv = abase.rearrange("p (k t) -> p k t", t=TSM)
                abk = "xn" if i == 0 else "xt"
                c.dma(c.pool, d_ab[i], abv[:, 0:nk, 0:TS], ats[kb * 8:kb * 8 + nk, :, 0:TS].rearrange("k p t -> p k t"),
                      reads=Tats[kb * 8:kb * 8 + nk], writes=[tk(abk)])
                for ci in range(nch):
                    for k8 in range(nk):
                        k = kb * 8 + k8
                        mm(pb[ci][:, :], abv[:, k8, ci * 128:(ci + 1) * 128], wview(s, 0, k8, 512, 0, 512), k == 0, k == NFF - 1,
                           [trk, tk(abk)], [Tpb[ci]], sig=(k8 == nk - 1))
            for ci, ch in enumerate(chunks):
                i2 = cnt["xp"] % 2
                cnt["xp"] += 1
                c.dma(c.sp, d_xp[i2], xpc[i2][:], ch["dst"][:, cb * 512:(cb + 1) * 512], reads=[Ty[ci][cb]], writes=[tk(f"tS{i2}")])
                tt(c.dve, xpc[i2][:], xpc[i2][:], pb[ci][:, :], ALU.add, [tk(f"tS{i2}"), Tpb[ci]], [tk(f"tS{i2}")])
                c.dma(c.sp, d_ys[i2], ch["dst"][:, cb * 512:(cb + 1) * 512], xpc[i2][:], reads=[tk(f"tS{i2}")], writes=[Ty[ci][cb]])

    pre = [{"src": xp[i * 128:(i + 1) * 128, :], "dst": xdum, "kind": "prompt"} for i in range(NPRE)]
    own = [{"src": xp[(NPRE + i) * 128:(NPRE + i + 1) * 128, :], "dst": yp[i * 128:(i + 1) * 128, :], "kind": "prompt"} for i in range(NOWN)]
    own[-1]["kv_out"] = (pk, pv)
    try:
        stage(0)
        first = True
        for s0 in range(0, NPRE - 1, TSC):
            mixer(pre[s0:min(s0 + TSC, NPRE - 1)], first, state_only=True)
            first = False
        if NPRE > 0:
            ch7 = [pre[NPRE - 1]]
            Ty = [[Trk() for _ in range(8)]]
            mixer(ch7, first)
            outproj(ch7, Ty)
            ffn(ch7, Ty, 128, gate_only=True)
            hcol = hmc[:, 0:1]
            tsc(c.dve, ST[:], ST[:], hcol, ALU.mult, [tk("ST"), Tc], [tk("ST")])
            tsc(c.dve, Sb[:], Sb[:], hcol, ALU.mult, [tk("Sb"), Tc], [tk("Sb")])
            tsc(c.dve, sap(ccar, 0, [(1, 96)]), sap(ccar, 0, [(1, 96)]), hcol, ALU.mult, [tk("ccar"), Tc], [tk("ccar")])
            tsc(c.dve, sap(fcar, 0, [(1, 2 * NFF)]), sap(fcar, 0, [(1, 2 * NFF)]), hcol, ALU.mult, [tk("fcar"), Tc], [tk("fcar")])
        for s0 in range(0, NOWN, TSC):
            chunks = own[s0:s0 + TSC]
            Ty = [[Trk() for _ in range(8)] for _ in chunks]
            cur["st"] = s0 // TSC
            mixer(chunks, NPRE == 0 and s0 == 0, pmask=(NPRE > 0 and s0 == 0))
            stage(7)
            outproj(chunks, Ty)
            stage(8)
            ffn(chunks, Ty, len(chunks) * 128)
        if SAMPLE:
            cur["st"] = 50
            tt(c.dve, sap(Sb, 0, [(128, 16), (1, 128)]), biasC[:], sap(stc, MASKS, [(0, 16), (1, 128)]), ALU.add, [tk("bias"), Tc], [tk("Sb")])
            chunks = [{"src": xs, "dst": ys, "kind": "sample", "kv_out": ("s",)}]
            Ty = [[Trk() for _ in range(8)]]
            mixer(chunks, False)
            stage(7)
            outproj(chunks, Ty)
            stage(8)
            ffn(chunks, Ty, 128)

        osb = ktm
        d_osb = c.dsem("osb")
        for g4 in range(4):
            for q in range(4):
                t_ = g4 * 4 + q
                tr(pb[0][:, q * 128:(q + 1) * 128], ST[:, t_ * 128:(t_ + 1) * 128], ident, [tk("ST"), Tc], [Tpb[0]], sig=(q == 3))
            cp(c.act, osb[:], pb[0][:, :], [Tpb[0]], [tk("ktm")])
            c.dma(c.sp, d_osb, pssm[g4 * 512:(g4 + 1) * 512, :].rearrange("(q p) n -> p q n", p=128), sap(osb, 0, [(128, 4), (1, 128)]),
                  reads=[tk("ktm")], writes=[tk("o_ssm")])
        for g4 in range(8):
            for q in range(4):
                t_ = g4 * 4 + q
                tr(pb[1][0:3, q * 128:(q + 1) * 128], ccar[:, t_, :], ident, [tk("ccar"), Tc], [Tpb[1]], sig=(q == 3))
            cp(c.act, osb[0:3, :], pb[1][0:3, :], [Tpb[1]], [tk("ktm")])
            c.dma(c.sp, d_osb, pconv[:, g4 * 512:(g4 + 1) * 512], osb[0:3, :], reads=[tk("ktm")], writes=[tk("o_conv")])
        for g4 in range(22):
            nq = 4 if g4 < 21 else 2
            for q in range(nq):
                t_ = g4 * 4 + q
                tr(pb[2][0:2, q * 128:(q + 1) * 128], fcar[:, t_, :], ident, [tk("fcar"), Tc], [Tpb[2]], sig=(q == nq - 1))
            cp(c.act, osb[0:2, 0:nq * 128], pb[2][0:2, 0:nq * 128], [Tpb[2]], [tk("ktm")])
            c.dma(c.sp, d_osb, pffn[:, g4 * 512:g4 * 512 + nq * 128], osb[0:2, 0:nq * 128], reads=[tk("ktm")], writes=[tk("o_ffn")])


    except _Stop:
        pass
    c.finish(c.sp)
    c.emit()
    return nc, c


def _t5_bucket_np(dist):
    n = np.maximum(dist, 0)
    nf = np.maximum(n, 1).astype(np.float32)
    large = 16 + (np.log(nf / 16) / math.log(128 / 16) * 16).astype(np.int32)
    large = np.minimum(large, 31)
    return np.where(n < 16, n, large)


def _static_consts():
    stc = np.zeros((128, NSTC), np.float32)
    stc[:, IDENT:IDENT + 128] = np.eye(128)
    s = np.arange(128)[:, None]
    l = np.arange(128)[None, :]
    stc[:, MASKP:MASKP + 128] = np.where(l >= s, 0.0, NEG)
    stc[:, MASKS:MASKS + 128] = np.where((l >= s) & (l // 8 == s // 8), 0.0, NEG)
    stc[:, ROWM:ROWM + 16] = (s // 8 == np.arange(16)[None, :]).astype(np.float32)
    stc[:, ANTI:ANTI + 128] = np.eye(128)[::-1]
    ohd = np.zeros((33, 384), np.float32)
    for i in range(384):
        d = i - 127
        if 0 <= d < 128:
            ohd[int(_t5_bucket_np(np.array(d))), i] = 1.0
        else:
            ohd[32, i] = NEG
    return stc, ohd


_CACHE = {}


def _prep(inputs, NPRE, NOWN):
    f = lambda a: np.ascontiguousarray(np.asarray(a, dtype=np.float32))
    fm = lambda v: np.ascontiguousarray(v.reshape(-1, 128).T)
    cf = np.zeros((128, NCF), np.float32)
    cf[:, MIXNW:MIXNW + 32] = fm(f(inputs["mix_norm_w"])[0])
    cf[:, FFNNW:FFNNW + 32] = fm(f(inputs["ffn_norm_w"])[0])
    cf[:, QNW] = f(inputs["q_norm_w"])[0]
    cf[:, KNW] = f(inputs["k_norm_w"])[0]
    cw = f(inputs["ssd_conv_w"])[0]
    cf[:, CONVW:CONVW + 128] = cw.reshape(4, 32, 128).transpose(2, 1, 0).reshape(128, 128)
    cf[:, CONVB:CONVB + 32] = fm(f(inputs["ssd_conv_b"])[0])
    cf[:, DEXP:DEXP + 16] = fm(np.repeat(f(inputs["ssd_D"])[0], 64))
    cf[:, SSDNW:SSDNW + 16] = fm(f(inputs["ssd_norm_w"])[0])
    fw_ = f(inputs["ffn_conv_w"])[0]
    cf[:, FCW:FCW + 258] = fw_.reshape(3, NFF, 128).transpose(2, 1, 0).reshape(128, 258)
    cf[:, FCB:FCB + NFF] = fm(f(inputs["ffn_conv_b"])[0])
    c32 = np.zeros((32, NC32), np.float32)
    c32[:, DTB] = f(inputs["ssd_dt_bias"])[0]
    c32[:, ALOG] = f(inputs["ssd_A_log"])[0]
    c32[:, OH32:OH32 + 32] = np.eye(32)
    m = np.ones(128, np.float32)
    m[::8] = 0
    c32[:, MSEG:MSEG + 128] = m[None, :]
    stc, ohd = _static_consts()
    relb = np.concatenate([f(inputs["rel_bias"]), np.ones((1, 16), np.float32)], 0)
    shared = {"w_in": f(inputs["w_in"])[0], "w_out": f(inputs["w_out"])[0], "w_gate": f(inputs["w_gate"])[0],
              "w_up": f(inputs["w_up"])[0], "w_down": f(inputs["w_down"])[0], "cf": cf, "c32": c32, "stc": stc,
              "relb33": relb, "ohd33": ohd, "sinks": f(inputs["attn_sinks"])}
    xp = f(inputs["x_prompt"]); xs = f(inputs["x_sample"])
    sk_ = f(inputs["state_attn_k"])[0]; sv_ = f(inputs["state_attn_v"])[0]; ssm = f(inputs["state_ssm"])[0]
    scv = f(inputs["state_ssd_conv"])[0]; sff = f(inputs["state_ffn_conv"])[0]
    maps = []
    for cidx in range(8):
        sl = slice(16 * cidx, 16 * cidx + 16)
        m_ = dict(shared)
        b, half = cidx // 2, cidx % 2
        if half == 0:
            m_["xp"] = np.ascontiguousarray(np.concatenate([np.zeros((NPRE * 128, D), np.float32), xp[b, :NOWN * 128]], 0))
        else:
            m_["xp"] = np.ascontiguousarray(xp[b, :(NPRE + NOWN) * 128])
        hm = np.zeros((128, 2), np.float32)
        hm[:, 0] = float(half)
        hm[:, 1] = 0.0 if half == 1 else NEG
        m_["hm"] = hm
        m_["xs"] = np.ascontiguousarray(xs[sl].reshape(128, D))
        m_["stk"] = np.ascontiguousarray(sk_[sl].reshape(16, 128, 512))
        m_["stv"] = np.ascontiguousarray(sv_[sl].reshape(16, 128, 512))
        m_["stssm"] = np.ascontiguousarray(ssm[sl].reshape(16, 2048, 128))
        m_["stconv"] = np.ascontiguousarray(scv[sl].reshape(48, D))
        m_["stffn"] = np.ascontiguousarray(sff[sl].reshape(32, DFF))
        maps.append(m_)
    return maps


def run(inputs, NPRE=8, NOWN=8, TSC=4, STOP=99, ncores=8):
    key = (NPRE, NOWN, TSC, STOP)
    if key not in _CACHE:
        _CACHE[key] = build(NPRE, NOWN, TSC, STOP=STOP)[0]
    nc = _CACHE[key]
    maps = _prep(inputs, NPRE, NOWN)
    if STOP < 7 or 50 <= STOP < 60:
        maps = [{k: v for k, v in m.items() if k not in ('w_out', 'w_gate', 'w_up', 'w_down')} for m in maps]
    res = run_bass_kernel_spmd(nc, maps[:ncores], core_ids=list(range(ncores)))
    return res


def kernel(**inputs):
    res = run(inputs)
    R = res.results
    yp = np.stack([np.concatenate([R[2 * b]["yp"].reshape(1024, D), R[2 * b + 1]["yp"].reshape(1024, D)], 0) for b in range(4)])
    ys = np.concatenate([R[cidx]["ys"].reshape(16, 8, D) for cidx in range(8)], 0)
    p_k = np.stack([R[2 * b + 1]["pk"].reshape(128, 4, 128) for b in range(4)])[None]
    p_v = np.stack([R[2 * b + 1]["pv"].reshape(128, 4, 128) for b in range(4)])[None]
    p_ssm = np.stack([R[2 * b + 1]["pssm"].reshape(32, 64, 128) for b in range(4)])[None]
    p_conv = np.stack([R[2 * b + 1]["pconv"].reshape(3, D) for b in range(4)])[None]
    p_ffn = np.stack([R[2 * b + 1]["pffn"].reshape(2, DFF) for b in range(4)])[None]
    s_k = np.concatenate([R[cidx]["sk"].reshape(16, 128, 4, 128) for cidx in range(8)], 0)[None]
    s_v = np.concatenate([R[cidx]["sv"].reshape(16, 128, 4, 128) for cidx in range(8)], 0)[None]
    s_ssm = np.concatenate([R[cidx]["sssm"].reshape(16, 32, 64, 128) for cidx in range(8)], 0)[None]
    s_conv = np.concatenate([R[cidx]["sconv"].reshape(16, 3, D) for cidx in range(8)], 0)[None]
    s_ffn = np.concatenate([R[cidx]["sffn"].reshape(16, 2, DFF) for cidx in range(8)], 0)[None]
    outs = (yp, ys, p_k, p_v, p_ssm, p_conv, p_ffn, s_k, s_v, s_ssm, s_conv, s_ffn)
    return tuple(np.ascontiguousarray(o, dtype=np.float32) for o in outs)
```
